# Optimizing a Trainium2 kernel written in Bass

```python
import math
import jax, jax.numpy as jnp
from jax import lax
import numpy as np

D_MODEL = 1024
BATCH = 16
SEQ = 2048
DEPTH = 1

RMS_EPS = 1e-6
D_FF = 2816
BLOCK = 128
MLA_HEADS = 8
MLA_Q_RANK = 256
MLA_KV_RANK = 128
MLA_NOPE = 64
MLA_ROPE = 32
MLA_V = 64
ROPE_THETA = 10000.0
SWA_HEADS = 8
SWA_KV_HEADS = 2
SWA_HEAD_DIM = 64
SWA_GROUP = SWA_HEADS // SWA_KV_HEADS
WINDOW = 128
N_SIDE = -(-WINDOW // BLOCK)
PAD = N_SIDE * BLOCK
KEY_SPAN = (2 * N_SIDE + 1) * BLOCK

MLA_OUT = MLA_HEADS * MLA_V
SWA_OUT = SWA_HEADS * SWA_HEAD_DIM
MIX_WIDTH = MLA_OUT + SWA_OUT
IN_SPLITS = [MLA_Q_RANK, MLA_KV_RANK, MLA_ROPE,
             SWA_HEADS * SWA_HEAD_DIM, SWA_KV_HEADS * SWA_HEAD_DIM, SWA_KV_HEADS * SWA_HEAD_DIM]
IN_WIDTH = sum(IN_SPLITS)

kernel_name = "hybrid_mla_swa_macaron_encoder_layer"


def rms_norm(x, g, eps=RMS_EPS):
    xf = x.astype(jnp.float32)
    y = xf * lax.rsqrt(jnp.mean(xf * xf, axis=-1, keepdims=True) + eps)
    return (y * g.astype(jnp.float32)).astype(x.dtype)


def swiglu(x, w_gate, w_up, w_down):
    return (jax.nn.silu(x @ w_gate) * (x @ w_up)) @ w_down


def rope_tables(seq_len, dim):
    pos = jnp.arange(seq_len, dtype=jnp.float32)
    inv = 1.0 / (ROPE_THETA ** (jnp.arange(0, dim, 2, dtype=jnp.float32) / dim))
    ang = pos[:, None] * inv[None, :]
    return jnp.cos(ang), jnp.sin(ang)


def apply_rope(x, cos, sin):
    xf = x.astype(jnp.float32)
    x1, x2 = jnp.split(xf, 2, axis=-1)
    return jnp.concatenate([x1 * cos - x2 * sin, x2 * cos + x1 * sin], axis=-1).astype(x.dtype)


def mla_mixer(hq, hkv, hkr, g_q_a, w_uq, g_kv_a, w_ukv, g_qn, g_qr, g_kn, g_kr):
    B, S, _ = hq.shape
    nb = S // BLOCK
    cos, sin = rope_tables(S, MLA_ROPE)
    c_q = rms_norm(hq, g_q_a)
    q = (c_q @ w_uq).reshape(B, S, MLA_HEADS, MLA_NOPE + MLA_ROPE)
    q_nope = rms_norm(q[..., :MLA_NOPE], g_qn)
    q_pe = apply_rope(rms_norm(q[..., MLA_NOPE:], g_qr), cos[:, None], sin[:, None])
    c_kv = rms_norm(hkv, g_kv_a)
    kv = (c_kv @ w_ukv).reshape(B, S, MLA_HEADS, MLA_NOPE + MLA_V)
    k_nope = rms_norm(kv[..., :MLA_NOPE], g_kn)
    v = kv[..., MLA_NOPE:]
    k_pe = apply_rope(rms_norm(hkr, g_kr), cos, sin)
    scale = 1.0 / math.sqrt(MLA_NOPE + MLA_ROPE)
    qn_b = jnp.moveaxis(q_nope.reshape(B, nb, BLOCK, MLA_HEADS, MLA_NOPE), 1, 0)
    qp_b = jnp.moveaxis(q_pe.reshape(B, nb, BLOCK, MLA_HEADS, MLA_ROPE), 1, 0)

    def one_block(args):
        qn, qp = args
        s = (jnp.einsum('bqhd,bkhd->bhqk', qn, k_nope).astype(jnp.float32)
             + jnp.einsum('bqhr,bkr->bhqk', qp, k_pe).astype(jnp.float32)) * scale
        p = jax.nn.softmax(s, axis=-1)
        return jnp.einsum('bhqk,bkhd->bqhd', p.astype(v.dtype), v)

    out = lax.map(one_block, (qn_b, qp_b))
    return jnp.moveaxis(out, 0, 1).reshape(B, S, MLA_OUT)


def swa_mixer(q, k, v, g_q, g_k, sink):
    B, S, _ = q.shape
    nb = S // BLOCK
    D = SWA_HEAD_DIM
    q = rms_norm(q.reshape(B, S, SWA_HEADS, D), g_q)
    k = rms_norm(k.reshape(B, S, SWA_KV_HEADS, D), g_k)
    v = v.reshape(B, S, SWA_KV_HEADS, D)
    qb = q.reshape(B, nb, BLOCK, SWA_KV_HEADS, SWA_GROUP, D)
    kp = jnp.pad(k, ((0, 0), (PAD, PAD), (0, 0), (0, 0)))
    vp = jnp.pad(v, ((0, 0), (PAD, PAD), (0, 0), (0, 0)))
    kb = jnp.concatenate([kp[:, j * BLOCK:j * BLOCK + S].reshape(B, nb, BLOCK, SWA_KV_HEADS, D)
                          for j in range(2 * N_SIDE + 1)], axis=2)
    vb = jnp.concatenate([vp[:, j * BLOCK:j * BLOCK + S].reshape(B, nb, BLOCK, SWA_KV_HEADS, D)
                          for j in range(2 * N_SIDE + 1)], axis=2)
    qi = jnp.arange(BLOCK)[:, None]
    kj = jnp.arange(KEY_SPAN)[None, :]
    rel = kj - PAD - qi
    s_abs = jnp.arange(nb)[:, None, None] * BLOCK - PAD + kj[None]
    valid = (jnp.abs(rel)[None] <= WINDOW) & (s_abs >= 0) & (s_abs < S)
    slopes = 2.0 ** (-(8.0 / SWA_HEADS) * jnp.arange(1, SWA_HEADS + 1, dtype=jnp.float32))
    alibi = -slopes.reshape(SWA_KV_HEADS, SWA_GROUP)[:, :, None, None, None] * \
        jnp.abs(rel).astype(jnp.float32)[None, None, None]
    logits = jnp.einsum('bnqhgd,bnkhd->bhgnqk', qb, kb).astype(jnp.float32) / math.sqrt(D)
    logits = jnp.where(valid, logits + alibi[None], -jnp.inf)
    sk = sink.astype(jnp.float32).reshape(SWA_KV_HEADS, SWA_GROUP)[None, :, :, None, None, None]
    m = jnp.maximum(jnp.max(logits, axis=-1, keepdims=True), sk)
    p = jnp.exp(logits - m)
    w = p / (jnp.sum(p, axis=-1, keepdims=True) + jnp.exp(sk - m))
    out = jnp.einsum('bhgnqk,bnkhd->bnqhgd', w.astype(vb.dtype), vb)
    return out.reshape(B, S, SWA_OUT)


def setup_inputs(seed: int = 0) -> dict:
    key = jax.random.key(seed)
    ks = jax.random.split(key, 24)

    def w(k, shape):
        return jax.random.normal(k, shape, jnp.float32) * shape[0] ** -0.5

    def g(k, n):
        return 1.0 + 0.02 * jax.random.normal(k, (n,), jnp.float32)

    return {
        "x": jax.random.normal(ks[0], (BATCH, SEQ, D_MODEL), jnp.float32),
        "g_ffn1": g(ks[1], D_MODEL),
        "w1_gate": w(ks[2], (D_MODEL, D_FF)),
        "w1_up": w(ks[3], (D_MODEL, D_FF)),
        "w1_down": w(ks[4], (D_FF, D_MODEL)),
        "g_mix": g(ks[5], D_MODEL),
        "w_in": w(ks[6], (D_MODEL, IN_WIDTH)),
        "g_q_a": g(ks[7], MLA_Q_RANK),
        "w_uq": w(ks[8], (MLA_Q_RANK, MLA_HEADS * (MLA_NOPE + MLA_ROPE))),
        "g_kv_a": g(ks[9], MLA_KV_RANK),
        "w_ukv": w(ks[10], (MLA_KV_RANK, MLA_HEADS * (MLA_NOPE + MLA_V))),
        "g_mla_qn": g(ks[11], MLA_NOPE),
        "g_mla_qr": g(ks[12], MLA_ROPE),
        "g_mla_kn": g(ks[13], MLA_NOPE),
        "g_mla_kr": g(ks[14], MLA_ROPE),
        "g_swa_q": g(ks[15], SWA_HEAD_DIM),
        "g_swa_k": g(ks[16], SWA_HEAD_DIM),
        "sink": 0.5 * jax.random.normal(ks[17], (SWA_HEADS,), jnp.float32),
        "w_o": w(ks[18], (MIX_WIDTH, D_MODEL)),
        "g_ffn2": g(ks[19], D_MODEL),
        "w2_gate": w(ks[20], (D_MODEL, D_FF)),
        "w2_up": w(ks[21], (D_MODEL, D_FF)),
        "w2_down": w(ks[22], (D_FF, D_MODEL)),
    }


def reference(x, g_ffn1, w1_gate, w1_up, w1_down, g_mix, w_in, g_q_a, w_uq, g_kv_a, w_ukv,
              g_mla_qn, g_mla_qr, g_mla_kn, g_mla_kr, g_swa_q, g_swa_k, sink, w_o,
              g_ffn2, w2_gate, w2_up, w2_down):
    split_idx = list(np.cumsum(IN_SPLITS)[:-1])
    for _ in range(DEPTH):
        x = x + 0.5 * swiglu(rms_norm(x, g_ffn1), w1_gate, w1_up, w1_down)
        h = rms_norm(x, g_mix)
        hq, hkv, hkr, sq, sk_, sv = jnp.split(h @ w_in, split_idx, axis=-1)
        y_a = mla_mixer(hq, hkv, hkr, g_q_a, w_uq, g_kv_a, w_ukv,
                        g_mla_qn, g_mla_qr, g_mla_kn, g_mla_kr)
        y_b = swa_mixer(sq, sk_, sv, g_swa_q, g_swa_k, sink)
        x = x + jnp.concatenate([y_a, y_b], axis=-1) @ w_o
        x = x + 0.5 * swiglu(rms_norm(x, g_ffn2), w2_gate, w2_up, w2_down)
    return x
```

```python
import math
from contextlib import ExitStack

import numpy as np
import concourse.bass as bass
import concourse.mybir as mybir
from concourse.bass_utils import run_bass_kernel_spmd

F32 = mybir.dt.float32
BF16 = mybir.dt.bfloat16
ALU = mybir.AluOpType
AF = mybir.ActivationFunctionType

ENGS = ("pe", "act", "dve", "pool", "sp")

D_MODEL = 1024
D_FF = 2816
SEQ = 2048
NSEQ = 2
NCORES = 8
TILE = 512
NT = SEQ // TILE
NJ = D_FF // 128
RMS_EPS = 1e-6


class Res:
    __slots__ = ("name", "w", "r")

    def __init__(self, name, inherit=()):
        self.name = name
        self.w = None
        self.r = list(inherit)


class DmaSlot:
    def __init__(self, name):
        self.name = name
        self.count = 0
        self.sem = None


class Op:
    __slots__ = ("eng", "fn", "reads", "writes", "deps", "signal", "sigval",
                 "dma", "slot", "waits", "idx", "tag")

    def __init__(self, eng, fn, reads, writes, dma, slot):
        self.eng = eng
        self.fn = fn
        self.reads = reads
        self.writes = writes
        self.deps = []
        self.signal = False
        self.sigval = None
        self.dma = dma
        self.slot = slot
        self.waits = []


class Sched:
    def __init__(self):
        self.ops = {e: [] for e in ENGS}
        self.all = []
        self.slots = []
        self.phase_res = []
        self.frontier = []
        self._final = None
        self.tag = None
        self.scopes = False

    def slot(self, name):
        s = DmaSlot(name)
        self.slots.append(s)
        return s

    def res(self, name, persist=False):
        if persist:
            return Res(name)
        r = Res(name, self.frontier)
        self.phase_res.append(r)
        return r

    def new_phase(self):
        last = {}
        for r in self.phase_res:
            for op in ([r.w] if r.w is not None else []) + r.r:
                key = ("slot", id(op.slot)) if op.dma else ("eng", op.eng)
                if key not in last or last[key].idx < op.idx:
                    last[key] = op
        for op in self.frontier:
            key = ("slot", id(op.slot)) if op.dma else ("eng", op.eng)
            if key not in last or last[key].idx < op.idx:
                last[key] = op
        self.frontier = list(last.values())

    def _dep(self, op, prod):
        if prod is None or prod is op:
            return
        if (not prod.dma) and (not op.dma) and prod.eng == op.eng:
            return
        op.deps.append(prod)

    def add(self, eng, fn, reads=(), writes=(), dma=False, slot=None):
        op = Op(eng, fn, tuple(reads), tuple(writes), dma, slot)
        op.idx = len(self.all)
        op.tag = self.tag
        for r in op.reads:
            self._dep(op, r.w)
        for w in op.writes:
            self._dep(op, w.w)
            for rd in w.r:
                self._dep(op, rd)
        for r in op.reads:
            r.r.append(op)
        for w in op.writes:
            w.w = op
            w.r = []
        self.ops[eng].append(op)
        self.all.append(op)
        return op

    def final_wait(self, eng, slots):
        self._final = (eng, list(slots))

    def finalize(self):
        for op in self.all:
            for d in op.deps:
                d.signal = True
        cnt = {e: 0 for e in ENGS}
        for op in self.all:
            if op.dma:
                op.slot.count += 16
                op.sigval = op.slot.count
            elif op.signal:
                cnt[op.eng] += 1
                op.sigval = cnt[op.eng]
        seen = {e: {} for e in ENGS}
        for op in self.all:
            need = {}
            for d in op.deps:
                key = ("slot", d.slot) if d.dma else ("eng", d.eng)
                if need.get(key, 0) < d.sigval:
                    need[key] = d.sigval
            for key, v in need.items():
                if seen[op.eng].get(key, 0) >= v:
                    continue
                seen[op.eng][key] = v
                op.waits.append((key, v))

    def emit(self, nc):
        self.finalize()
        with ExitStack() as st:
            esem = {e: st.enter_context(nc.semaphore("s_" + e)) for e in ENGS if e != "sp"}
            for s in self.slots:
                s.sem = st.enter_context(nc.semaphore("d_" + s.name))
            block = st.enter_context(nc.Block())
            fin = self._final

            def run(engname, eng):
                cur = [None, None]

                def set_scope(tag):
                    if not self.scopes or tag == cur[0]:
                        return
                    if cur[1] is not None:
                        cur[1].close()
                        cur[1] = None
                    cur[0] = tag
                    if tag is not None:
                        es = ExitStack()
                        es.enter_context(nc.named_scope(tag))
                        cur[1] = es

                for op in self.ops[engname]:
                    set_scope(op.tag)
                    for key, v in op.waits:
                        sem = key[1].sem if key[0] == "slot" else esem[key[1]]
                        eng.wait_ge(sem, v)
                    ins = op.fn(eng)
                    if op.dma:
                        ins.then_inc(op.slot.sem, 16)
                    elif op.signal:
                        ins.then_inc(esem[op.eng], 1)
                set_scope(None)
                if fin is not None and fin[0] == engname:
                    for s in fin[1]:
                        if s.count:
                            eng.wait_ge(s.sem, s.count)

            @block.tensor
            def _(e):
                run("pe", e)

            @block.scalar
            def _(e):
                run("act", e)

            @block.vector
            def _(e):
                run("dve", e)

            @block.gpsimd
            def _(e):
                run("pool", e)

            @block.sync
            def _(e):
                run("sp", e)


class Cyc:
    def __init__(self, ids):
        self.ids = list(ids)
        self.k = 0

    def next(self):
        v = self.ids[self.k % len(self.ids)]
        self.k += 1
        return v


class Arena:
    def __init__(self, ap2d, nwords):
        self.ap = ap2d
        self.n = nwords
        self.off = 0

    def alloc(self, nelem, dt):
        n4 = nelem if dt == F32 else (nelem + 1) // 2
        assert self.off + n4 <= self.n, ("SBUF arena overflow", self.off, n4, self.n)
        a = self.ap[:, self.off:self.off + n4]
        self.off += n4
        if dt != F32:
            a = a.bitcast(dt)
        return a

    def mark(self):
        return self.off

    def release(self, m):
        self.off = m


G_FFN1, G_MIX, G_FFN2, G_QA, G_KVA = 0, 8, 16, 24, 26
G_QN, G_KN, G_SWQ, G_SWK = 27, 28, 29, 30
G_QRS, G_QRW, G_KRS, G_KRW = 31, 32, 33, 34
G_SINK = 35
NG = 48


def build_program(scopes=False):
    nc = bass.Bass("TRN2", target_bir_lowering=False)
    NTOK = NSEQ * SEQ

    def din(name, shape):
        return nc.dram_tensor(name, list(shape), F32, kind="ExternalInput").ap()

    x_d = din("x", [NTOK, D_MODEL])
    out_d = nc.dram_tensor("out", [NTOK, D_MODEL], F32, kind="ExternalOutput").ap()
    wg_d = [din("w1_gate", [D_MODEL, D_FF]), din("w2_gate", [D_MODEL, D_FF])]
    wu_d = [din("w1_up", [D_MODEL, D_FF]), din("w2_up", [D_MODEL, D_FF])]
    wd_d = [din("w1_down", [D_FF, D_MODEL]), din("w2_down", [D_FF, D_MODEL])]
    wsw_d = din("w_swa", [D_MODEL, 896])
    wm_d = din("w_mla", [D_MODEL, 640])
    wukv_d = din("w_ukv_r", [128, 1024])
    wuqa_d = din("w_uq_a", [256, 1024])
    wuqb_d = din("w_uq_b", [256, 1024])
    wo_d = din("w_o", [D_MODEL, D_MODEL])
    gains_d = din("gains", [128, NG])
    ident_d = din("ident", [128, 128])
    cbf_d = din("cbf", [128, 512])
    alibi_d = din("alibi", [128, 8 * 384])
    cos_d = din("cos2", [32, SEQ])
    sin_d = din("sin2", [32, SEQ])

    S = Sched()
    S.scopes = scopes
    with ExitStack() as st:
        NW = 53100
        arena_t = st.enter_context(nc.sbuf_tensor("arena", [128, NW], F32))
        A = Arena(arena_t[:, :], NW)
        banks = [st.enter_context(nc.psum_tensor("bank%d" % i, [128, 512], F32))[:] for i in range(8)]
        RB = [S.res("bank%d" % i, persist=True) for i in range(8)]

        def v3(ap, b):
            return ap.rearrange("p (a b) -> p a b", b=b)

        X = v3(A.alloc(8 * SEQ, F32), SEQ)
        XR = [[S.res("X%d_%d" % (c, t), persist=True) for t in range(NT)] for c in range(8)]
        gains = A.alloc(NG, F32)
        ident = A.alloc(128, F32)
        epsc = A.alloc(2, F32)
        esink = A.alloc(8, F32)
        cbf = A.alloc(512, BF16)
        ONES, BD64, BD96, IDB = cbf[:, 0:128], cbf[:, 128:256], cbf[:, 256:384], cbf[:, 384:512]
        Rc = S.res("consts", persist=True)
        cslot = S.slot("consts")
        S.add("sp", lambda e: e.dma_start(out=gains, in_=gains_d), writes=[Rc], dma=True, slot=cslot)
        S.add("sp", lambda e: e.dma_start(out=ident, in_=ident_d), writes=[Rc], dma=True, slot=cslot)
        cslot2 = S.slot("consts_sw")
        S.add("pool", lambda e: e.dma_start(out=cbf, in_=cbf_d), writes=[Rc], dma=True, slot=cslot2)
        S.add("dve", lambda e: e.memset(epsc[:, 0:1], RMS_EPS), writes=[Rc])
        S.add("act", lambda e: e.activation(out=esink, in_=gains[:, G_SINK:G_SINK + 8], func=AF.Exp),
              reads=[Rc], writes=[Rc])
        EPS = epsc[:, 0:1]
        base_mark = A.mark()

        def gcol(i, lo=0, hi=128):
            return gains[lo:hi, i:i + 1]

        def mm(out, lhsT, rhs, start, stop, reads, writes):
            S.add("pe", lambda e: e.matmul(out, lhsT=lhsT, rhs=rhs, start=start, stop=stop),
                  reads=reads, writes=writes)

        def act(out, in_, func, reads, writes, scale=None, bias=None):
            kw = {}
            if scale is not None:
                kw["scale"] = scale
            if bias is not None:
                kw["bias"] = bias
            S.add("act", lambda e: e.activation(out=out, in_=in_, func=func, **kw), reads=reads, writes=writes)

        def stt(eng, out, in0, scalar, in1, op0, op1, reads, writes):
            S.add(eng, lambda e: e.scalar_tensor_tensor(out=out, in0=in0, scalar=scalar, in1=in1, op0=op0, op1=op1),
                  reads=reads, writes=writes)

        def tt(eng, out, in0, in1, op, reads, writes):
            S.add(eng, lambda e: e.tensor_tensor(out=out, in0=in0, in1=in1, op=op), reads=reads, writes=writes)

        def copy(eng, out, in_, reads, writes):
            if eng == "act":
                S.add("act", lambda e: e.activation(out=out, in_=in_, func=AF.Copy), reads=reads, writes=writes)
            else:
                S.add(eng, lambda e: e.tensor_copy(out=out, in_=in_), reads=reads, writes=writes)

        def dma(q, out, in_, reads, writes, slot):
            S.add(q, lambda e: e.dma_start(out=out, in_=in_), reads=reads, writes=writes, dma=True, slot=slot)

        class Scratch:
            def __init__(self, tag):
                self.sqb = [A.alloc(512, BF16) for _ in range(4)]
                self.Rsqb = [S.res(tag + "sqb%d" % i) for i in range(4)]
                self.lnt = A.alloc(512, F32)
                self.Rlnt = S.res(tag + "lnt")
                self.rstd = [A.alloc(512, F32) for _ in range(2)]
                self.Rrstd = [S.res(tag + "rstd%d" % i) for i in range(2)]
                self.k = 0
                self.kr = 0

            def sq(self):
                i = self.k % 4
                self.k += 1
                return self.sqb[i], self.Rsqb[i]

            def rs(self):
                i = self.kr % 2
                self.kr += 1
                return self.rstd[i], self.Rrstd[i]

        def rstd_from(sc, bank_i, rows, scale):
            lo, hi = rows
            r, Rr = sc.rs()
            act(sc.lnt[lo:hi, :], banks[bank_i][lo:hi, :], AF.Ln, [RB[bank_i], Rc], [sc.Rlnt],
                scale=scale, bias=epsc[lo:hi, 0:1])
            act(r[lo:hi, :], sc.lnt[lo:hi, :], AF.Exp, [sc.Rlnt], [Rr], scale=-0.5)
            return r, Rr

        def norm_tile(sc, s_unused, t, gbase, Hc, HR, statc):
            cols = slice(t * TILE, (t + 1) * TILE)
            sb = statc.next()
            for c in range(8):
                q, Rq = sc.sq()
                if c % 2 == 0:
                    act(q, X[:, c, cols], AF.Square, [XR[c][t]], [Rq])
                else:
                    tt("dve", q, X[:, c, cols], X[:, c, cols], ALU.mult, [XR[c][t]], [Rq])
                mm(banks[sb], ONES, q, c == 0, c == 7, [Rq, Rc], [RB[sb]])
            r, Rr = rstd_from(sc, sb, (0, 128), 1.0 / D_MODEL)
            for c in range(8):
                stt("dve", Hc[c], X[:, c, cols], gcol(gbase + c), r, ALU.mult, ALU.mult,
                    [XR[c][t], Rr, Rc], [HR[c]])

        xslots = [S.slot("xin0"), S.slot("xin1")]
        yslots = [S.slot("yout0"), S.slot("yout1")]
        wgslots = [S.slot("wg0"), S.slot("wg1")]
        wuslots = [S.slot("wu0"), S.slot("wu1")]
        wdslots = [S.slot("wd0"), S.slot("wd1")]

        def ffn_segment(sts):
            S.new_phase()
            A.release(base_mark)
            tag = "fs%d_%d_%d_" % sts[0]
            actb = v3(A.alloc(NJ * 1024, BF16), 1024)
            actR = [[S.res(tag + "act%d_%d" % (j, u)) for u in range(2)] for j in range(NJ)]
            H2 = [v3(A.alloc(8 * 1024, BF16), 1024) for _ in range(2)]
            HR2 = [[[S.res(tag + "H%d_%d_%d" % (i, c, u)) for c in range(8)] for u in range(2)] for i in range(2)]
            wgs = [v3(A.alloc(8 * 256, BF16), 256) for _ in range(2)]
            wus = [v3(A.alloc(8 * 256, BF16), 256) for _ in range(2)]
            wds = [v3(A.alloc(NJ * 256, BF16), 256) for _ in range(2)]
            Rwg = [S.res(tag + "wg%d" % i) for i in range(2)]
            Rwu = [S.res(tag + "wu%d" % i) for i in range(2)]
            Rwd = [S.res(tag + "wd%d" % i) for i in range(2)]
            io = [A.alloc(1024, F32) for _ in range(2)]
            Rio = [S.res(tag + "io%d" % i) for i in range(2)]
            sg = [A.alloc(512, BF16) for _ in range(2)]
            Rsg = [S.res(tag + "sg%d" % i) for i in range(2)]
            sc = Scratch(tag)
            gc, uc, dc, mc = Cyc([0, 1]), Cyc([2, 3]), Cyc([4, 5]), Cyc([6, 7])
            cnt = {"sg": 0, "ev": 0, "io": 0}

            def prep(k):
                s, which, stile = sts[k]
                S.tag = "ffn%d_s%d" % (which + 1, s)
                gbase = G_FFN1 if which == 0 else G_FFN2
                H, HR = H2[k % 2], HR2[k % 2]
                for u in range(2):
                    t = stile * 2 + u
                    cols = slice(t * TILE, (t + 1) * TILE)
                    if which == 0:
                        for bb in range(4):
                            tok = s * SEQ + t * TILE + bb * 128
                            tc0 = t * TILE + bb * 128
                            ii = cnt["io"] % 2
                            cnt["io"] += 1
                            dma("sp", io[ii], x_d[tok:tok + 128, :], [], [Rio[ii]], xslots[ii])
                            for half in range(2):
                                b = mc.next()
                                for c4 in range(4):
                                    c = half * 4 + c4
                                    S.add("pe", (lambda b=b, c4=c4, c=c, ii=ii: lambda e: e.transpose(
                                        banks[b][:, c4 * 128:(c4 + 1) * 128], io[ii][:, c * 128:(c + 1) * 128], ident))(),
                                        reads=[Rio[ii], Rc], writes=[RB[b]])
                                copy("act" if cnt["ev"] % 2 == 0 else "dve", X[:, half * 4:half * 4 + 4, tc0:tc0 + 128],
                                     banks[b].rearrange("p (k n) -> p k n", n=128), [RB[b]],
                                     [XR[half * 4 + c4][t] for c4 in range(4)])
                                cnt["ev"] += 1
                    Hc = [H[:, c, u * TILE:(u + 1) * TILE] for c in range(8)]
                    norm_tile(sc, s, t, gbase, Hc, HR[u], mc)

            def phase_a(k):
                s, which, stile = sts[k]
                S.tag = "ffn%d_s%d" % (which + 1, s)
                H, HR = H2[k % 2], HR2[k % 2]
                wgv = wg_d[which].rearrange("(c p) n -> p c n", p=128)
                wuv = wu_d[which].rearrange("(c p) n -> p c n", p=128)
                for jp in range(NJ // 2):
                    sl = jp % 2
                    dma("pool", wgs[sl], wgv[:, :, jp * 256:(jp + 1) * 256], [], [Rwg[sl]], wgslots[sl])
                    dma("pool", wus[sl], wuv[:, :, jp * 256:(jp + 1) * 256], [], [Rwu[sl]], wuslots[sl])
                    for jj in range(2):
                        j = jp * 2 + jj
                        for u in range(2):
                            bg = gc.next()
                            bu = uc.next()
                            for c in range(8):
                                mm(banks[bg], wgs[sl][:, c, jj * 128:(jj + 1) * 128], H[:, c, u * TILE:(u + 1) * TILE],
                                   c == 0, c == 7, [Rwg[sl], HR[u][c]], [RB[bg]])
                            for c in range(8):
                                mm(banks[bu], wus[sl][:, c, jj * 128:(jj + 1) * 128], H[:, c, u * TILE:(u + 1) * TILE],
                                   c == 0, c == 7, [Rwu[sl], HR[u][c]], [RB[bu]])
                            i = cnt["sg"] % 2
                            cnt["sg"] += 1
                            act(sg[i], banks[bg], AF.Silu, [RB[bg]], [Rsg[i]])
                            tt("dve", actb[:, j, u * TILE:(u + 1) * TILE], sg[i], banks[bu], ALU.mult,
                               [Rsg[i], RB[bu]], [actR[j][u]])

            def phase_b(k):
                s, which, stile = sts[k]
                wdv = wd_d[which].rearrange("(j p) n -> p j n", p=128)
                for cp in range(4):
                    S.tag = "ffn%d_s%d" % (which + 1, s)
                    sl = cp % 2
                    dma("pool", wds[sl], wdv[:, :, cp * 256:(cp + 1) * 256], [], [Rwd[sl]], wdslots[sl])
                    for cc in range(2):
                        c = cp * 2 + cc
                        for u in range(2):
                            t = stile * 2 + u
                            cols = slice(t * TILE, (t + 1) * TILE)
                            b = dc.next()
                            for j in range(NJ):
                                mm(banks[b], wds[sl][:, j, cc * 128:(cc + 1) * 128], actb[:, j, u * TILE:(u + 1) * TILE],
                                   j == 0, j == NJ - 1, [Rwd[sl], actR[j][u]], [RB[b]])
                            stt("dve", X[:, c, cols], banks[b], 0.5, X[:, c, cols], ALU.mult, ALU.add,
                                [RB[b], XR[c][t]], [XR[c][t]])
                    if cp == 1 and k + 1 < len(sts):
                        prep(k + 1)

            def output(k):
                s, which, stile = sts[k]
                S.tag = "ffn%d_s%d" % (which + 1, s)
                for u in range(2):
                    t = stile * 2 + u
                    for bb in range(4):
                        tok = s * SEQ + t * TILE + bb * 128
                        tc0 = t * TILE + bb * 128
                        ii = cnt["io"] % 2
                        cnt["io"] += 1
                        for half in range(2):
                            b = mc.next()
                            for c4 in range(4):
                                c = half * 4 + c4
                                S.add("pe", (lambda b=b, c4=c4, c=c, tc0=tc0: lambda e: e.transpose(
                                    banks[b][:, c4 * 128:(c4 + 1) * 128], X[:, c, tc0:tc0 + 128], ident))(),
                                    reads=[XR[c][t], Rc], writes=[RB[b]])
                            copy("act" if cnt["ev"] % 2 == 0 else "dve", io[ii][:, half * 512:(half + 1) * 512],
                                 banks[b], [RB[b]], [Rio[ii]])
                            cnt["ev"] += 1
                        dma("sp", out_d[tok:tok + 128, :], io[ii], [Rio[ii]], [], yslots[ii])

            prep(0)
            for k in range(len(sts)):
                phase_a(k)
                phase_b(k)
                if sts[k][1] == 1:
                    output(k)

        wslots = {n: S.slot(n) for n in ("wsw", "alibi", "wm", "wukv", "wuqa", "wuqb", "wo", "tabc0", "tabs0", "tabc1", "tabs1")}

        def mix_phase(s):
            S.new_phase()
            S.tag = "swa_s%d" % s
            A.release(base_mark)
            tag = "m%d_" % s
            attn_s = v3(A.alloc(4 * SEQ, BF16), SEQ)
            Rattn_s = [[S.res(tag + "as%d_%d" % (c, t)) for t in range(NT)] for c in range(4)]
            mix_mark = A.mark()

            Wsw = v3(A.alloc(8 * 896, BF16), 896)
            RWsw = S.res(tag + "wsw")
            alibi = v3(A.alloc(8 * 384, F32), 384)
            Ralibi = S.res(tag + "alibi")
            exs = [A.alloc(384, F32) for _ in range(3)]
            Rexs = [S.res(tag + "exs%d" % i) for i in range(3)]
            lnd = A.alloc(512, F32)
            Rlnd = S.res(tag + "lnd")
            Qs = v3(A.alloc(4 * SEQ, BF16), SEQ)
            RQs = [[S.res(tag + "qs%d_%d" % (c, t)) for t in range(NT)] for c in range(4)]
            Ks = v3(A.alloc(2 * SEQ, BF16), SEQ)
            RKs = [[S.res(tag + "ks%d_%d" % (g, t)) for t in range(NT)] for g in range(2)]
            Vs = v3(A.alloc(16 * 320, BF16), 320)
            RVs = [S.res(tag + "vs%d" % t) for t in range(NT)]
            RVs1 = S.res(tag + "vs_ones")
            Hs2 = [v3(A.alloc(8 * TILE, BF16), TILE) for _ in range(2)]
            RHs2 = [[S.res(tag + "hs%d_%d" % (i, c)) for c in range(8)] for i in range(2)]
            NPT = 6
            PT = [A.alloc(384, BF16) for _ in range(NPT)]
            RPT = [S.res(tag + "pts%d" % i) for i in range(NPT)]
            dent = A.alloc(512, F32)
            Rdent = S.res(tag + "dent")
            Rr = A.alloc(512, F32)
            RRr = S.res(tag + "Rr")
            sc = Scratch(tag + "s")
            dma("pool", Wsw, wsw_d.rearrange("(c p) n -> p c n", p=128), [], [RWsw], wslots["wsw"])
            dma("sp", alibi, alibi_d.rearrange("p (h n) -> p h n", n=384), [], [Ralibi], wslots["alibi"])
            act(alibi, alibi, AF.Exp, [Ralibi], [Ralibi], scale=0.125)
            for lo in (0, 128, 256):
                S.add("pool", (lambda lo=lo: lambda e: e.memset(Vs[:, :, lo:lo + 64], 1.0))(), writes=[RVs1])
            mainc, statc, vc = Cyc([0, 1, 2, 3]), Cyc([4, 5]), Cyc([6, 7])

            def run_jobs(jobs):
                prev = None
                for jb in jobs:
                    st_ = jb[0]()
                    if prev is not None:
                        prev[0](prev[1])
                    prev = (jb[1], st_)
                if prev is not None:
                    prev[0](prev[1])

            norm_tile(sc, s, 0, G_MIX, [Hs2[0][:, c, :] for c in range(8)], RHs2[0], statc)
            for t in range(NT):
                cols = slice(t * TILE, (t + 1) * TILE)
                Hs, RHs = Hs2[t % 2], RHs2[t % 2]

                def mk_job(wcol, gidx, out_ap, out_res, Hs=Hs, RHs=RHs):
                    def stage_a():
                        b = mainc.next()
                        for c in range(8):
                            mm(banks[b], Wsw[:, c, wcol:wcol + 128], Hs[:, c, :], c == 0, c == 7, [RWsw, RHs[c]], [RB[b]])
                        q, Rq = sc.sq()
                        act(q, banks[b], AF.Square, [RB[b]], [Rq])
                        return (b, q, Rq)

                    def stage_b(st_):
                        b, q, Rq = st_
                        b2 = statc.next()
                        mm(banks[b2], BD64, q, True, True, [Rq, Rc], [RB[b2]])
                        r, Rr_ = rstd_from(sc, b2, (0, 128), 1.0)
                        stt("dve", out_ap, banks[b], gcol(gidx), r, ALU.mult, ALU.mult, [RB[b], Rr_, Rc], [out_res])
                    return (stage_a, stage_b)

                jobs = [mk_job(cq_ * 128, G_SWQ, Qs[:, cq_, cols], RQs[cq_][t]) for cq_ in range(4)]
                jobs += [mk_job(512 + g * 128, G_SWK, Ks[:, g, cols], RKs[g][t]) for g in range(2)]
                run_jobs(jobs[:3])
                if t + 1 < NT:
                    norm_tile(sc, s, t + 1, G_MIX, [Hs2[(t + 1) % 2][:, c, :] for c in range(8)], RHs2[(t + 1) % 2], statc)
                run_jobs(jobs[3:])
                b = vc.next()
                for blk in range(4):
                    for c in range(8):
                        mm(banks[b][:, blk * 128:(blk + 1) * 128], Hs[:, c, blk * 128:(blk + 1) * 128],
                           Wsw[:, c, 768:896], c == 0, c == 7, [RWsw, RHs[c]], [RB[b]])
                bv = banks[b].rearrange("p (k n) -> p k n", n=128)
                copy("act", Vs[:, 4 * t:4 * t + 4, 64:128], bv[:, :, 0:64], [RB[b], RVs1], [RVs[t]])
                copy("dve", Vs[:, 4 * t:4 * t + 4, 192:256], bv[:, :, 64:128], [RB[b], RVs1], [RVs[t]])

            S.tag = "swaattn_s%d" % s
            sbank = [0, 1, 2]
            obank = [3, 4]
            LOOK = 2
            NS = 8 * 16

            def hinfo(h):
                g, half, qc = h // 4, h % 2, h // 2
                r0 = half * 64
                if half == 0:
                    vlo = 64 if g == 0 else 192
                else:
                    vlo = 0 if g == 0 else 128
                return g, half, qc, r0, vlo

            def swa_S(i):
                h, j = i // 16, i % 16
                g, half, qc, r0, vlo = hinfo(h)
                r1 = r0 + 64
                qlo, qhi = max(j - 1, 0), min(j + 1, 15)
                ncol = (qhi - qlo + 1) * 128
                off = (qlo - (j - 1)) * 128
                b = sbank[i % 3]
                qt_res = [RQs[qc][tt_] for tt_ in range((qlo * 128) // TILE, (qhi * 128) // TILE + 1)]
                mm(banks[b][:, 0:ncol], Ks[r0:r1, g, j * 128:(j + 1) * 128], Qs[r0:r1, qc, qlo * 128:(qhi + 1) * 128],
                   True, True, [RKs[g][j // 4]] + qt_res, [RB[b]])
                ex, Rex = exs[i % 3], Rexs[i % 3]
                act(ex[:, 0:ncol], banks[b][:, 0:ncol], AF.Exp, [RB[b]], [Rex], scale=0.125)
                tt("pool" if i % 3 == 0 else "dve", PT[i % NPT][:, 0:ncol], ex[:, 0:ncol], alibi[:, h, off:off + ncol],
                   ALU.mult, [Rex, Ralibi], [RPT[i % NPT]])

            def swa_pv(h, n):
                g, half, qc, r0, vlo = hinfo(h)
                ob = obank[(h * 4 + n // 4) % 2]
                jjs = [jj for jj in (n - 1, n, n + 1) if 0 <= jj < 16]
                for k, jj in enumerate(jjs):
                    qlo = max(jj - 1, 0)
                    off = (n - qlo) * 128
                    pi = (h * 16 + jj) % NPT
                    mm(banks[ob][:, (n % 4) * 128:(n % 4 + 1) * 128], Vs[:, jj, vlo:vlo + 128],
                       PT[pi][:, off:off + 128], k == 0, k == len(jjs) - 1,
                       [RVs[jj // 4], RVs1, RPT[pi]], [RB[ob]])
                if n % 4 == 3:
                    tq = n // 4
                    nr = (r0, r0 + 64)
                    dr = (64 - r0, 128 - r0)
                    act(lnd[dr[0]:dr[1], :], banks[ob][dr[0]:dr[1], :], AF.Ln, [RB[ob], Rc], [Rlnd],
                        scale=1.0, bias=esink[dr[0]:dr[1], h:h + 1])
                    act(Rr[dr[0]:dr[1], :], lnd[dr[0]:dr[1], :], AF.Exp, [Rlnd], [RRr], scale=-1.0)
                    tt("dve", attn_s[nr[0]:nr[1], qc, tq * TILE:(tq + 1) * TILE], banks[ob][nr[0]:nr[1], :],
                       Rr[dr[0]:dr[1], :], ALU.mult, [RB[ob], RRr], [Rattn_s[qc][tq]])

            def swa_PV(i):
                h, j = i // 16, i % 16
                if j >= 1:
                    swa_pv(h, j - 1)
                if j == 15:
                    swa_pv(h, 15)

            for i in range(NS + LOOK):
                if i < NS:
                    swa_S(i)
                if i - LOOK >= 0:
                    swa_PV(i - LOOK)

            S.new_phase()
            S.tag = "m1_s%d" % s
            A.release(mix_mark)
            Kt = v3(A.alloc(8 * SEQ, BF16), SEQ)
            RKn = [[S.res(tag + "kn%d_%d" % (h, t)) for t in range(NT)] for h in range(8)]
            RKp = [[S.res(tag + "kp%d_%d" % (h, t)) for t in range(NT)] for h in range(8)]
            Va = v3(A.alloc(16 * 768, BF16), 768)
            RVa = [S.res(tag + "va%d" % t) for t in range(NT)]
            RVa1 = S.res(tag + "va_ones")
            cq = v3(A.alloc(2 * SEQ, BF16), SEQ)
            Rcq = [[S.res(tag + "cq%d_%d" % (k, t)) for t in range(NT)] for k in range(2)]
            m1_mark = A.mark()
            Wm = v3(A.alloc(8 * 640, BF16), 640)
            RWm = S.res(tag + "wm")
            Wukv = A.alloc(1024, BF16)
            RWukv = S.res(tag + "wukv")
            Hm2 = [v3(A.alloc(8 * TILE, BF16), TILE) for _ in range(2)]
            RHm2 = [[S.res(tag + "hm%d_%d" % (i, c)) for c in range(8)] for i in range(2)]
            ckv2 = [A.alloc(512, BF16) for _ in range(2)]
            Rckv2 = [S.res(tag + "ckv%d" % i) for i in range(2)]
            tabC2 = [A.alloc(512, F32) for _ in range(2)]
            tabS2 = [A.alloc(512, F32) for _ in range(2)]
            RtC2 = [S.res(tag + "tabC%d" % i) for i in range(2)]
            RtS2 = [S.res(tag + "tabS%d" % i) for i in range(2)]
            t1 = A.alloc(512, F32)
            t2 = A.alloc(512, F32)
            Rt1, Rt2 = S.res(tag + "t1"), S.res(tag + "t2")
            kpt = A.alloc(512, BF16)
            Rkpt = S.res(tag + "kpt")
            sc = Scratch(tag + "m")
            dma("pool", Wm, wm_d.rearrange("(c p) n -> p c n", p=128), [], [RWm], wslots["wm"])
            dma("pool", Wukv, wukv_d, [], [RWukv], wslots["wukv"])
            S.add("dve", lambda e: e.tensor_scalar(out=Wm[:, :, 576:592], in0=Wm[:, :, 576:592], scalar1=-1.0,
                                                   scalar2=None, op0=ALU.mult), reads=[RWm], writes=[RWm])
            Va4 = Va.rearrange("p k (i n) -> p k i n", n=192)
            S.add("pool", lambda e: e.memset(Va4[:, :, :, 64:128], 1.0), writes=[RVa1])
            mainc, statc, vc = Cyc([0, 1, 2, 3]), Cyc([4, 5]), Cyc([6, 7])

            def m1_tabs(t):
                cols = slice(t * TILE, (t + 1) * TILE)
                dma("sp", tabC2[t % 2][64:96, :], cos_d[:, cols], [], [RtC2[t % 2]], wslots["tabc%d" % (t % 2)])
                dma("sp", tabS2[t % 2][64:96, :], sin_d[:, cols], [], [RtS2[t % 2]], wslots["tabs%d" % (t % 2)])

            m1_tabs(0)
            norm_tile(sc, s, 0, G_MIX, [Hm2[0][:, c, :] for c in range(8)], RHm2[0], statc)
            for t in range(NT):
                cols = slice(t * TILE, (t + 1) * TILE)
                Hm, RHm = Hm2[t % 2], RHm2[t % 2]
                ckv, Rckv = ckv2[t % 2], Rckv2[t % 2]
                tabC, tabS, RtC, RtS = tabC2[t % 2], tabS2[t % 2], RtC2[t % 2], RtS2[t % 2]

                def proj(wcol, m, Hm=Hm, RHm=RHm):
                    b = mainc.next()
                    for c in range(8):
                        mm(banks[b][0:m, :], Wm[:, c, wcol:wcol + m], Hm[:, c, :], c == 0, c == 7, [RWm, RHm[c]], [RB[b]])
                    return b

                def cq_a():
                    bq0, bq1 = proj(0, 128), proj(128, 128)
                    qa, Rqa = sc.sq()
                    act(qa, banks[bq0], AF.Square, [RB[bq0]], [Rqa])
                    qb, Rqb = sc.sq()
                    act(qb, banks[bq1], AF.Square, [RB[bq1]], [Rqb])
                    return (bq0, bq1, qa, Rqa, qb, Rqb)

                def cq_b(st_, t=t, cols=cols):
                    bq0, bq1, qa, Rqa, qb, Rqb = st_
                    b2 = statc.next()
                    mm(banks[b2], ONES, qa, True, False, [Rqa, Rc], [RB[b2]])
                    mm(banks[b2], ONES, qb, False, True, [Rqb, Rc], [RB[b2]])
                    r, Rr_ = rstd_from(sc, b2, (0, 128), 1.0 / 256)
                    stt("dve", cq[:, 0, cols], banks[bq0], gcol(G_QA), r, ALU.mult, ALU.mult, [RB[bq0], Rr_, Rc], [Rcq[0][t]])
                    stt("dve", cq[:, 1, cols], banks[bq1], gcol(G_QA + 1), r, ALU.mult, ALU.mult, [RB[bq1], Rr_, Rc], [Rcq[1][t]])

                def ckv_a():
                    bkv = proj(256, 128)
                    q, Rq = sc.sq()
                    act(q, banks[bkv], AF.Square, [RB[bkv]], [Rq])
                    return (bkv, q, Rq)

                def ckv_b(st_, ckv=ckv, Rckv=Rckv):
                    bkv, q, Rq = st_
                    b2 = statc.next()
                    mm(banks[b2], ONES, q, True, True, [Rq, Rc], [RB[b2]])
                    r, Rr_ = rstd_from(sc, b2, (0, 128), 1.0 / 128)
                    stt("dve", ckv, banks[bkv], gcol(G_KVA), r, ALU.mult, ALU.mult, [RB[bkv], Rr_, Rc], [Rckv])

                def kpe_a():
                    bka, bkb = proj(384, 128), proj(512, 128)
                    q, Rq = sc.sq()
                    act(q[0:96, :], banks[bka][0:96, :], AF.Square, [RB[bka]], [Rq])
                    return (bka, bkb, q, Rq)

                def kpe_b(st_, t=t, cols=cols, tabC=tabC, tabS=tabS, RtC=RtC, RtS=RtS):
                    bka, bkb, q, Rq = st_
                    b2 = statc.next()
                    mm(banks[b2][0:96, :], BD96[0:96, 0:96], q[0:96, :], True, True, [Rq, Rc], [RB[b2]])
                    r, Rr_ = rstd_from(sc, b2, (0, 96), 1.0)
                    stt("dve", t1[64:96, :], banks[bka][64:96, :], gcol(G_KRS, 64, 96), tabC[64:96, :], ALU.mult, ALU.mult,
                        [RB[bka], RtC, Rc], [Rt1])
                    stt("dve", t2[64:96, :], banks[bkb][64:96, :], gcol(G_KRW, 64, 96), tabS[64:96, :], ALU.mult, ALU.mult,
                        [RB[bkb], RtS, Rc], [Rt2])
                    tt("dve", t1[64:96, :], t1[64:96, :], t2[64:96, :], ALU.add, [Rt1, Rt2], [Rt1])
                    tt("dve", kpt[64:96, :], t1[64:96, :], r[64:96, :], ALU.mult, [Rt1, Rr_], [Rkpt])

                def mk_kn(hp, t=t, cols=cols, ckv=ckv, Rckv=Rckv):
                    def a():
                        b = mainc.next()
                        mm(banks[b], Wukv[:, hp * 128:(hp + 1) * 128], ckv, True, True, [RWukv, Rckv], [RB[b]])
                        q, Rq = sc.sq()
                        act(q, banks[b], AF.Square, [RB[b]], [Rq])
                        return (b, q, Rq)

                    def bfn(st_):
                        b, q, Rq = st_
                        b2 = statc.next()
                        mm(banks[b2], BD64, q, True, True, [Rq, Rc], [RB[b2]])
                        r, Rr_ = rstd_from(sc, b2, (0, 128), 1.0)
                        stt("dve", Kt[0:64, 2 * hp, cols], banks[b][0:64, :], gcol(G_KN, 0, 64), r[0:64, :],
                            ALU.mult, ALU.mult, [RB[b], Rr_, Rc], [RKn[2 * hp][t]])
                        stt("dve", Kt[0:64, 2 * hp + 1, cols], banks[b][64:128, :], gcol(G_KN, 64, 128), r[64:128, :],
                            ALU.mult, ALU.mult, [RB[b], Rr_, Rc], [RKn[2 * hp + 1][t]])
                    return (a, bfn)

                jobs = [(ckv_a, ckv_b), (cq_a, cq_b), (kpe_a, kpe_b)] + [mk_kn(hp) for hp in range(4)]
                run_jobs(jobs[:3])
                for blk in range(4):
                    b = vc.next()
                    mm(banks[b], ckv[:, blk * 128:(blk + 1) * 128], Wukv[:, 512:1024], True, True, [Rckv, RWukv], [RB[b]])
                    bv = banks[b].rearrange("p (i two n) -> p i two n", two=2, n=64)
                    copy("act", Va4[:, 4 * t + blk, :, 0:64], bv[:, :, 0, :], [RB[b], RVa1], [RVa[t]])
                    copy("dve", Va4[:, 4 * t + blk, :, 128:192], bv[:, :, 1, :], [RB[b], RVa1], [RVa[t]])
                if t + 1 < NT:
                    m1_tabs(t + 1)
                    norm_tile(sc, s, t + 1, G_MIX, [Hm2[(t + 1) % 2][:, c, :] for c in range(8)], RHm2[(t + 1) % 2], statc)
                run_jobs(jobs[3:])
                for h in range(8):
                    copy("pool", Kt[64:96, h, cols], kpt[64:96, :], [Rkpt], [RKp[h][t]])

            S.new_phase()
            S.tag = "m2_s%d" % s
            A.release(m1_mark)
            WuqA = v3(A.alloc(2 * 1024, BF16), 1024)
            WuqB = v3(A.alloc(2 * 1024, BF16), 1024)
            RWa, RWb = S.res(tag + "wuqa"), S.res(tag + "wuqb")
            Wo = v3(A.alloc(8 * 1024, BF16), 1024)
            RWo = S.res(tag + "wo")
            Qh = [A.alloc(512, BF16) for _ in range(3)]
            RQh = [S.res(tag + "qh%d" % i) for i in range(3)]
            PTm = [A.alloc(512, BF16) for _ in range(4)]
            RPTm = [S.res(tag + "ptm%d" % i) for i in range(4)]
            attn_m2 = [v3(A.alloc(4 * TILE, BF16), TILE) for _ in range(2)]
            Rattn_m2 = [[S.res(tag + "am%d_%d" % (i, c)) for c in range(4)] for i in range(2)]
            tabC2 = [A.alloc(512, F32)] * 2
            tabS2 = [A.alloc(512, F32)] * 2
            RtC2 = [S.res(tag + "tabCb")] * 2
            RtS2 = [S.res(tag + "tabSb")] * 2
            t1 = A.alloc(512, F32)
            t2 = A.alloc(512, F32)
            Rt1, Rt2 = S.res(tag + "t1b"), S.res(tag + "t2b")
            Rr = A.alloc(512, F32)
            RRr = S.res(tag + "Rrm")
            sc = Scratch(tag + "a")
            dma("pool", WuqA, wuqa_d.rearrange("(c p) n -> p c n", p=128), [], [RWa], wslots["wuqa"])
            dma("pool", WuqB, wuqb_d.rearrange("(c p) n -> p c n", p=128), [], [RWb], wslots["wuqb"])
            dma("pool", Wo, wo_d.rearrange("(c p) n -> p c n", p=128), [], [RWo], wslots["wo"])
            WuqB4 = WuqB.rearrange("p k (h n) -> p k h n", n=128)
            for kc in range(2):
                S.add("dve", (lambda kc=kc: lambda e: e.tensor_scalar(
                    out=WuqB4[:, kc, :, 64:80], in0=WuqB4[:, kc, :, 64:80], scalar1=-1.0, scalar2=None,
                    op0=ALU.mult))(), reads=[RWb], writes=[RWb])
            scale = 1.0 / math.sqrt(96.0)
            sbank = [0, 1, 2]
            obank = [3, 4]
            bA, bB, bM = 5, 6, 7
            bW = bM
            LOOK = 2
            NH = NT * 8
            NS = NH * 16

            def m2_tabs(t):
                cols = slice(t * TILE, (t + 1) * TILE)
                dma("sp", tabC2[t % 2][64:96, :], cos_d[:, cols], [], [RtC2[t % 2]], wslots["tabc%d" % (t % 2)])
                dma("sp", tabS2[t % 2][64:96, :], sin_d[:, cols], [], [RtS2[t % 2]], wslots["tabs%d" % (t % 2)])

            qstate = {}

            def qprod_a(th):
                t, h = th // 8, th % 8
                cols = slice(t * TILE, (t + 1) * TILE)
                for kc in range(2):
                    mm(banks[bA], WuqA[:, kc, h * 128:(h + 1) * 128], cq[:, kc, cols], kc == 0, kc == 1,
                       [RWa, Rcq[kc][t]], [RB[bA]])
                for kc in range(2):
                    mm(banks[bB], WuqB[:, kc, h * 128:(h + 1) * 128], cq[:, kc, cols], kc == 0, kc == 1,
                       [RWb, Rcq[kc][t]], [RB[bB]])
                q, Rq = sc.sq()
                act(q[0:96, :], banks[bA][0:96, :], AF.Square, [RB[bA]], [Rq])
                qstate[th] = (q, Rq)

            def qprod_b(th):
                t, h = th // 8, th % 8
                q, Rq = qstate.pop(th)
                qi = th % 3
                tabC, tabS, RtC, RtS = tabC2[t % 2], tabS2[t % 2], RtC2[t % 2], RtS2[t % 2]
                mm(banks[bM][0:96, :], BD96[0:96, 0:96], q[0:96, :], True, True, [Rq, Rc], [RB[bM]])
                r, Rr_ = rstd_from(sc, bM, (0, 96), 1.0)
                stt("dve", Qh[qi][0:64, :], banks[bA][0:64, :], gcol(G_QN, 0, 64), r[0:64, :], ALU.mult, ALU.mult,
                    [RB[bA], Rr_, Rc], [RQh[qi]])
                stt("dve", t1[64:96, :], banks[bA][64:96, :], gcol(G_QRS, 64, 96), tabC[64:96, :], ALU.mult, ALU.mult,
                    [RB[bA], RtC, Rc], [Rt1])
                stt("dve", t2[64:96, :], banks[bB][64:96, :], gcol(G_QRW, 64, 96), tabS[64:96, :], ALU.mult, ALU.mult,
                    [RB[bB], RtS, Rc], [Rt2])
                tt("dve", t1[64:96, :], t1[64:96, :], t2[64:96, :], ALU.add, [Rt1, Rt2], [Rt1])
                tt("dve", Qh[qi][64:96, :], t1[64:96, :], r[64:96, :], ALU.mult, [Rt1, Rr_], [RQh[qi]])

            def wo_piece(t, m):
                cols = slice(t * TILE, (t + 1) * TILE)
                dc_, c = m // 8, m % 8
                if c < 4:
                    rhs, rr = attn_m2[t % 2][:, c, :], Rattn_m2[t % 2][c]
                else:
                    rhs, rr = attn_s[:, c - 4, cols], Rattn_s[c - 4][t]
                mm(banks[bW], Wo[:, c, dc_ * 128:(dc_ + 1) * 128], rhs, c == 0, c == 7, [RWo, rr], [RB[bW]])
                if c == 7:
                    tt("dve", X[:, dc_, cols], banks[bW], X[:, dc_, cols], ALU.add, [RB[bW], XR[dc_][t]], [XR[dc_][t]])

            def m2_S(i):
                th, kc = i // 16, i % 16
                t, h = th // 8, th % 8
                qi = th % 3
                bs = sbank[i % 3]
                mm(banks[bs], Kt[0:96, h, kc * 128:(kc + 1) * 128], Qh[qi][0:96, :], True, True,
                   [RKn[h][kc // 4], RKp[h][kc // 4], RQh[qi]], [RB[bs]])
                act(PTm[i % 4], banks[bs], AF.Exp, [RB[bs]], [RPTm[i % 4]], scale=scale)
                if th + 1 < NH:
                    if kc == 0:
                        qprod_a(th + 1)
                    if kc == 3:
                        qprod_b(th + 1)
                if h == 6 and kc == 4 and t + 1 < NT:
                    m2_tabs(t + 1)
                if t > 0 and 6 <= kc < 14:
                    wo_piece(t - 1, h * 8 + (kc - 6))

            def m2_PV(i):
                th, kc = i // 16, i % 16
                t, h = th // 8, th % 8
                ob = obank[th % 2]
                half, pair = h % 2, h // 2
                vlo = pair * 192 + (0 if half == 0 else 64)
                mm(banks[ob], Va[:, kc, vlo:vlo + 128], PTm[i % 4], kc == 0, kc == 15,
                   [RVa[kc // 4], RVa1, RPTm[i % 4]], [RB[ob]])
                if kc == 15:
                    nr = (half * 64, half * 64 + 64)
                    dr = (64 - half * 64, 128 - half * 64)
                    r_o, r_i = Rr[nr[0]:nr[1], :], banks[ob][dr[0]:dr[1], :]
                    S.add("dve", lambda e: e.reciprocal(out=r_o, in_=r_i), reads=[RB[ob]], writes=[RRr])
                    tt("dve", attn_m2[t % 2][nr[0]:nr[1], pair, :], banks[ob][nr[0]:nr[1], :], r_o, ALU.mult,
                       [RB[ob], RRr], [Rattn_m2[t % 2][pair]])

            m2_tabs(0)
            qprod_a(0)
            qprod_b(0)
            for i in range(NS + LOOK):
                if i < NS:
                    m2_S(i)
                if i - LOOK >= 0:
                    m2_PV(i - LOOK)
            for m in range(64):
                wo_piece(NT - 1, m)

        ffn_segment([(0, 0, 0), (0, 0, 1)])
        mix_phase(0)
        ffn_segment([(0, 1, 0), (0, 1, 1), (1, 0, 0), (1, 0, 1)])
        mix_phase(1)
        ffn_segment([(1, 1, 0), (1, 1, 1)])
        S.final_wait("sp", yslots)
        S.emit(nc)
    return nc


def _host_constants():
    ident = np.eye(128, dtype=np.float32)
    cbf = np.zeros((128, 512), np.float32)
    cbf[:, 0:128] = 1.0
    bd64 = np.zeros((128, 128), np.float32)
    bd64[0:64, 0:64] = 1.0 / 64
    bd64[64:128, 64:128] = 1.0 / 64
    bd96 = np.zeros((128, 128), np.float32)
    bd96[0:64, 0:64] = 1.0 / 64
    bd96[64:96, 64:96] = 1.0 / 32
    cbf[:, 128:256] = bd64
    cbf[:, 256:384] = bd96
    cbf[:, 384:512] = ident
    i = np.arange(128)[:, None]
    c = np.arange(384)[None, :]
    rel = 128 + i - c
    valid = np.abs(rel) <= 128
    al = np.zeros((128, 8, 384), np.float32)
    for h in range(8):
        al[:, h, :] = np.where(valid, -np.abs(rel) * (2.0 ** (2 - h)), -30000.0)
    pos = np.arange(SEQ, dtype=np.float64)
    inv = 1.0 / (10000.0 ** (np.arange(0, 32, 2, dtype=np.float64) / 32))
    ang = pos[None, :] * inv[:, None]
    cos2 = np.concatenate([np.cos(ang), np.cos(ang)], 0).astype(np.float32)
    sin2 = np.concatenate([np.sin(ang), np.sin(ang)], 0).astype(np.float32)
    return ident, cbf, al.reshape(128, 8 * 384), cos2, sin2


_NC_CACHE = {}


def kernel(x, g_ffn1, w1_gate, w1_up, w1_down, g_mix, w_in, g_q_a, w_uq, g_kv_a, w_ukv,
           g_mla_qn, g_mla_qr, g_mla_kn, g_mla_kr, g_swa_q, g_swa_k, sink, w_o,
           g_ffn2, w2_gate, w2_up, w2_down):
    f = lambda a: np.ascontiguousarray(np.asarray(a, dtype=np.float32))
    x = f(x)
    w_in = f(w_in); w_uq = f(w_uq); w_ukv = f(w_ukv)
    hq, hkv, hkr = w_in[:, 0:256], w_in[:, 256:384], w_in[:, 384:416]
    sq, sk, sv = w_in[:, 416:928], w_in[:, 928:1056], w_in[:, 1056:1184]
    w_swa = np.concatenate([sq, sk[:, 0:64], sk[:, 0:64], sk[:, 64:128], sk[:, 64:128], sv], axis=1)
    pad = hkv[:, 0:64]
    pad2 = hkv[:, 64:96]
    w_mla = np.concatenate([hq, hkv, pad, hkr, pad2, pad, hkr[:, 16:32], hkr[:, 0:16], pad2], axis=1)
    uq = w_uq.reshape(256, 8, 96)
    w_uq_a = np.concatenate([uq, uq[:, :, 0:32]], axis=2).reshape(256, 1024)
    w_uq_b = np.concatenate([uq[:, :, 0:64], uq[:, :, 80:96], uq[:, :, 64:80], uq[:, :, 0:32]], axis=2).reshape(256, 1024)
    ukv = w_ukv.reshape(128, 8, 128)
    w_ukv_r = np.concatenate([ukv[:, :, 0:64].reshape(128, 512), ukv[:, :, 64:128].reshape(128, 512)], axis=1)
    gains = np.ones((128, NG), np.float32)
    gains[:, G_FFN1:G_FFN1 + 8] = f(g_ffn1).reshape(8, 128).T
    gains[:, G_MIX:G_MIX + 8] = f(g_mix).reshape(8, 128).T
    gains[:, G_FFN2:G_FFN2 + 8] = f(g_ffn2).reshape(8, 128).T
    gains[:, G_QA:G_QA + 2] = f(g_q_a).reshape(2, 128).T
    gains[:, G_KVA] = f(g_kv_a)
    gains[0:64, G_QN] = f(g_mla_qn)
    gains[:, G_KN] = np.tile(f(g_mla_kn), 2)
    gains[:, G_SWQ] = np.tile(f(g_swa_q), 2)
    gains[:, G_SWK] = np.tile(f(g_swa_k), 2)
    gqr, gkr = f(g_mla_qr), f(g_mla_kr)
    gains[64:96, G_QRS] = gqr
    gains[64:96, G_QRW] = np.concatenate([gqr[16:32], gqr[0:16]])
    gains[64:96, G_KRS] = gkr
    gains[64:96, G_KRW] = np.concatenate([gkr[16:32], gkr[0:16]])
    gains[:, G_SINK:G_SINK + 8] = np.broadcast_to(f(sink)[None, :], (128, 8))
    ident, cbf, alibi, cos2, sin2 = _host_constants()

    if "nc" not in _NC_CACHE:
        _NC_CACHE["nc"] = build_program()
    nc = _NC_CACHE["nc"]
    shared = {
        "w1_gate": f(w1_gate), "w1_up": f(w1_up), "w1_down": f(w1_down),
        "w2_gate": f(w2_gate), "w2_up": f(w2_up), "w2_down": f(w2_down),
        "w_swa": np.ascontiguousarray(w_swa), "w_mla": np.ascontiguousarray(w_mla),
        "w_ukv_r": np.ascontiguousarray(w_ukv_r), "w_uq_a": np.ascontiguousarray(w_uq_a), "w_uq_b": np.ascontiguousarray(w_uq_b),
        "w_o": f(w_o), "gains": gains, "ident": ident, "cbf": cbf, "alibi": alibi, "cos2": cos2, "sin2": sin2,
    }
    in_maps = []
    for c in range(NCORES):
        m = dict(shared)
        m["x"] = np.ascontiguousarray(x[NSEQ * c:NSEQ * (c + 1)].reshape(NSEQ * SEQ, D_MODEL))
        in_maps.append(m)
    res = run_bass_kernel_spmd(nc, in_maps, core_ids=list(range(NCORES)))
    out = np.stack([np.asarray(r["out"]).reshape(NSEQ, SEQ, D_MODEL) for r in res.results], axis=0)
    return out.reshape(NCORES * NSEQ, SEQ, D_MODEL).astype(np.float32)
```

```python
import math
from contextlib import ExitStack

import numpy as np
import concourse.bass as bass
import concourse.mybir as mybir
from concourse.bass_utils import run_bass_kernel_spmd

F32 = mybir.dt.float32
BF16 = mybir.dt.bfloat16
ALU = mybir.AluOpType
AF = mybir.ActivationFunctionType

ENGS = ("pe", "act", "dve", "pool", "sp")

D_MODEL = 1024
D_FF = 2816
SEQ = 2048
NSEQ = 2
NCORES = 8
TILE = 512
NT = SEQ // TILE
NJ = D_FF // 128
RMS_EPS = 1e-6


class Res:
    __slots__ = ("name", "w", "r")

    def __init__(self, name, inherit=()):
        self.name = name
        self.w = None
        self.r = list(inherit)


class DmaSlot:
    def __init__(self, name):
        self.name = name
        self.count = 0
        self.sem = None


class Op:
    __slots__ = ("eng", "fn", "reads", "writes", "deps", "signal", "sigval",
                 "dma", "slot", "waits", "idx", "tag")

    def __init__(self, eng, fn, reads, writes, dma, slot):
        self.eng = eng
        self.fn = fn
        self.reads = reads
        self.writes = writes
        self.deps = []
        self.signal = False
        self.sigval = None
        self.dma = dma
        self.slot = slot
        self.waits = []


class Sched:
    def __init__(self):
        self.ops = {e: [] for e in ENGS}
        self.all = []
        self.slots = []
        self.phase_res = []
        self.frontier = []
        self._final = None
        self.tag = None
        self.scopes = False

    def slot(self, name):
        s = DmaSlot(name)
        self.slots.append(s)
        return s

    def res(self, name, persist=False):
        if persist:
            return Res(name)
        r = Res(name, self.frontier)
        self.phase_res.append(r)
        return r

    def new_phase(self):
        last = {}
        for r in self.phase_res:
            for op in ([r.w] if r.w is not None else []) + r.r:
                key = ("slot", id(op.slot)) if op.dma else ("eng", op.eng)
                if key not in last or last[key].idx < op.idx:
                    last[key] = op
        for op in self.frontier:
            key = ("slot", id(op.slot)) if op.dma else ("eng", op.eng)
            if key not in last or last[key].idx < op.idx:
                last[key] = op
        self.frontier = list(last.values())

    def _dep(self, op, prod):
        if prod is None or prod is op:
            return
        if (not prod.dma) and (not op.dma) and prod.eng == op.eng:
            return
        op.deps.append(prod)

    def add(self, eng, fn, reads=(), writes=(), dma=False, slot=None):
        op = Op(eng, fn, tuple(reads), tuple(writes), dma, slot)
        op.idx = len(self.all)
        op.tag = self.tag
        for r in op.reads:
            self._dep(op, r.w)
        for w in op.writes:
            self._dep(op, w.w)
            for rd in w.r:
                self._dep(op, rd)
        for r in op.reads:
            r.r.append(op)
        for w in op.writes:
            w.w = op
            w.r = []
        self.ops[eng].append(op)
        self.all.append(op)
        return op

    def final_wait(self, eng, slots):
        self._final = (eng, list(slots))

    def finalize(self):
        for op in self.all:
            for d in op.deps:
                d.signal = True
        cnt = {e: 0 for e in ENGS}
        for op in self.all:
            if op.dma:
                op.slot.count += 16
                op.sigval = op.slot.count
            elif op.signal:
                cnt[op.eng] += 1
                op.sigval = cnt[op.eng]
        seen = {e: {} for e in ENGS}
        for op in self.all:
            need = {}
            for d in op.deps:
                key = ("slot", d.slot) if d.dma else ("eng", d.eng)
                if need.get(key, 0) < d.sigval:
                    need[key] = d.sigval
            for key, v in need.items():
                if seen[op.eng].get(key, 0) >= v:
                    continue
                seen[op.eng][key] = v
                op.waits.append((key, v))

    def emit(self, nc):
        self.finalize()
        with ExitStack() as st:
            esem = {e: st.enter_context(nc.semaphore("s_" + e)) for e in ENGS if e != "sp"}
            for s in self.slots:
                s.sem = st.enter_context(nc.semaphore("d_" + s.name))
            block = st.enter_context(nc.Block())
            fin = self._final

            def run(engname, eng):
                cur = [None, None]

                def set_scope(tag):
                    if not self.scopes or tag == cur[0]:
                        return
                    if cur[1] is not None:
                        cur[1].close()
                        cur[1] = None
                    cur[0] = tag
                    if tag is not None:
                        es = ExitStack()
                        es.enter_context(nc.named_scope(tag))
                        cur[1] = es

                for op in self.ops[engname]:
                    set_scope(op.tag)
                    for key, v in op.waits:
                        sem = key[1].sem if key[0] == "slot" else esem[key[1]]
                        eng.wait_ge(sem, v)
                    ins = op.fn(eng)
                    if op.dma:
                        ins.then_inc(op.slot.sem, 16)
                    elif op.signal:
                        ins.then_inc(esem[op.eng], 1)
                set_scope(None)
                if fin is not None and fin[0] == engname:
                    for s in fin[1]:
                        if s.count:
                            eng.wait_ge(s.sem, s.count)

            @block.tensor
            def _(e):
                run("pe", e)

            @block.scalar
            def _(e):
                run("act", e)

            @block.vector
            def _(e):
                run("dve", e)

            @block.gpsimd
            def _(e):
                run("pool", e)

            @block.sync
            def _(e):
                run("sp", e)


class Cyc:
    def __init__(self, ids):
        self.ids = list(ids)
        self.k = 0

    def next(self):
        v = self.ids[self.k % len(self.ids)]
        self.k += 1
        return v


class Arena:
    def __init__(self, ap2d, nwords):
        self.ap = ap2d
        self.n = nwords
        self.off = 0

    def alloc(self, nelem, dt):
        n4 = nelem if dt == F32 else (nelem + 1) // 2
        assert self.off + n4 <= self.n, ("SBUF arena overflow", self.off, n4, self.n)
        a = self.ap[:, self.off:self.off + n4]
        self.off += n4
        if dt != F32:
            a = a.bitcast(dt)
        return a

    def mark(self):
        return self.off

    def release(self, m):
        self.off = m


G_FFN1, G_MIX, G_FFN2, G_QA, G_KVA = 0, 8, 16, 24, 26
G_QN, G_KN, G_SWQ, G_SWK = 27, 28, 29, 30
G_QRS, G_QRW, G_KRS, G_KRW = 31, 32, 33, 34
G_SINK = 35
NG = 48


def build_program(scopes=False):
    nc = bass.Bass("TRN2", target_bir_lowering=False)
    NTOK = NSEQ * SEQ

    def din(name, shape):
        return nc.dram_tensor(name, list(shape), F32, kind="ExternalInput").ap()

    x_d = din("x", [NTOK, D_MODEL])
    out_d = nc.dram_tensor("out", [NTOK, D_MODEL], F32, kind="ExternalOutput").ap()
    wg_d = [din("w1_gate", [D_MODEL, D_FF]), din("w2_gate", [D_MODEL, D_FF])]
    wu_d = [din("w1_up", [D_MODEL, D_FF]), din("w2_up", [D_MODEL, D_FF])]
    wd_d = [din("w1_down", [D_FF, D_MODEL]), din("w2_down", [D_FF, D_MODEL])]
    wsw_d = din("w_swa", [D_MODEL, 896])
    wm_d = din("w_mla", [D_MODEL, 640])
    wukv_d = din("w_ukv_r", [128, 1024])
    wuqa_d = din("w_uq_a", [256, 1024])
    wuqb_d = din("w_uq_b", [256, 1024])
    wo_d = din("w_o", [D_MODEL, D_MODEL])
    gains_d = din("gains", [128, NG])
    ident_d = din("ident", [128, 128])
    cbf_d = din("cbf", [128, 512])
    alibi_d = din("alibi", [128, 8 * 384])
    cos_d = din("cos2", [32, SEQ])
    sin_d = din("sin2", [32, SEQ])

    S = Sched()
    S.scopes = scopes
    with ExitStack() as st:
        NW = 53100
        arena_t = st.enter_context(nc.sbuf_tensor("arena", [128, NW], F32))
        A = Arena(arena_t[:, :], NW)
        banks = [st.enter_context(nc.psum_tensor("bank%d" % i, [128, 512], F32))[:] for i in range(8)]
        RB = [S.res("bank%d" % i, persist=True) for i in range(8)]

        def v3(ap, b):
            return ap.rearrange("p (a b) -> p a b", b=b)

        X = v3(A.alloc(8 * SEQ, F32), SEQ)
        XR = [[S.res("X%d_%d" % (c, t), persist=True) for t in range(NT)] for c in range(8)]
        gains = A.alloc(NG, F32)
        ident = A.alloc(128, F32)
        epsc = A.alloc(2, F32)
        esink = A.alloc(8, F32)
        cbf = A.alloc(512, BF16)
        ONES, BD64, BD96, IDB = cbf[:, 0:128], cbf[:, 128:256], cbf[:, 256:384], cbf[:, 384:512]
        Rc = S.res("consts", persist=True)
        cslot = S.slot("consts")
        S.add("sp", lambda e: e.dma_start(out=gains, in_=gains_d), writes=[Rc], dma=True, slot=cslot)
        S.add("sp", lambda e: e.dma_start(out=ident, in_=ident_d), writes=[Rc], dma=True, slot=cslot)
        cslot2 = S.slot("consts_sw")
        S.add("pool", lambda e: e.dma_start(out=cbf, in_=cbf_d), writes=[Rc], dma=True, slot=cslot2)
        S.add("dve", lambda e: e.memset(epsc[:, 0:1], RMS_EPS), writes=[Rc])
        S.add("act", lambda e: e.activation(out=esink, in_=gains[:, G_SINK:G_SINK + 8], func=AF.Exp),
              reads=[Rc], writes=[Rc])
        EPS = epsc[:, 0:1]
        base_mark = A.mark()

        def gcol(i, lo=0, hi=128):
            return gains[lo:hi, i:i + 1]

        def mm(out, lhsT, rhs, start, stop, reads, writes):
            S.add("pe", lambda e: e.matmul(out, lhsT=lhsT, rhs=rhs, start=start, stop=stop),
                  reads=reads, writes=writes)

        def act(out, in_, func, reads, writes, scale=None, bias=None):
            kw = {}
            if scale is not None:
                kw["scale"] = scale
            if bias is not None:
                kw["bias"] = bias
            S.add("act", lambda e: e.activation(out=out, in_=in_, func=func, **kw), reads=reads, writes=writes)

        def stt(eng, out, in0, scalar, in1, op0, op1, reads, writes):
            S.add(eng, lambda e: e.scalar_tensor_tensor(out=out, in0=in0, scalar=scalar, in1=in1, op0=op0, op1=op1),
                  reads=reads, writes=writes)

        def tt(eng, out, in0, in1, op, reads, writes):
            S.add(eng, lambda e: e.tensor_tensor(out=out, in0=in0, in1=in1, op=op), reads=reads, writes=writes)

        def copy(eng, out, in_, reads, writes):
            if eng == "act":
                S.add("act", lambda e: e.activation(out=out, in_=in_, func=AF.Copy), reads=reads, writes=writes)
            else:
                S.add(eng, lambda e: e.tensor_copy(out=out, in_=in_), reads=reads, writes=writes)

        def dma(q, out, in_, reads, writes, slot):
            S.add(q, lambda e: e.dma_start(out=out, in_=in_), reads=reads, writes=writes, dma=True, slot=slot)

        class Scratch:
            def __init__(self, tag):
                self.sqb = [A.alloc(512, BF16) for _ in range(4)]
                self.Rsqb = [S.res(tag + "sqb%d" % i) for i in range(4)]
                self.lnt = A.alloc(512, F32)
                self.Rlnt = S.res(tag + "lnt")
                self.rstd = [A.alloc(512, F32) for _ in range(2)]
                self.Rrstd = [S.res(tag + "rstd%d" % i) for i in range(2)]
                self.k = 0
                self.kr = 0

            def sq(self):
                i = self.k % 4
                self.k += 1
                return self.sqb[i], self.Rsqb[i]

            def rs(self):
                i = self.kr % 2
                self.kr += 1
                return self.rstd[i], self.Rrstd[i]

        def rstd_from(sc, bank_i, rows, scale):
            lo, hi = rows
            r, Rr = sc.rs()
            act(sc.lnt[lo:hi, :], banks[bank_i][lo:hi, :], AF.Ln, [RB[bank_i], Rc], [sc.Rlnt],
                scale=scale, bias=epsc[lo:hi, 0:1])
            act(r[lo:hi, :], sc.lnt[lo:hi, :], AF.Exp, [sc.Rlnt], [Rr], scale=-0.5)
            return r, Rr

        def norm_tile(sc, s_unused, t, gbase, Hc, HR, statc):
            cols = slice(t * TILE, (t + 1) * TILE)
            sb = statc.next()
            for c in range(8):
                q, Rq = sc.sq()
                if c % 2 == 0:
                    act(q, X[:, c, cols], AF.Square, [XR[c][t]], [Rq])
                else:
                    tt("dve", q, X[:, c, cols], X[:, c, cols], ALU.mult, [XR[c][t]], [Rq])
                mm(banks[sb], ONES, q, c == 0, c == 7, [Rq, Rc], [RB[sb]])
            r, Rr = rstd_from(sc, sb, (0, 128), 1.0 / D_MODEL)
            for c in range(8):
                stt("dve", Hc[c], X[:, c, cols], gcol(gbase + c), r, ALU.mult, ALU.mult,
                    [XR[c][t], Rr, Rc], [HR[c]])

        xslots = [S.slot("xin0"), S.slot("xin1")]
        yslots = [S.slot("yout0"), S.slot("yout1")]
        wgslots = [S.slot("wg0"), S.slot("wg1")]
        wuslots = [S.slot("wu0"), S.slot("wu1")]
        wdslots = [S.slot("wd0"), S.slot("wd1")]

        def ffn_segment(sts):
            S.new_phase()
            A.release(base_mark)
            tag = "fs%d_%d_%d_" % sts[0]
            actb = v3(A.alloc(NJ * 1024, BF16), 1024)
            actR = [[S.res(tag + "act%d_%d" % (j, u)) for u in range(2)] for j in range(NJ)]
            H2 = [v3(A.alloc(8 * 1024, BF16), 1024) for _ in range(2)]
            HR2 = [[[S.res(tag + "H%d_%d_%d" % (i, c, u)) for c in range(8)] for u in range(2)] for i in range(2)]
            wgs = [v3(A.alloc(8 * 256, BF16), 256) for _ in range(2)]
            wus = [v3(A.alloc(8 * 256, BF16), 256) for _ in range(2)]
            wds = [v3(A.alloc(NJ * 256, BF16), 256) for _ in range(2)]
            Rwg = [S.res(tag + "wg%d" % i) for i in range(2)]
            Rwu = [S.res(tag + "wu%d" % i) for i in range(2)]
            Rwd = [S.res(tag + "wd%d" % i) for i in range(2)]
            io = [A.alloc(1024, F32) for _ in range(2)]
            Rio = [S.res(tag + "io%d" % i) for i in range(2)]
            sg = [A.alloc(512, BF16) for _ in range(2)]
            Rsg = [S.res(tag + "sg%d" % i) for i in range(2)]
            sc = Scratch(tag)
            gc, uc, dc, mc = Cyc([0, 1]), Cyc([2, 3]), Cyc([4, 5]), Cyc([6, 7])
            cnt = {"sg": 0, "ev": 0, "io": 0, "cur": 0, "cur_o": 0}

            def prep_pieces(k):
                s, which, stile = sts[k]
                ptag = "ffn%d_s%d" % (which + 1, s)
                gbase = G_FFN1 if which == 0 else G_FFN2
                H, HR = H2[k % 2], HR2[k % 2]
                pcs = []
                for u in range(2):
                    t = stile * 2 + u
                    cols = slice(t * TILE, (t + 1) * TILE)
                    if which == 0:
                        for bb in range(4):
                            for half in range(2):
                                def pT(bb=bb, half=half, t=t):
                                    tok = s * SEQ + t * TILE + bb * 128
                                    tc0 = t * TILE + bb * 128
                                    if half == 0:
                                        ii = cnt["io"] % 2
                                        cnt["io"] += 1
                                        cnt["cur"] = ii
                                        dma("sp", io[ii], x_d[tok:tok + 128, :], [], [Rio[ii]], xslots[ii])
                                    ii = cnt["cur"]
                                    b = mc.next()
                                    for c4 in range(4):
                                        c = half * 4 + c4
                                        S.add("pe", (lambda b=b, c4=c4, c=c, ii=ii: lambda e: e.transpose(
                                            banks[b][:, c4 * 128:(c4 + 1) * 128], io[ii][:, c * 128:(c + 1) * 128], ident))(),
                                            reads=[Rio[ii], Rc], writes=[RB[b]])
                                    copy("act" if cnt["ev"] % 2 == 0 else "dve", X[:, half * 4:half * 4 + 4, tc0:tc0 + 128],
                                         banks[b].rearrange("p (k n) -> p k n", n=128), [RB[b]],
                                         [XR[half * 4 + c4][t] for c4 in range(4)])
                                    cnt["ev"] += 1
                                pcs.append(pT)
                    st_ = {}

                    def p_sq(part, t=t, cols=cols, st_=st_):
                        if part == 0:
                            st_["sb"] = mc.next()
                        st_["q%d" % part] = []
                        for c in range(part * 4, part * 4 + 4):
                            q, Rq = sc.sq()
                            if c % 2 == 0:
                                act(q, X[:, c, cols], AF.Square, [XR[c][t]], [Rq])
                            else:
                                tt("dve", q, X[:, c, cols], X[:, c, cols], ALU.mult, [XR[c][t]], [Rq])
                            st_["q%d" % part].append((q, Rq))

                    def p_stat(part, st_=st_):
                        sb = st_["sb"]
                        for i, (q, Rq) in enumerate(st_["q%d" % part]):
                            c = part * 4 + i
                            mm(banks[sb], ONES, q, c == 0, c == 7, [Rq, Rc], [RB[sb]])

                    def p_fin(t=t, cols=cols, u=u, st_=st_):
                        r, Rr_ = rstd_from(sc, st_["sb"], (0, 128), 1.0 / D_MODEL)
                        for c in range(8):
                            stt("dve", H[:, c, u * TILE:(u + 1) * TILE], X[:, c, cols], gcol(gbase + c), r, ALU.mult, ALU.mult,
                                [XR[c][t], Rr_, Rc], [HR[u][c]])
                    pcs += [lambda p_sq=p_sq: p_sq(0), lambda p_stat=p_stat: p_stat(0),
                            lambda p_sq=p_sq: p_sq(1), lambda p_stat=p_stat: p_stat(1), p_fin]

                def wrap(f):
                    def g():
                        old = S.tag
                        S.tag = ptag
                        f()
                        S.tag = old
                    return g
                return [wrap(f) for f in pcs]

            def output_pieces(k):
                s, which, stile = sts[k]
                ptag = "ffn%d_s%d" % (which + 1, s)
                pcs = []
                for u in range(2):
                    t = stile * 2 + u
                    for bb in range(4):
                        for half in range(2):
                            def pO(t=t, bb=bb, half=half):
                                tok = s * SEQ + t * TILE + bb * 128
                                tc0 = t * TILE + bb * 128
                                if half == 0:
                                    cnt["cur_o"] = cnt["io"] % 2
                                    cnt["io"] += 1
                                ii = cnt["cur_o"]
                                b = mc.next()
                                for c4 in range(4):
                                    c = half * 4 + c4
                                    S.add("pe", (lambda b=b, c4=c4, c=c, tc0=tc0: lambda e: e.transpose(
                                        banks[b][:, c4 * 128:(c4 + 1) * 128], X[:, c, tc0:tc0 + 128], ident))(),
                                        reads=[XR[c][t], Rc], writes=[RB[b]])
                                copy("act" if cnt["ev"] % 2 == 0 else "dve", io[ii][:, half * 512:(half + 1) * 512],
                                     banks[b], [RB[b]], [Rio[ii]])
                                cnt["ev"] += 1
                                if half == 1:
                                    dma("sp", out_d[tok:tok + 128, :], io[ii], [Rio[ii]], [], yslots[ii])
                            pcs.append(pO)

                def wrap(f):
                    def g():
                        old = S.tag
                        S.tag = ptag
                        f()
                        S.tag = old
                    return g
                return [wrap(f) for f in pcs]

            def phase_a(k, pieces=()):
                pieces = list(pieces)
                s, which, stile = sts[k]
                S.tag = "ffn%d_s%d" % (which + 1, s)
                H, HR = H2[k % 2], HR2[k % 2]
                wgv = wg_d[which].rearrange("(c p) n -> p c n", p=128)
                wuv = wu_d[which].rearrange("(c p) n -> p c n", p=128)
                for jp in range(NJ // 2):
                    sl = jp % 2
                    dma("pool", wgs[sl], wgv[:, :, jp * 256:(jp + 1) * 256], [], [Rwg[sl]], wgslots[sl])
                    dma("pool", wus[sl], wuv[:, :, jp * 256:(jp + 1) * 256], [], [Rwu[sl]], wuslots[sl])
                    for jj in range(2):
                        j = jp * 2 + jj
                        for u in range(2):
                            bg = gc.next()
                            bu = uc.next()
                            for c in range(8):
                                mm(banks[bg], wgs[sl][:, c, jj * 128:(jj + 1) * 128], H[:, c, u * TILE:(u + 1) * TILE],
                                   c == 0, c == 7, [Rwg[sl], HR[u][c]], [RB[bg]])
                            for c in range(8):
                                mm(banks[bu], wus[sl][:, c, jj * 128:(jj + 1) * 128], H[:, c, u * TILE:(u + 1) * TILE],
                                   c == 0, c == 7, [Rwu[sl], HR[u][c]], [RB[bu]])
                            i = cnt["sg"] % 2
                            cnt["sg"] += 1
                            act(sg[i], banks[bg], AF.Silu, [RB[bg]], [Rsg[i]])
                            tt("dve", actb[:, j, u * TILE:(u + 1) * TILE], sg[i], banks[bu], ALU.mult,
                               [Rsg[i], RB[bu]], [actR[j][u]])
                            if pieces:
                                pieces.pop(0)()
                while pieces:
                    pieces.pop(0)()

            def phase_b(k, pieces=()):
                pieces = list(pieces)
                s, which, stile = sts[k]
                wdv = wd_d[which].rearrange("(j p) n -> p j n", p=128)
                for cp in range(4):
                    S.tag = "ffn%d_s%d" % (which + 1, s)
                    sl = cp % 2
                    dma("pool", wds[sl], wdv[:, :, cp * 256:(cp + 1) * 256], [], [Rwd[sl]], wdslots[sl])
                    for cc in range(2):
                        c = cp * 2 + cc
                        for u in range(2):
                            t = stile * 2 + u
                            cols = slice(t * TILE, (t + 1) * TILE)
                            b = dc.next()
                            for j in range(NJ):
                                mm(banks[b], wds[sl][:, j, cc * 128:(cc + 1) * 128], actb[:, j, u * TILE:(u + 1) * TILE],
                                   j == 0, j == NJ - 1, [Rwd[sl], actR[j][u]], [RB[b]])
                            stt("dve", X[:, c, cols], banks[b], 0.5, X[:, c, cols], ALU.mult, ALU.add,
                                [RB[b], XR[c][t]], [XR[c][t]])
                            for _ in range(2):
                                if pieces:
                                    pieces.pop(0)()
                while pieces:
                    pieces.pop(0)()

            for p in prep_pieces(0):
                p()
            pend_out = []
            for k in range(len(sts)):
                phase_a(k, pend_out)
                nxt = prep_pieces(k + 1) if k + 1 < len(sts) else []
                phase_b(k, nxt)
                pend_out = output_pieces(k) if sts[k][1] == 1 else []
            for p in pend_out:
                p()

        wslots = {n: S.slot(n) for n in ("wsw", "alibi", "wm", "wukv", "wuqa", "wuqb", "wo", "tabc0", "tabs0", "tabc1", "tabs1")}

        def mix_phase(s):
            S.new_phase()
            S.tag = "swa_s%d" % s
            A.release(base_mark)
            tag = "m%d_" % s
            attn_s = v3(A.alloc(4 * SEQ, BF16), SEQ)
            Rattn_s = [[S.res(tag + "as%d_%d" % (c, t)) for t in range(NT)] for c in range(4)]
            mix_mark = A.mark()

            Wsw = v3(A.alloc(8 * 896, BF16), 896)
            RWsw = S.res(tag + "wsw")
            alibi = v3(A.alloc(8 * 384, F32), 384)
            Ralibi = S.res(tag + "alibi")
            exs = [A.alloc(384, F32) for _ in range(3)]
            Rexs = [S.res(tag + "exs%d" % i) for i in range(3)]
            lnd = A.alloc(512, F32)
            Rlnd = S.res(tag + "lnd")
            Qs = v3(A.alloc(4 * SEQ, BF16), SEQ)
            RQs = [[S.res(tag + "qs%d_%d" % (c, t)) for t in range(NT)] for c in range(4)]
            Ks = v3(A.alloc(2 * SEQ, BF16), SEQ)
            RKs = [[S.res(tag + "ks%d_%d" % (g, t)) for t in range(NT)] for g in range(2)]
            Vs = v3(A.alloc(16 * 320, BF16), 320)
            RVs = [S.res(tag + "vs%d" % t) for t in range(NT)]
            RVs1 = S.res(tag + "vs_ones")
            Hs2 = [v3(A.alloc(8 * TILE, BF16), TILE) for _ in range(2)]
            RHs2 = [[S.res(tag + "hs%d_%d" % (i, c)) for c in range(8)] for i in range(2)]
            NPT = 6
            PT = [A.alloc(384, BF16) for _ in range(NPT)]
            RPT = [S.res(tag + "pts%d" % i) for i in range(NPT)]
            dent = A.alloc(512, F32)
            Rdent = S.res(tag + "dent")
            Rr = A.alloc(512, F32)
            RRr = S.res(tag + "Rr")
            sc = Scratch(tag + "s")
            dma("pool", Wsw, wsw_d.rearrange("(c p) n -> p c n", p=128), [], [RWsw], wslots["wsw"])
            dma("sp", alibi, alibi_d.rearrange("p (h n) -> p h n", n=384), [], [Ralibi], wslots["alibi"])
            act(alibi, alibi, AF.Exp, [Ralibi], [Ralibi], scale=0.125)
            for lo in (0, 128, 256):
                S.add("pool", (lambda lo=lo: lambda e: e.memset(Vs[:, :, lo:lo + 64], 1.0))(), writes=[RVs1])
            mainc, statc, vc = Cyc([0, 1, 2, 3]), Cyc([4, 5]), Cyc([6, 7])

            def run_jobs(jobs):
                prev = None
                for jb in jobs:
                    st_ = jb[0]()
                    if prev is not None:
                        prev[0](prev[1])
                    prev = (jb[1], st_)
                if prev is not None:
                    prev[0](prev[1])

            norm_tile(sc, s, 0, G_MIX, [Hs2[0][:, c, :] for c in range(8)], RHs2[0], statc)
            for t in range(NT):
                cols = slice(t * TILE, (t + 1) * TILE)
                Hs, RHs = Hs2[t % 2], RHs2[t % 2]

                def mk_job(wcol, gidx, out_ap, out_res, Hs=Hs, RHs=RHs):
                    def stage_a():
                        b = mainc.next()
                        for c in range(8):
                            mm(banks[b], Wsw[:, c, wcol:wcol + 128], Hs[:, c, :], c == 0, c == 7, [RWsw, RHs[c]], [RB[b]])
                        q, Rq = sc.sq()
                        act(q, banks[b], AF.Square, [RB[b]], [Rq])
                        return (b, q, Rq)

                    def stage_b(st_):
                        b, q, Rq = st_
                        b2 = statc.next()
                        mm(banks[b2], BD64, q, True, True, [Rq, Rc], [RB[b2]])
                        r, Rr_ = rstd_from(sc, b2, (0, 128), 1.0)
                        stt("dve", out_ap, banks[b], gcol(gidx), r, ALU.mult, ALU.mult, [RB[b], Rr_, Rc], [out_res])
                    return (stage_a, stage_b)

                jobs = [mk_job(cq_ * 128, G_SWQ, Qs[:, cq_, cols], RQs[cq_][t]) for cq_ in range(4)]
                jobs += [mk_job(512 + g * 128, G_SWK, Ks[:, g, cols], RKs[g][t]) for g in range(2)]
                run_jobs(jobs[:3])
                if t + 1 < NT:
                    norm_tile(sc, s, t + 1, G_MIX, [Hs2[(t + 1) % 2][:, c, :] for c in range(8)], RHs2[(t + 1) % 2], statc)
                run_jobs(jobs[3:])
                b = vc.next()
                for blk in range(4):
                    for c in range(8):
                        mm(banks[b][:, blk * 128:(blk + 1) * 128], Hs[:, c, blk * 128:(blk + 1) * 128],
                           Wsw[:, c, 768:896], c == 0, c == 7, [RWsw, RHs[c]], [RB[b]])
                bv = banks[b].rearrange("p (k n) -> p k n", n=128)
                copy("act", Vs[:, 4 * t:4 * t + 4, 64:128], bv[:, :, 0:64], [RB[b], RVs1], [RVs[t]])
                copy("dve", Vs[:, 4 * t:4 * t + 4, 192:256], bv[:, :, 64:128], [RB[b], RVs1], [RVs[t]])

            S.tag = "swaattn_s%d" % s
            sbank = [0, 1, 2]
            obank = [3, 4]
            LOOK = 2
            NS = 8 * 16

            def hinfo(h):
                g, half, qc = h // 4, h % 2, h // 2
                r0 = half * 64
                if half == 0:
                    vlo = 64 if g == 0 else 192
                else:
                    vlo = 0 if g == 0 else 128
                return g, half, qc, r0, vlo

            def swa_S(i):
                h, j = i // 16, i % 16
                g, half, qc, r0, vlo = hinfo(h)
                r1 = r0 + 64
                qlo, qhi = max(j - 1, 0), min(j + 1, 15)
                ncol = (qhi - qlo + 1) * 128
                off = (qlo - (j - 1)) * 128
                b = sbank[i % 3]
                qt_res = [RQs[qc][tt_] for tt_ in range((qlo * 128) // TILE, (qhi * 128) // TILE + 1)]
                mm(banks[b][:, 0:ncol], Ks[r0:r1, g, j * 128:(j + 1) * 128], Qs[r0:r1, qc, qlo * 128:(qhi + 1) * 128],
                   True, True, [RKs[g][j // 4]] + qt_res, [RB[b]])
                ex, Rex = exs[i % 3], Rexs[i % 3]
                act(ex[:, 0:ncol], banks[b][:, 0:ncol], AF.Exp, [RB[b]], [Rex], scale=0.125)
                tt("pool" if i % 3 == 0 else "dve", PT[i % NPT][:, 0:ncol], ex[:, 0:ncol], alibi[:, h, off:off + ncol],
                   ALU.mult, [Rex, Ralibi], [RPT[i % NPT]])

            def swa_pv(h, n):
                g, half, qc, r0, vlo = hinfo(h)
                ob = obank[(h * 4 + n // 4) % 2]
                jjs = [jj for jj in (n - 1, n, n + 1) if 0 <= jj < 16]
                for k, jj in enumerate(jjs):
                    qlo = max(jj - 1, 0)
                    off = (n - qlo) * 128
                    pi = (h * 16 + jj) % NPT
                    mm(banks[ob][:, (n % 4) * 128:(n % 4 + 1) * 128], Vs[:, jj, vlo:vlo + 128],
                       PT[pi][:, off:off + 128], k == 0, k == len(jjs) - 1,
                       [RVs[jj // 4], RVs1, RPT[pi]], [RB[ob]])
                if n % 4 == 3:
                    tq = n // 4
                    nr = (r0, r0 + 64)
                    dr = (64 - r0, 128 - r0)
                    act(lnd[dr[0]:dr[1], :], banks[ob][dr[0]:dr[1], :], AF.Ln, [RB[ob], Rc], [Rlnd],
                        scale=1.0, bias=esink[dr[0]:dr[1], h:h + 1])
                    act(Rr[dr[0]:dr[1], :], lnd[dr[0]:dr[1], :], AF.Exp, [Rlnd], [RRr], scale=-1.0)
                    tt("dve", attn_s[nr[0]:nr[1], qc, tq * TILE:(tq + 1) * TILE], banks[ob][nr[0]:nr[1], :],
                       Rr[dr[0]:dr[1], :], ALU.mult, [RB[ob], RRr], [Rattn_s[qc][tq]])

            def swa_PV(i):
                h, j = i // 16, i % 16
                if j >= 1:
                    swa_pv(h, j - 1)
                if j == 15:
                    swa_pv(h, 15)

            for i in range(NS + LOOK):
                if i < NS:
                    swa_S(i)
                if i - LOOK >= 0:
                    swa_PV(i - LOOK)

            S.new_phase()
            S.tag = "m1_s%d" % s
            A.release(mix_mark)
            Kt = v3(A.alloc(8 * SEQ, BF16), SEQ)
            RKn = [[S.res(tag + "kn%d_%d" % (h, t)) for t in range(NT)] for h in range(8)]
            RKp = [[S.res(tag + "kp%d_%d" % (h, t)) for t in range(NT)] for h in range(8)]
            Va = v3(A.alloc(16 * 768, BF16), 768)
            RVa = [S.res(tag + "va%d" % t) for t in range(NT)]
            RVa1 = S.res(tag + "va_ones")
            cq = v3(A.alloc(2 * SEQ, BF16), SEQ)
            Rcq = [[S.res(tag + "cq%d_%d" % (k, t)) for t in range(NT)] for k in range(2)]
            m1_mark = A.mark()
            Wm = v3(A.alloc(8 * 640, BF16), 640)
            RWm = S.res(tag + "wm")
            Wukv = A.alloc(1024, BF16)
            RWukv = S.res(tag + "wukv")
            Hm2 = [v3(A.alloc(8 * TILE, BF16), TILE) for _ in range(2)]
            RHm2 = [[S.res(tag + "hm%d_%d" % (i, c)) for c in range(8)] for i in range(2)]
            ckv2 = [A.alloc(512, BF16) for _ in range(2)]
            Rckv2 = [S.res(tag + "ckv%d" % i) for i in range(2)]
            tabC2 = [A.alloc(512, F32) for _ in range(2)]
            tabS2 = [A.alloc(512, F32) for _ in range(2)]
            RtC2 = [S.res(tag + "tabC%d" % i) for i in range(2)]
            RtS2 = [S.res(tag + "tabS%d" % i) for i in range(2)]
            t1 = A.alloc(512, F32)
            t2 = A.alloc(512, F32)
            Rt1, Rt2 = S.res(tag + "t1"), S.res(tag + "t2")
            kpt = A.alloc(512, BF16)
            Rkpt = S.res(tag + "kpt")
            sc = Scratch(tag + "m")
            dma("pool", Wm, wm_d.rearrange("(c p) n -> p c n", p=128), [], [RWm], wslots["wm"])
            dma("pool", Wukv, wukv_d, [], [RWukv], wslots["wukv"])
            S.add("dve", lambda e: e.tensor_scalar(out=Wm[:, :, 576:592], in0=Wm[:, :, 576:592], scalar1=-1.0,
                                                   scalar2=None, op0=ALU.mult), reads=[RWm], writes=[RWm])
            Va4 = Va.rearrange("p k (i n) -> p k i n", n=192)
            S.add("pool", lambda e: e.memset(Va4[:, :, :, 64:128], 1.0), writes=[RVa1])
            mainc, statc, vc = Cyc([0, 1, 2, 3]), Cyc([4, 5]), Cyc([6, 7])

            def m1_tabs(t):
                cols = slice(t * TILE, (t + 1) * TILE)
                dma("sp", tabC2[t % 2][64:96, :], cos_d[:, cols], [], [RtC2[t % 2]], wslots["tabc%d" % (t % 2)])
                dma("sp", tabS2[t % 2][64:96, :], sin_d[:, cols], [], [RtS2[t % 2]], wslots["tabs%d" % (t % 2)])

            m1_tabs(0)
            norm_tile(sc, s, 0, G_MIX, [Hm2[0][:, c, :] for c in range(8)], RHm2[0], statc)
            for t in range(NT):
                cols = slice(t * TILE, (t + 1) * TILE)
                Hm, RHm = Hm2[t % 2], RHm2[t % 2]
                ckv, Rckv = ckv2[t % 2], Rckv2[t % 2]
                tabC, tabS, RtC, RtS = tabC2[t % 2], tabS2[t % 2], RtC2[t % 2], RtS2[t % 2]

                def proj(wcol, m, Hm=Hm, RHm=RHm):
                    b = mainc.next()
                    for c in range(8):
                        mm(banks[b][0:m, :], Wm[:, c, wcol:wcol + m], Hm[:, c, :], c == 0, c == 7, [RWm, RHm[c]], [RB[b]])
                    return b

                def cq_a():
                    bq0, bq1 = proj(0, 128), proj(128, 128)
                    qa, Rqa = sc.sq()
                    act(qa, banks[bq0], AF.Square, [RB[bq0]], [Rqa])
                    qb, Rqb = sc.sq()
                    act(qb, banks[bq1], AF.Square, [RB[bq1]], [Rqb])
                    return (bq0, bq1, qa, Rqa, qb, Rqb)

                def cq_b(st_, t=t, cols=cols):
                    bq0, bq1, qa, Rqa, qb, Rqb = st_
                    b2 = statc.next()
                    mm(banks[b2], ONES, qa, True, False, [Rqa, Rc], [RB[b2]])
                    mm(banks[b2], ONES, qb, False, True, [Rqb, Rc], [RB[b2]])
                    r, Rr_ = rstd_from(sc, b2, (0, 128), 1.0 / 256)
                    stt("dve", cq[:, 0, cols], banks[bq0], gcol(G_QA), r, ALU.mult, ALU.mult, [RB[bq0], Rr_, Rc], [Rcq[0][t]])
                    stt("dve", cq[:, 1, cols], banks[bq1], gcol(G_QA + 1), r, ALU.mult, ALU.mult, [RB[bq1], Rr_, Rc], [Rcq[1][t]])

                def ckv_a():
                    bkv = proj(256, 128)
                    q, Rq = sc.sq()
                    act(q, banks[bkv], AF.Square, [RB[bkv]], [Rq])
                    return (bkv, q, Rq)

                def ckv_b(st_, ckv=ckv, Rckv=Rckv):
                    bkv, q, Rq = st_
                    b2 = statc.next()
                    mm(banks[b2], ONES, q, True, True, [Rq, Rc], [RB[b2]])
                    r, Rr_ = rstd_from(sc, b2, (0, 128), 1.0 / 128)
                    stt("dve", ckv, banks[bkv], gcol(G_KVA), r, ALU.mult, ALU.mult, [RB[bkv], Rr_, Rc], [Rckv])

                def kpe_a():
                    bka, bkb = proj(384, 128), proj(512, 128)
                    q, Rq = sc.sq()
                    act(q[0:96, :], banks[bka][0:96, :], AF.Square, [RB[bka]], [Rq])
                    return (bka, bkb, q, Rq)

                def kpe_b(st_, t=t, cols=cols, tabC=tabC, tabS=tabS, RtC=RtC, RtS=RtS):
                    bka, bkb, q, Rq = st_
                    b2 = statc.next()
                    mm(banks[b2][0:96, :], BD96[0:96, 0:96], q[0:96, :], True, True, [Rq, Rc], [RB[b2]])
                    r, Rr_ = rstd_from(sc, b2, (0, 96), 1.0)
                    stt("dve", t1[64:96, :], banks[bka][64:96, :], gcol(G_KRS, 64, 96), tabC[64:96, :], ALU.mult, ALU.mult,
                        [RB[bka], RtC, Rc], [Rt1])
                    stt("dve", t2[64:96, :], banks[bkb][64:96, :], gcol(G_KRW, 64, 96), tabS[64:96, :], ALU.mult, ALU.mult,
                        [RB[bkb], RtS, Rc], [Rt2])
                    tt("dve", t1[64:96, :], t1[64:96, :], t2[64:96, :], ALU.add, [Rt1, Rt2], [Rt1])
                    tt("dve", kpt[64:96, :], t1[64:96, :], r[64:96, :], ALU.mult, [Rt1, Rr_], [Rkpt])

                def mk_kn(hp, t=t, cols=cols, ckv=ckv, Rckv=Rckv):
                    def a():
                        b = mainc.next()
                        mm(banks[b], Wukv[:, hp * 128:(hp + 1) * 128], ckv, True, True, [RWukv, Rckv], [RB[b]])
                        q, Rq = sc.sq()
                        act(q, banks[b], AF.Square, [RB[b]], [Rq])
                        return (b, q, Rq)

                    def bfn(st_):
                        b, q, Rq = st_
                        b2 = statc.next()
                        mm(banks[b2], BD64, q, True, True, [Rq, Rc], [RB[b2]])
                        r, Rr_ = rstd_from(sc, b2, (0, 128), 1.0)
                        stt("dve", Kt[0:64, 2 * hp, cols], banks[b][0:64, :], gcol(G_KN, 0, 64), r[0:64, :],
                            ALU.mult, ALU.mult, [RB[b], Rr_, Rc], [RKn[2 * hp][t]])
                        stt("dve", Kt[0:64, 2 * hp + 1, cols], banks[b][64:128, :], gcol(G_KN, 64, 128), r[64:128, :],
                            ALU.mult, ALU.mult, [RB[b], Rr_, Rc], [RKn[2 * hp + 1][t]])
                    return (a, bfn)

                jobs = [(ckv_a, ckv_b), (cq_a, cq_b), (kpe_a, kpe_b)] + [mk_kn(hp) for hp in range(4)]
                run_jobs(jobs[:3])
                for blk in range(4):
                    b = vc.next()
                    mm(banks[b], ckv[:, blk * 128:(blk + 1) * 128], Wukv[:, 512:1024], True, True, [Rckv, RWukv], [RB[b]])
                    bv = banks[b].rearrange("p (i two n) -> p i two n", two=2, n=64)
                    copy("act", Va4[:, 4 * t + blk, :, 0:64], bv[:, :, 0, :], [RB[b], RVa1], [RVa[t]])
                    copy("dve", Va4[:, 4 * t + blk, :, 128:192], bv[:, :, 1, :], [RB[b], RVa1], [RVa[t]])
                if t + 1 < NT:
                    m1_tabs(t + 1)
                    norm_tile(sc, s, t + 1, G_MIX, [Hm2[(t + 1) % 2][:, c, :] for c in range(8)], RHm2[(t + 1) % 2], statc)
                run_jobs(jobs[3:])
                for h in range(8):
                    copy("pool", Kt[64:96, h, cols], kpt[64:96, :], [Rkpt], [RKp[h][t]])

            S.new_phase()
            S.tag = "m2_s%d" % s
            A.release(m1_mark)
            WuqA = v3(A.alloc(2 * 1024, BF16), 1024)
            WuqB = v3(A.alloc(2 * 1024, BF16), 1024)
            RWa, RWb = S.res(tag + "wuqa"), S.res(tag + "wuqb")
            Wo = v3(A.alloc(8 * 1024, BF16), 1024)
            RWo = S.res(tag + "wo")
            Qh = [A.alloc(512, BF16) for _ in range(3)]
            RQh = [S.res(tag + "qh%d" % i) for i in range(3)]
            PTm = [A.alloc(512, BF16) for _ in range(4)]
            RPTm = [S.res(tag + "ptm%d" % i) for i in range(4)]
            attn_m2 = [v3(A.alloc(4 * TILE, BF16), TILE) for _ in range(2)]
            Rattn_m2 = [[S.res(tag + "am%d_%d" % (i, c)) for c in range(4)] for i in range(2)]
            tabC2 = [A.alloc(512, F32)] * 2
            tabS2 = [A.alloc(512, F32)] * 2
            RtC2 = [S.res(tag + "tabCb")] * 2
            RtS2 = [S.res(tag + "tabSb")] * 2
            t1 = A.alloc(512, F32)
            t2 = A.alloc(512, F32)
            Rt1, Rt2 = S.res(tag + "t1b"), S.res(tag + "t2b")
            Rr = A.alloc(512, F32)
            RRr = S.res(tag + "Rrm")
            sc = Scratch(tag + "a")
            dma("pool", WuqA, wuqa_d.rearrange("(c p) n -> p c n", p=128), [], [RWa], wslots["wuqa"])
            dma("pool", WuqB, wuqb_d.rearrange("(c p) n -> p c n", p=128), [], [RWb], wslots["wuqb"])
            dma("pool", Wo, wo_d.rearrange("(c p) n -> p c n", p=128), [], [RWo], wslots["wo"])
            WuqB4 = WuqB.rearrange("p k (h n) -> p k h n", n=128)
            for kc in range(2):
                S.add("dve", (lambda kc=kc: lambda e: e.tensor_scalar(
                    out=WuqB4[:, kc, :, 64:80], in0=WuqB4[:, kc, :, 64:80], scalar1=-1.0, scalar2=None,
                    op0=ALU.mult))(), reads=[RWb], writes=[RWb])
            scale = 1.0 / math.sqrt(96.0)
            sbank = [0, 1, 2]
            obank = [3, 4]
            bA, bB, bM = 5, 6, 7
            bW = bM
            LOOK = 2
            NH = NT * 8
            NS = NH * 16

            def m2_tabs(t):
                cols = slice(t * TILE, (t + 1) * TILE)
                dma("sp", tabC2[t % 2][64:96, :], cos_d[:, cols], [], [RtC2[t % 2]], wslots["tabc%d" % (t % 2)])
                dma("sp", tabS2[t % 2][64:96, :], sin_d[:, cols], [], [RtS2[t % 2]], wslots["tabs%d" % (t % 2)])

            qstate = {}

            def qprod_a(th):
                t, h = th // 8, th % 8
                cols = slice(t * TILE, (t + 1) * TILE)
                for kc in range(2):
                    mm(banks[bA], WuqA[:, kc, h * 128:(h + 1) * 128], cq[:, kc, cols], kc == 0, kc == 1,
                       [RWa, Rcq[kc][t]], [RB[bA]])
                for kc in range(2):
                    mm(banks[bB], WuqB[:, kc, h * 128:(h + 1) * 128], cq[:, kc, cols], kc == 0, kc == 1,
                       [RWb, Rcq[kc][t]], [RB[bB]])
                q, Rq = sc.sq()
                act(q[0:96, :], banks[bA][0:96, :], AF.Square, [RB[bA]], [Rq])
                qstate[th] = (q, Rq)

            def qprod_b(th):
                t, h = th // 8, th % 8
                q, Rq = qstate.pop(th)
                qi = th % 3
                tabC, tabS, RtC, RtS = tabC2[t % 2], tabS2[t % 2], RtC2[t % 2], RtS2[t % 2]
                mm(banks[bM][0:96, :], BD96[0:96, 0:96], q[0:96, :], True, True, [Rq, Rc], [RB[bM]])
                r, Rr_ = rstd_from(sc, bM, (0, 96), 1.0)
                stt("dve", Qh[qi][0:64, :], banks[bA][0:64, :], gcol(G_QN, 0, 64), r[0:64, :], ALU.mult, ALU.mult,
                    [RB[bA], Rr_, Rc], [RQh[qi]])
                stt("dve", t1[64:96, :], banks[bA][64:96, :], gcol(G_QRS, 64, 96), tabC[64:96, :], ALU.mult, ALU.mult,
                    [RB[bA], RtC, Rc], [Rt1])
                stt("dve", t2[64:96, :], banks[bB][64:96, :], gcol(G_QRW, 64, 96), tabS[64:96, :], ALU.mult, ALU.mult,
                    [RB[bB], RtS, Rc], [Rt2])
                tt("dve", t1[64:96, :], t1[64:96, :], t2[64:96, :], ALU.add, [Rt1, Rt2], [Rt1])
                tt("dve", Qh[qi][64:96, :], t1[64:96, :], r[64:96, :], ALU.mult, [Rt1, Rr_], [RQh[qi]])

            def wo_piece(t, m):
                cols = slice(t * TILE, (t + 1) * TILE)
                dc_, c = m // 8, m % 8
                if c < 4:
                    rhs, rr = attn_m2[t % 2][:, c, :], Rattn_m2[t % 2][c]
                else:
                    rhs, rr = attn_s[:, c - 4, cols], Rattn_s[c - 4][t]
                mm(banks[bW], Wo[:, c, dc_ * 128:(dc_ + 1) * 128], rhs, c == 0, c == 7, [RWo, rr], [RB[bW]])
                if c == 7:
                    tt("dve", X[:, dc_, cols], banks[bW], X[:, dc_, cols], ALU.add, [RB[bW], XR[dc_][t]], [XR[dc_][t]])

            def m2_S(i):
                th, kc = i // 16, i % 16
                t, h = th // 8, th % 8
                qi = th % 3
                bs = sbank[i % 3]
                mm(banks[bs], Kt[0:96, h, kc * 128:(kc + 1) * 128], Qh[qi][0:96, :], True, True,
                   [RKn[h][kc // 4], RKp[h][kc // 4], RQh[qi]], [RB[bs]])
                act(PTm[i % 4], banks[bs], AF.Exp, [RB[bs]], [RPTm[i % 4]], scale=scale)
                if th + 1 < NH:
                    if kc == 0:
                        qprod_a(th + 1)
                    if kc == 3:
                        qprod_b(th + 1)
                if h == 6 and kc == 4 and t + 1 < NT:
                    m2_tabs(t + 1)
                if t > 0 and 6 <= kc < 14:
                    wo_piece(t - 1, h * 8 + (kc - 6))

            def m2_PV(i):
                th, kc = i // 16, i % 16
                t, h = th // 8, th % 8
                ob = obank[th % 2]
                half, pair = h % 2, h // 2
                vlo = pair * 192 + (0 if half == 0 else 64)
                mm(banks[ob], Va[:, kc, vlo:vlo + 128], PTm[i % 4], kc == 0, kc == 15,
                   [RVa[kc // 4], RVa1, RPTm[i % 4]], [RB[ob]])
                if kc == 15:
                    nr = (half * 64, half * 64 + 64)
                    dr = (64 - half * 64, 128 - half * 64)
                    r_o, r_i = Rr[nr[0]:nr[1], :], banks[ob][dr[0]:dr[1], :]
                    S.add("dve", lambda e: e.reciprocal(out=r_o, in_=r_i), reads=[RB[ob]], writes=[RRr])
                    tt("dve", attn_m2[t % 2][nr[0]:nr[1], pair, :], banks[ob][nr[0]:nr[1], :], r_o, ALU.mult,
                       [RB[ob], RRr], [Rattn_m2[t % 2][pair]])

            m2_tabs(0)
            qprod_a(0)
            qprod_b(0)
            for i in range(NS + LOOK):
                if i < NS:
                    m2_S(i)
                if i - LOOK >= 0:
                    m2_PV(i - LOOK)
            for m in range(64):
                wo_piece(NT - 1, m)

        ffn_segment([(0, 0, 0), (0, 0, 1)])
        mix_phase(0)
        ffn_segment([(0, 1, 0), (0, 1, 1), (1, 0, 0), (1, 0, 1)])
        mix_phase(1)
        ffn_segment([(1, 1, 0), (1, 1, 1)])
        S.final_wait("sp", yslots)
        S.emit(nc)
    return nc


def _host_constants():
    ident = np.eye(128, dtype=np.float32)
    cbf = np.zeros((128, 512), np.float32)
    cbf[:, 0:128] = 1.0
    bd64 = np.zeros((128, 128), np.float32)
    bd64[0:64, 0:64] = 1.0 / 64
    bd64[64:128, 64:128] = 1.0 / 64
    bd96 = np.zeros((128, 128), np.float32)
    bd96[0:64, 0:64] = 1.0 / 64
    bd96[64:96, 64:96] = 1.0 / 32
    cbf[:, 128:256] = bd64
    cbf[:, 256:384] = bd96
    cbf[:, 384:512] = ident
    i = np.arange(128)[:, None]
    c = np.arange(384)[None, :]
    rel = 128 + i - c
    valid = np.abs(rel) <= 128
    al = np.zeros((128, 8, 384), np.float32)
    for h in range(8):
        al[:, h, :] = np.where(valid, -np.abs(rel) * (2.0 ** (2 - h)), -30000.0)
    pos = np.arange(SEQ, dtype=np.float64)
    inv = 1.0 / (10000.0 ** (np.arange(0, 32, 2, dtype=np.float64) / 32))
    ang = pos[None, :] * inv[:, None]
    cos2 = np.concatenate([np.cos(ang), np.cos(ang)], 0).astype(np.float32)
    sin2 = np.concatenate([np.sin(ang), np.sin(ang)], 0).astype(np.float32)
    return ident, cbf, al.reshape(128, 8 * 384), cos2, sin2


_NC_CACHE = {}


def kernel(x, g_ffn1, w1_gate, w1_up, w1_down, g_mix, w_in, g_q_a, w_uq, g_kv_a, w_ukv,
           g_mla_qn, g_mla_qr, g_mla_kn, g_mla_kr, g_swa_q, g_swa_k, sink, w_o,
           g_ffn2, w2_gate, w2_up, w2_down):
    f = lambda a: np.ascontiguousarray(np.asarray(a, dtype=np.float32))
    x = f(x)
    w_in = f(w_in); w_uq = f(w_uq); w_ukv = f(w_ukv)
    hq, hkv, hkr = w_in[:, 0:256], w_in[:, 256:384], w_in[:, 384:416]
    sq, sk, sv = w_in[:, 416:928], w_in[:, 928:1056], w_in[:, 1056:1184]
    w_swa = np.concatenate([sq, sk[:, 0:64], sk[:, 0:64], sk[:, 64:128], sk[:, 64:128], sv], axis=1)
    pad = hkv[:, 0:64]
    pad2 = hkv[:, 64:96]
    w_mla = np.concatenate([hq, hkv, pad, hkr, pad2, pad, hkr[:, 16:32], hkr[:, 0:16], pad2], axis=1)
    uq = w_uq.reshape(256, 8, 96)
    w_uq_a = np.concatenate([uq, uq[:, :, 0:32]], axis=2).reshape(256, 1024)
    w_uq_b = np.concatenate([uq[:, :, 0:64], uq[:, :, 80:96], uq[:, :, 64:80], uq[:, :, 0:32]], axis=2).reshape(256, 1024)
    ukv = w_ukv.reshape(128, 8, 128)
    w_ukv_r = np.concatenate([ukv[:, :, 0:64].reshape(128, 512), ukv[:, :, 64:128].reshape(128, 512)], axis=1)
    gains = np.ones((128, NG), np.float32)
    gains[:, G_FFN1:G_FFN1 + 8] = f(g_ffn1).reshape(8, 128).T
    gains[:, G_MIX:G_MIX + 8] = f(g_mix).reshape(8, 128).T
    gains[:, G_FFN2:G_FFN2 + 8] = f(g_ffn2).reshape(8, 128).T
    gains[:, G_QA:G_QA + 2] = f(g_q_a).reshape(2, 128).T
    gains[:, G_KVA] = f(g_kv_a)
    gains[0:64, G_QN] = f(g_mla_qn)
    gains[:, G_KN] = np.tile(f(g_mla_kn), 2)
    gains[:, G_SWQ] = np.tile(f(g_swa_q), 2)
    gains[:, G_SWK] = np.tile(f(g_swa_k), 2)
    gqr, gkr = f(g_mla_qr), f(g_mla_kr)
    gains[64:96, G_QRS] = gqr
    gains[64:96, G_QRW] = np.concatenate([gqr[16:32], gqr[0:16]])
    gains[64:96, G_KRS] = gkr
    gains[64:96, G_KRW] = np.concatenate([gkr[16:32], gkr[0:16]])
    gains[:, G_SINK:G_SINK + 8] = np.broadcast_to(f(sink)[None, :], (128, 8))
    ident, cbf, alibi, cos2, sin2 = _host_constants()

    if "nc" not in _NC_CACHE:
        _NC_CACHE["nc"] = build_program()
    nc = _NC_CACHE["nc"]
    shared = {
        "w1_gate": f(w1_gate), "w1_up": f(w1_up), "w1_down": f(w1_down),
        "w2_gate": f(w2_gate), "w2_up": f(w2_up), "w2_down": f(w2_down),
        "w_swa": np.ascontiguousarray(w_swa), "w_mla": np.ascontiguousarray(w_mla),
        "w_ukv_r": np.ascontiguousarray(w_ukv_r), "w_uq_a": np.ascontiguousarray(w_uq_a), "w_uq_b": np.ascontiguousarray(w_uq_b),
        "w_o": f(w_o), "gains": gains, "ident": ident, "cbf": cbf, "alibi": alibi, "cos2": cos2, "sin2": sin2,
    }
    in_maps = []
    for c in range(NCORES):
        m = dict(shared)
        m["x"] = np.ascontiguousarray(x[NSEQ * c:NSEQ * (c + 1)].reshape(NSEQ * SEQ, D_MODEL))
        in_maps.append(m)
    res = run_bass_kernel_spmd(nc, in_maps, core_ids=list(range(NCORES)))
    out = np.stack([np.asarray(r["out"]).reshape(NSEQ, SEQ, D_MODEL) for r in res.results], axis=0)
    return out.reshape(NCORES * NSEQ, SEQ, D_MODEL).astype(np.float32)
```

```python
import math
from contextlib import ExitStack

import numpy as np
import concourse.bass as bass
import concourse.mybir as mybir
from concourse.bass_utils import run_bass_kernel_spmd

F32 = mybir.dt.float32
BF16 = mybir.dt.bfloat16
ALU = mybir.AluOpType
AF = mybir.ActivationFunctionType

ENGS = ("pe", "act", "dve", "pool", "sp")

D_MODEL = 1024
D_FF = 2816
SEQ = 2048
NSEQ = 2
NCORES = 8
TILE = 512
NT = SEQ // TILE
NJ = D_FF // 128
RMS_EPS = 1e-6


class Res:
    __slots__ = ("name", "w", "r")

    def __init__(self, name, inherit=()):
        self.name = name
        self.w = None
        self.r = list(inherit)


class DmaSlot:
    def __init__(self, name):
        self.name = name
        self.count = 0
        self.sem = None


class Op:
    __slots__ = ("eng", "fn", "reads", "writes", "deps", "signal", "sigval",
                 "dma", "slot", "waits", "idx", "tag")

    def __init__(self, eng, fn, reads, writes, dma, slot):
        self.eng = eng
        self.fn = fn
        self.reads = reads
        self.writes = writes
        self.deps = []
        self.signal = False
        self.sigval = None
        self.dma = dma
        self.slot = slot
        self.waits = []


class Sched:
    def __init__(self):
        self.ops = {e: [] for e in ENGS}
        self.all = []
        self.slots = []
        self.phase_res = []
        self.frontier = []
        self._final = None
        self.tag = None
        self.scopes = False

    def slot(self, name):
        s = DmaSlot(name)
        self.slots.append(s)
        return s

    def res(self, name, persist=False):
        if persist:
            return Res(name)
        r = Res(name, self.frontier)
        self.phase_res.append(r)
        return r

    def new_phase(self):
        last = {}
        for r in self.phase_res:
            for op in ([r.w] if r.w is not None else []) + r.r:
                key = ("slot", id(op.slot)) if op.dma else ("eng", op.eng)
                if key not in last or last[key].idx < op.idx:
                    last[key] = op
        for op in self.frontier:
            key = ("slot", id(op.slot)) if op.dma else ("eng", op.eng)
            if key not in last or last[key].idx < op.idx:
                last[key] = op
        self.frontier = list(last.values())

    def _dep(self, op, prod):
        if prod is None or prod is op:
            return
        if (not prod.dma) and (not op.dma) and prod.eng == op.eng:
            return
        op.deps.append(prod)

    def add(self, eng, fn, reads=(), writes=(), dma=False, slot=None):
        op = Op(eng, fn, tuple(reads), tuple(writes), dma, slot)
        op.idx = len(self.all)
        op.tag = self.tag
        for r in op.reads:
            self._dep(op, r.w)
        for w in op.writes:
            self._dep(op, w.w)
            for rd in w.r:
                self._dep(op, rd)
        for r in op.reads:
            r.r.append(op)
        for w in op.writes:
            w.w = op
            w.r = []
        self.ops[eng].append(op)
        self.all.append(op)
        return op

    def final_wait(self, eng, slots):
        self._final = (eng, list(slots))

    def finalize(self):
        for op in self.all:
            for d in op.deps:
                d.signal = True
        cnt = {e: 0 for e in ENGS}
        for op in self.all:
            if op.dma:
                op.slot.count += 16
                op.sigval = op.slot.count
            elif op.signal:
                cnt[op.eng] += 1
                op.sigval = cnt[op.eng]
        seen = {e: {} for e in ENGS}
        for op in self.all:
            need = {}
            for d in op.deps:
                key = ("slot", d.slot) if d.dma else ("eng", d.eng)
                if need.get(key, 0) < d.sigval:
                    need[key] = d.sigval
            for key, v in need.items():
                if seen[op.eng].get(key, 0) >= v:
                    continue
                seen[op.eng][key] = v
                op.waits.append((key, v))

    def emit(self, nc):
        self.finalize()
        with ExitStack() as st:
            esem = {e: st.enter_context(nc.semaphore("s_" + e)) for e in ENGS if e != "sp"}
            for s in self.slots:
                s.sem = st.enter_context(nc.semaphore("d_" + s.name))
            block = st.enter_context(nc.Block())
            fin = self._final

            def run(engname, eng):
                cur = [None, None]

                def set_scope(tag):
                    if not self.scopes or tag == cur[0]:
                        return
                    if cur[1] is not None:
                        cur[1].close()
                        cur[1] = None
                    cur[0] = tag
                    if tag is not None:
                        es = ExitStack()
                        es.enter_context(nc.named_scope(tag))
                        cur[1] = es

                for op in self.ops[engname]:
                    set_scope(op.tag)
                    for key, v in op.waits:
                        sem = key[1].sem if key[0] == "slot" else esem[key[1]]
                        eng.wait_ge(sem, v)
                    ins = op.fn(eng)
                    if op.dma:
                        ins.then_inc(op.slot.sem, 16)
                    elif op.signal:
                        ins.then_inc(esem[op.eng], 1)
                set_scope(None)
                if fin is not None and fin[0] == engname:
                    for s in fin[1]:
                        if s.count:
                            eng.wait_ge(s.sem, s.count)

            @block.tensor
            def _(e):
                run("pe", e)

            @block.scalar
            def _(e):
                run("act", e)

            @block.vector
            def _(e):
                run("dve", e)

            @block.gpsimd
            def _(e):
                run("pool", e)

            @block.sync
            def _(e):
                run("sp", e)


class Cyc:
    def __init__(self, ids):
        self.ids = list(ids)
        self.k = 0

    def next(self):
        v = self.ids[self.k % len(self.ids)]
        self.k += 1
        return v


class Arena:
    def __init__(self, ap2d, nwords):
        self.ap = ap2d
        self.n = nwords
        self.off = 0

    def alloc(self, nelem, dt):
        n4 = nelem if dt == F32 else (nelem + 1) // 2
        assert self.off + n4 <= self.n, ("SBUF arena overflow", self.off, n4, self.n)
        a = self.ap[:, self.off:self.off + n4]
        self.off += n4
        if dt != F32:
            a = a.bitcast(dt)
        return a

    def mark(self):
        return self.off

    def release(self, m):
        self.off = m


G_FFN1, G_MIX, G_FFN2, G_QA, G_KVA = 0, 8, 16, 24, 26
G_QN, G_KN, G_SWQ, G_SWK = 27, 28, 29, 30
G_QRS, G_QRW, G_KRS, G_KRW = 31, 32, 33, 34
G_SINK = 35
NG = 48


def build_program(scopes=False):
    nc = bass.Bass("TRN2", target_bir_lowering=False)
    NTOK = NSEQ * SEQ

    def din(name, shape):
        return nc.dram_tensor(name, list(shape), F32, kind="ExternalInput").ap()

    x_d = din("x", [NTOK, D_MODEL])
    out_d = nc.dram_tensor("out", [NTOK, D_MODEL], F32, kind="ExternalOutput").ap()
    wg_d = [din("w1_gate", [D_MODEL, D_FF]), din("w2_gate", [D_MODEL, D_FF])]
    wu_d = [din("w1_up", [D_MODEL, D_FF]), din("w2_up", [D_MODEL, D_FF])]
    wd_d = [din("w1_down", [D_FF, D_MODEL]), din("w2_down", [D_FF, D_MODEL])]
    wsw_d = din("w_swa", [D_MODEL, 896])
    wm_d = din("w_mla", [D_MODEL, 640])
    wukv_d = din("w_ukv_r", [128, 1024])
    wuqa_d = din("w_uq_a", [256, 1024])
    wuqb_d = din("w_uq_b", [256, 1024])
    wo_d = din("w_o", [D_MODEL, D_MODEL])
    gains_d = din("gains", [128, NG])
    ident_d = din("ident", [128, 128])
    cbf_d = din("cbf", [128, 512])
    alibi_d = din("alibi", [128, 8 * 384])
    cos_d = din("cos2", [32, SEQ])
    sin_d = din("sin2", [32, SEQ])

    S = Sched()
    S.scopes = scopes
    with ExitStack() as st:
        NW = 53200
        arena_t = st.enter_context(nc.sbuf_tensor("arena", [128, NW], F32))
        A = Arena(arena_t[:, :], NW)
        banks = [st.enter_context(nc.psum_tensor("bank%d" % i, [128, 512], F32))[:] for i in range(8)]
        RB = [S.res("bank%d" % i, persist=True) for i in range(8)]

        def v3(ap, b):
            return ap.rearrange("p (a b) -> p a b", b=b)

        X = v3(A.alloc(8 * SEQ, F32), SEQ)
        XR = [[S.res("X%d_%d" % (c, t), persist=True) for t in range(NT)] for c in range(8)]
        gains = A.alloc(NG, F32)
        ident = A.alloc(128, F32)
        epsc = A.alloc(2, F32)
        esink = A.alloc(8, F32)
        cbf = A.alloc(512, BF16)
        ONES, BD64, BD96, IDB = cbf[:, 0:128], cbf[:, 128:256], cbf[:, 256:384], cbf[:, 384:512]
        Rc = S.res("consts", persist=True)
        cslot = S.slot("consts")
        S.add("sp", lambda e: e.dma_start(out=gains, in_=gains_d), writes=[Rc], dma=True, slot=cslot)
        S.add("sp", lambda e: e.dma_start(out=ident, in_=ident_d), writes=[Rc], dma=True, slot=cslot)
        cslot2 = S.slot("consts_sw")
        S.add("pool", lambda e: e.dma_start(out=cbf, in_=cbf_d), writes=[Rc], dma=True, slot=cslot2)
        S.add("dve", lambda e: e.memset(epsc[:, 0:1], RMS_EPS), writes=[Rc])
        S.add("act", lambda e: e.activation(out=esink, in_=gains[:, G_SINK:G_SINK + 8], func=AF.Exp),
              reads=[Rc], writes=[Rc])
        EPS = epsc[:, 0:1]
        base_mark = A.mark()

        def gcol(i, lo=0, hi=128):
            return gains[lo:hi, i:i + 1]

        def mm(out, lhsT, rhs, start, stop, reads, writes):
            S.add("pe", lambda e: e.matmul(out, lhsT=lhsT, rhs=rhs, start=start, stop=stop),
                  reads=reads, writes=writes)

        def act(out, in_, func, reads, writes, scale=None, bias=None):
            kw = {}
            if scale is not None:
                kw["scale"] = scale
            if bias is not None:
                kw["bias"] = bias
            S.add("act", lambda e: e.activation(out=out, in_=in_, func=func, **kw), reads=reads, writes=writes)

        def stt(eng, out, in0, scalar, in1, op0, op1, reads, writes):
            S.add(eng, lambda e: e.scalar_tensor_tensor(out=out, in0=in0, scalar=scalar, in1=in1, op0=op0, op1=op1),
                  reads=reads, writes=writes)

        def tt(eng, out, in0, in1, op, reads, writes):
            S.add(eng, lambda e: e.tensor_tensor(out=out, in0=in0, in1=in1, op=op), reads=reads, writes=writes)

        def copy(eng, out, in_, reads, writes):
            if eng == "act":
                S.add("act", lambda e: e.activation(out=out, in_=in_, func=AF.Copy), reads=reads, writes=writes)
            else:
                S.add(eng, lambda e: e.tensor_copy(out=out, in_=in_), reads=reads, writes=writes)

        def dma(q, out, in_, reads, writes, slot):
            S.add(q, lambda e: e.dma_start(out=out, in_=in_), reads=reads, writes=writes, dma=True, slot=slot)

        class Scratch:
            def __init__(self, tag):
                self.sqb = [A.alloc(512, BF16) for _ in range(4)]
                self.Rsqb = [S.res(tag + "sqb%d" % i) for i in range(4)]
                self.lnt = A.alloc(512, F32)
                self.Rlnt = S.res(tag + "lnt")
                self.rstd = [A.alloc(512, F32) for _ in range(2)]
                self.Rrstd = [S.res(tag + "rstd%d" % i) for i in range(2)]
                self.k = 0
                self.kr = 0

            def sq(self):
                i = self.k % 4
                self.k += 1
                return self.sqb[i], self.Rsqb[i]

            def rs(self):
                i = self.kr % 2
                self.kr += 1
                return self.rstd[i], self.Rrstd[i]

        def rstd_from(sc, bank_i, rows, scale):
            lo, hi = rows
            r, Rr = sc.rs()
            act(sc.lnt[lo:hi, :], banks[bank_i][lo:hi, :], AF.Ln, [RB[bank_i], Rc], [sc.Rlnt],
                scale=scale, bias=epsc[lo:hi, 0:1])
            act(r[lo:hi, :], sc.lnt[lo:hi, :], AF.Exp, [sc.Rlnt], [Rr], scale=-0.5)
            return r, Rr

        def norm_tile(sc, s_unused, t, gbase, Hc, HR, statc, act_sq=False):
            cols = slice(t * TILE, (t + 1) * TILE)
            sb = statc.next()
            for c in range(8):
                q, Rq = sc.sq()
                if c % 2 == 0 or act_sq:
                    act(q, X[:, c, cols], AF.Square, [XR[c][t]], [Rq])
                else:
                    tt("dve", q, X[:, c, cols], X[:, c, cols], ALU.mult, [XR[c][t]], [Rq])
                mm(banks[sb], ONES, q, c == 0, c == 7, [Rq, Rc], [RB[sb]])
            r, Rr = rstd_from(sc, sb, (0, 128), 1.0 / D_MODEL)
            for c in range(8):
                stt("dve", Hc[c], X[:, c, cols], gcol(gbase + c), r, ALU.mult, ALU.mult,
                    [XR[c][t], Rr, Rc], [HR[c]])

        xslots = [S.slot("xin0"), S.slot("xin1")]
        yslots = [S.slot("yout0"), S.slot("yout1")]
        wgslots = [S.slot("wg0"), S.slot("wg1")]
        wuslots = [S.slot("wu0"), S.slot("wu1")]
        wdslots = [S.slot("wd0"), S.slot("wd1")]

        def ffn_segment(sts):
            S.new_phase()
            A.release(base_mark)
            tag = "fs%d_%d_%d_" % sts[0]
            actb = v3(A.alloc(NJ * 1024, BF16), 1024)
            actR = [[S.res(tag + "act%d_%d" % (j, u)) for u in range(2)] for j in range(NJ)]
            H2 = [v3(A.alloc(8 * 1024, BF16), 1024) for _ in range(2)]
            HR2 = [[[S.res(tag + "H%d_%d_%d" % (i, c, u)) for c in range(8)] for u in range(2)] for i in range(2)]
            wgs = [v3(A.alloc(8 * 256, BF16), 256) for _ in range(2)]
            wus = [v3(A.alloc(8 * 256, BF16), 256) for _ in range(2)]
            wds = [v3(A.alloc(NJ * 256, BF16), 256) for _ in range(2)]
            Rwg = [S.res(tag + "wg%d" % i) for i in range(2)]
            Rwu = [S.res(tag + "wu%d" % i) for i in range(2)]
            Rwd = [S.res(tag + "wd%d" % i) for i in range(2)]
            io = [A.alloc(1024, F32) for _ in range(2)]
            Rio = [S.res(tag + "io%d" % i) for i in range(2)]
            sg = [A.alloc(512, BF16) for _ in range(2)]
            Rsg = [S.res(tag + "sg%d" % i) for i in range(2)]
            sc = Scratch(tag)
            gc, uc, dc, mc = Cyc([0, 1]), Cyc([2, 3]), Cyc([4, 5]), Cyc([6, 7])
            cnt = {"sg": 0, "ev": 0, "io": 0, "cur": 0, "cur_o": 0}

            def prep_pieces(k):
                s, which, stile = sts[k]
                ptag = "ffn%d_s%d" % (which + 1, s)
                gbase = G_FFN1 if which == 0 else G_FFN2
                H, HR = H2[k % 2], HR2[k % 2]
                pcs = []
                for u in range(2):
                    t = stile * 2 + u
                    cols = slice(t * TILE, (t + 1) * TILE)
                    if which == 0:
                        for bb in range(4):
                            for half in range(2):
                                def pT(bb=bb, half=half, t=t):
                                    tok = s * SEQ + t * TILE + bb * 128
                                    tc0 = t * TILE + bb * 128
                                    if half == 0:
                                        ii = cnt["io"] % 2
                                        cnt["io"] += 1
                                        cnt["cur"] = ii
                                        dma("sp", io[ii], x_d[tok:tok + 128, :], [], [Rio[ii]], xslots[ii])
                                    ii = cnt["cur"]
                                    b = mc.next()
                                    for c4 in range(4):
                                        c = half * 4 + c4
                                        S.add("pe", (lambda b=b, c4=c4, c=c, ii=ii: lambda e: e.transpose(
                                            banks[b][:, c4 * 128:(c4 + 1) * 128], io[ii][:, c * 128:(c + 1) * 128], ident))(),
                                            reads=[Rio[ii], Rc], writes=[RB[b]])
                                    copy("act" if cnt["ev"] % 2 == 0 else "dve", X[:, half * 4:half * 4 + 4, tc0:tc0 + 128],
                                         banks[b].rearrange("p (k n) -> p k n", n=128), [RB[b]],
                                         [XR[half * 4 + c4][t] for c4 in range(4)])
                                    cnt["ev"] += 1
                                pcs.append(pT)
                    st_ = {}

                    def p_sq(part, t=t, cols=cols, st_=st_):
                        if part == 0:
                            st_["sb"] = mc.next()
                        st_["q%d" % part] = []
                        for c in range(part * 4, part * 4 + 4):
                            q, Rq = sc.sq()
                            if c % 2 == 0:
                                act(q, X[:, c, cols], AF.Square, [XR[c][t]], [Rq])
                            else:
                                tt("dve", q, X[:, c, cols], X[:, c, cols], ALU.mult, [XR[c][t]], [Rq])
                            st_["q%d" % part].append((q, Rq))

                    def p_stat(part, st_=st_):
                        sb = st_["sb"]
                        for i, (q, Rq) in enumerate(st_["q%d" % part]):
                            c = part * 4 + i
                            mm(banks[sb], ONES, q, c == 0, c == 7, [Rq, Rc], [RB[sb]])

                    def p_fin(t=t, cols=cols, u=u, st_=st_):
                        r, Rr_ = rstd_from(sc, st_["sb"], (0, 128), 1.0 / D_MODEL)
                        for c in range(8):
                            stt("dve", H[:, c, u * TILE:(u + 1) * TILE], X[:, c, cols], gcol(gbase + c), r, ALU.mult, ALU.mult,
                                [XR[c][t], Rr_, Rc], [HR[u][c]])
                    pcs += [lambda p_sq=p_sq: p_sq(0), lambda p_stat=p_stat: p_stat(0),
                            lambda p_sq=p_sq: p_sq(1), lambda p_stat=p_stat: p_stat(1), p_fin]

                def wrap(f):
                    def g():
                        old = S.tag
                        S.tag = ptag
                        f()
                        S.tag = old
                    return g
                return [wrap(f) for f in pcs]

            def output_pieces(k):
                s, which, stile = sts[k]
                ptag = "ffn%d_s%d" % (which + 1, s)
                pcs = []
                for u in range(2):
                    t = stile * 2 + u
                    for bb in range(4):
                        for half in range(2):
                            def pO(t=t, bb=bb, half=half):
                                tok = s * SEQ + t * TILE + bb * 128
                                tc0 = t * TILE + bb * 128
                                if half == 0:
                                    cnt["cur_o"] = cnt["io"] % 2
                                    cnt["io"] += 1
                                ii = cnt["cur_o"]
                                b = mc.next()
                                for c4 in range(4):
                                    c = half * 4 + c4
                                    S.add("pe", (lambda b=b, c4=c4, c=c, tc0=tc0: lambda e: e.transpose(
                                        banks[b][:, c4 * 128:(c4 + 1) * 128], X[:, c, tc0:tc0 + 128], ident))(),
                                        reads=[XR[c][t], Rc], writes=[RB[b]])
                                copy("act" if cnt["ev"] % 2 == 0 else "dve", io[ii][:, half * 512:(half + 1) * 512],
                                     banks[b], [RB[b]], [Rio[ii]])
                                cnt["ev"] += 1
                                if half == 1:
                                    dma("sp", out_d[tok:tok + 128, :], io[ii], [Rio[ii]], [], yslots[ii])
                            pcs.append(pO)

                def wrap(f):
                    def g():
                        old = S.tag
                        S.tag = ptag
                        f()
                        S.tag = old
                    return g
                return [wrap(f) for f in pcs]

            def phase_a(k, pieces=()):
                pieces = list(pieces)
                s, which, stile = sts[k]
                S.tag = "ffn%d_s%d" % (which + 1, s)
                H, HR = H2[k % 2], HR2[k % 2]
                wgv = wg_d[which].rearrange("(c p) n -> p c n", p=128)
                wuv = wu_d[which].rearrange("(c p) n -> p c n", p=128)
                for jp in range(NJ // 2):
                    sl = jp % 2
                    dma("pool", wgs[sl], wgv[:, :, jp * 256:(jp + 1) * 256], [], [Rwg[sl]], wgslots[sl])
                    dma("pool", wus[sl], wuv[:, :, jp * 256:(jp + 1) * 256], [], [Rwu[sl]], wuslots[sl])
                    for jj in range(2):
                        j = jp * 2 + jj
                        for u in range(2):
                            bg = gc.next()
                            bu = uc.next()
                            for c in range(8):
                                mm(banks[bg], wgs[sl][:, c, jj * 128:(jj + 1) * 128], H[:, c, u * TILE:(u + 1) * TILE],
                                   c == 0, c == 7, [Rwg[sl], HR[u][c]], [RB[bg]])
                            for c in range(8):
                                mm(banks[bu], wus[sl][:, c, jj * 128:(jj + 1) * 128], H[:, c, u * TILE:(u + 1) * TILE],
                                   c == 0, c == 7, [Rwu[sl], HR[u][c]], [RB[bu]])
                            i = cnt["sg"] % 2
                            cnt["sg"] += 1
                            act(sg[i], banks[bg], AF.Silu, [RB[bg]], [Rsg[i]])
                            tt("dve", actb[:, j, u * TILE:(u + 1) * TILE], sg[i], banks[bu], ALU.mult,
                               [Rsg[i], RB[bu]], [actR[j][u]])
                            if pieces:
                                pieces.pop(0)()
                while pieces:
                    pieces.pop(0)()

            def phase_b(k, pieces=()):
                pieces = list(pieces)
                s, which, stile = sts[k]
                wdv = wd_d[which].rearrange("(j p) n -> p j n", p=128)
                for cp in range(4):
                    S.tag = "ffn%d_s%d" % (which + 1, s)
                    sl = cp % 2
                    dma("pool", wds[sl], wdv[:, :, cp * 256:(cp + 1) * 256], [], [Rwd[sl]], wdslots[sl])
                    for cc in range(2):
                        c = cp * 2 + cc
                        for u in range(2):
                            t = stile * 2 + u
                            cols = slice(t * TILE, (t + 1) * TILE)
                            b = dc.next()
                            for j in range(NJ):
                                mm(banks[b], wds[sl][:, j, cc * 128:(cc + 1) * 128], actb[:, j, u * TILE:(u + 1) * TILE],
                                   j == 0, j == NJ - 1, [Rwd[sl], actR[j][u]], [RB[b]])
                            stt("dve", X[:, c, cols], banks[b], 0.5, X[:, c, cols], ALU.mult, ALU.add,
                                [RB[b], XR[c][t]], [XR[c][t]])
                            for _ in range(2):
                                if pieces:
                                    pieces.pop(0)()
                while pieces:
                    pieces.pop(0)()

            for p in prep_pieces(0):
                p()
            pend_out = []
            for k in range(len(sts)):
                phase_a(k, pend_out)
                nxt = prep_pieces(k + 1) if k + 1 < len(sts) else []
                phase_b(k, nxt)
                pend_out = output_pieces(k) if sts[k][1] == 1 else []
            for p in pend_out:
                p()

        wslots = {n: S.slot(n) for n in ("wsw", "alibi", "wm", "wukv", "wuqa", "wuqb", "wo", "tabc0", "tabs0", "tabc1", "tabs1", "kpe")}

        def mix_phase(s):
            S.new_phase()
            S.tag = "swa_s%d" % s
            A.release(base_mark)
            tag = "m%d_" % s
            attn_s = v3(A.alloc(4 * SEQ, BF16), SEQ)
            Rattn_s = [[S.res(tag + "as%d_%d" % (c, t)) for t in range(NT)] for c in range(4)]
            mix_mark = A.mark()

            Wsw = v3(A.alloc(8 * 896, BF16), 896)
            RWsw = S.res(tag + "wsw")
            alibi = v3(A.alloc(8 * 384, F32), 384)
            Ralibi = S.res(tag + "alibi")
            exs = [A.alloc(384, F32) for _ in range(3)]
            Rexs = [S.res(tag + "exs%d" % i) for i in range(3)]
            lnd = A.alloc(512, F32)
            Rlnd = S.res(tag + "lnd")
            Qs = v3(A.alloc(4 * SEQ, BF16), SEQ)
            RQs = [[S.res(tag + "qs%d_%d" % (c, t)) for t in range(NT)] for c in range(4)]
            Ks = v3(A.alloc(2 * SEQ, BF16), SEQ)
            RKs = [[S.res(tag + "ks%d_%d" % (g, t)) for t in range(NT)] for g in range(2)]
            Vs = v3(A.alloc(16 * 320, BF16), 320)
            RVs = [S.res(tag + "vs%d" % t) for t in range(NT)]
            RVs1 = S.res(tag + "vs_ones")
            Hs2 = [v3(A.alloc(8 * TILE, BF16), TILE) for _ in range(2)]
            RHs2 = [[S.res(tag + "hs%d_%d" % (i, c)) for c in range(8)] for i in range(2)]
            NPT = 6
            PT = [A.alloc(384, BF16) for _ in range(NPT)]
            RPT = [S.res(tag + "pts%d" % i) for i in range(NPT)]
            dent = A.alloc(512, F32)
            Rdent = S.res(tag + "dent")
            Rr = A.alloc(512, F32)
            RRr = S.res(tag + "Rr")
            sc = Scratch(tag + "s")
            dma("pool", Wsw, wsw_d.rearrange("(c p) n -> p c n", p=128), [], [RWsw], wslots["wsw"])
            dma("sp", alibi, alibi_d.rearrange("p (h n) -> p h n", n=384), [], [Ralibi], wslots["alibi"])
            act(alibi, alibi, AF.Exp, [Ralibi], [Ralibi], scale=0.125)
            for lo in (0, 128, 256):
                S.add("pool", (lambda lo=lo: lambda e: e.memset(Vs[:, :, lo:lo + 64], 1.0))(), writes=[RVs1])
            mainc, statc, vc = Cyc([0, 1, 2, 3]), Cyc([4, 5]), Cyc([6, 7])

            def run_jobs(jobs):
                prev = None
                for jb in jobs:
                    st_ = jb[0]()
                    if prev is not None:
                        prev[0](prev[1])
                    prev = (jb[1], st_)
                if prev is not None:
                    prev[0](prev[1])

            norm_tile(sc, s, 0, G_MIX, [Hs2[0][:, c, :] for c in range(8)], RHs2[0], statc, act_sq=True)
            for t in range(NT):
                cols = slice(t * TILE, (t + 1) * TILE)
                Hs, RHs = Hs2[t % 2], RHs2[t % 2]

                def mk_job(wcol, gidx, out_ap, out_res, Hs=Hs, RHs=RHs):
                    def stage_a():
                        b = mainc.next()
                        for c in range(8):
                            mm(banks[b], Wsw[:, c, wcol:wcol + 128], Hs[:, c, :], c == 0, c == 7, [RWsw, RHs[c]], [RB[b]])
                        q, Rq = sc.sq()
                        act(q, banks[b], AF.Square, [RB[b]], [Rq])
                        return (b, q, Rq)

                    def stage_b(st_):
                        b, q, Rq = st_
                        b2 = statc.next()
                        mm(banks[b2], BD64, q, True, True, [Rq, Rc], [RB[b2]])
                        r, Rr_ = rstd_from(sc, b2, (0, 128), 1.0)
                        stt("dve", out_ap, banks[b], gcol(gidx), r, ALU.mult, ALU.mult, [RB[b], Rr_, Rc], [out_res])
                    return (stage_a, stage_b)

                jobs = [mk_job(cq_ * 128, G_SWQ, Qs[:, cq_, cols], RQs[cq_][t]) for cq_ in range(4)]
                jobs += [mk_job(512 + g * 128, G_SWK, Ks[:, g, cols], RKs[g][t]) for g in range(2)]
                run_jobs(jobs[:3])
                if t + 1 < NT:
                    norm_tile(sc, s, t + 1, G_MIX, [Hs2[(t + 1) % 2][:, c, :] for c in range(8)], RHs2[(t + 1) % 2], statc, act_sq=True)
                run_jobs(jobs[3:])
                b = vc.next()
                for blk in range(4):
                    for c in range(8):
                        mm(banks[b][:, blk * 128:(blk + 1) * 128], Hs[:, c, blk * 128:(blk + 1) * 128],
                           Wsw[:, c, 768:896], c == 0, c == 7, [RWsw, RHs[c]], [RB[b]])
                bv = banks[b].rearrange("p (k n) -> p k n", n=128)
                copy("act", Vs[:, 4 * t:4 * t + 4, 64:128], bv[:, :, 0:64], [RB[b], RVs1], [RVs[t]])
                copy("dve", Vs[:, 4 * t:4 * t + 4, 192:256], bv[:, :, 64:128], [RB[b], RVs1], [RVs[t]])

            S.tag = "swaattn_s%d" % s
            sbank = [0, 1, 2]
            obank = [3, 4]
            LOOK = 2
            NS = 8 * 16

            def hinfo(h):
                g, half, qc = h // 4, h % 2, h // 2
                r0 = half * 64
                if half == 0:
                    vlo = 64 if g == 0 else 192
                else:
                    vlo = 0 if g == 0 else 128
                return g, half, qc, r0, vlo

            def swa_S(i):
                h, j = i // 16, i % 16
                g, half, qc, r0, vlo = hinfo(h)
                r1 = r0 + 64
                qlo, qhi = max(j - 1, 0), min(j + 1, 15)
                ncol = (qhi - qlo + 1) * 128
                off = (qlo - (j - 1)) * 128
                b = sbank[i % 3]
                qt_res = [RQs[qc][tt_] for tt_ in range((qlo * 128) // TILE, (qhi * 128) // TILE + 1)]
                mm(banks[b][:, 0:ncol], Ks[r0:r1, g, j * 128:(j + 1) * 128], Qs[r0:r1, qc, qlo * 128:(qhi + 1) * 128],
                   True, True, [RKs[g][j // 4]] + qt_res, [RB[b]])
                ex, Rex = exs[i % 3], Rexs[i % 3]
                act(ex[:, 0:ncol], banks[b][:, 0:ncol], AF.Exp, [RB[b]], [Rex], scale=0.125)
                tt("pool" if i % 3 == 0 else "dve", PT[i % NPT][:, 0:ncol], ex[:, 0:ncol], alibi[:, h, off:off + ncol],
                   ALU.mult, [Rex, Ralibi], [RPT[i % NPT]])

            def swa_pv(h, n):
                g, half, qc, r0, vlo = hinfo(h)
                ob = obank[(h * 4 + n // 4) % 2]
                jjs = [jj for jj in (n - 1, n, n + 1) if 0 <= jj < 16]
                for k, jj in enumerate(jjs):
                    qlo = max(jj - 1, 0)
                    off = (n - qlo) * 128
                    pi = (h * 16 + jj) % NPT
                    mm(banks[ob][:, (n % 4) * 128:(n % 4 + 1) * 128], Vs[:, jj, vlo:vlo + 128],
                       PT[pi][:, off:off + 128], k == 0, k == len(jjs) - 1,
                       [RVs[jj // 4], RVs1, RPT[pi]], [RB[ob]])
                if n % 4 == 3:
                    tq = n // 4
                    nr = (r0, r0 + 64)
                    dr = (64 - r0, 128 - r0)
                    act(lnd[dr[0]:dr[1], :], banks[ob][dr[0]:dr[1], :], AF.Ln, [RB[ob], Rc], [Rlnd],
                        scale=1.0, bias=esink[dr[0]:dr[1], h:h + 1])
                    act(Rr[dr[0]:dr[1], :], lnd[dr[0]:dr[1], :], AF.Exp, [Rlnd], [RRr], scale=-1.0)
                    tt("dve", attn_s[nr[0]:nr[1], qc, tq * TILE:(tq + 1) * TILE], banks[ob][nr[0]:nr[1], :],
                       Rr[dr[0]:dr[1], :], ALU.mult, [RB[ob], RRr], [Rattn_s[qc][tq]])

            def swa_PV(i):
                h, j = i // 16, i % 16
                if j >= 1:
                    swa_pv(h, j - 1)
                if j == 15:
                    swa_pv(h, 15)

            for i in range(NS + LOOK):
                if i < NS:
                    swa_S(i)
                if i - LOOK >= 0:
                    swa_PV(i - LOOK)

            S.new_phase()
            S.tag = "m1_s%d" % s
            A.release(mix_mark)
            Kt = v3(A.alloc(8 * SEQ, BF16), SEQ)
            RKn = [[S.res(tag + "kn%d_%d" % (h, t)) for t in range(NT)] for h in range(8)]
            RKp = [[S.res(tag + "kp%d_%d" % (h, t)) for t in range(NT)] for h in range(8)]
            Va = v3(A.alloc(16 * 768, BF16), 768)
            RVa = [S.res(tag + "va%d" % t) for t in range(NT)]
            RVa1 = S.res(tag + "va_ones")
            cq = v3(A.alloc(2 * SEQ, BF16), SEQ)
            Rcq = [[S.res(tag + "cq%d_%d" % (k, t)) for t in range(NT)] for k in range(2)]
            WuqA = v3(A.alloc(2 * 1024, BF16), 1024)
            WuqB = v3(A.alloc(2 * 1024, BF16), 1024)
            RWa, RWb = S.res(tag + "wuqa"), S.res(tag + "wuqb")
            m1_mark = A.mark()
            Wm = v3(A.alloc(8 * 640, BF16), 640)
            RWm = S.res(tag + "wm")
            Wukv = A.alloc(1024, BF16)
            RWukv = S.res(tag + "wukv")
            Hm2 = [v3(A.alloc(8 * TILE, BF16), TILE) for _ in range(2)]
            RHm2 = [[S.res(tag + "hm%d_%d" % (i, c)) for c in range(8)] for i in range(2)]
            ckv2 = [A.alloc(512, BF16) for _ in range(2)]
            Rckv2 = [S.res(tag + "ckv%d" % i) for i in range(2)]
            tabC2 = [A.alloc(512, F32) for _ in range(2)]
            tabS2 = [A.alloc(512, F32) for _ in range(2)]
            RtC2 = [S.res(tag + "tabC%d" % i) for i in range(2)]
            RtS2 = [S.res(tag + "tabS%d" % i) for i in range(2)]
            t1 = A.alloc(512, F32)
            t2 = A.alloc(512, F32)
            Rt1, Rt2 = S.res(tag + "t1"), S.res(tag + "t2")
            kpt = A.alloc(512, BF16)
            Rkpt = S.res(tag + "kpt")
            Rkpdma = S.res(tag + "kpdma")
            sc = Scratch(tag + "m")
            dma("pool", Wm, wm_d.rearrange("(c p) n -> p c n", p=128), [], [RWm], wslots["wm"])
            dma("pool", Wukv, wukv_d, [], [RWukv], wslots["wukv"])
            dma("pool", WuqA, wuqa_d.rearrange("(c p) n -> p c n", p=128), [], [RWa], wslots["wuqa"])
            dma("pool", WuqB, wuqb_d.rearrange("(c p) n -> p c n", p=128), [], [RWb], wslots["wuqb"])
            WuqB4 = WuqB.rearrange("p k (h n) -> p k h n", n=128)
            for kc in range(2):
                S.add("dve", (lambda kc=kc: lambda e: e.tensor_scalar(
                    out=WuqB4[:, kc, :, 64:80], in0=WuqB4[:, kc, :, 64:80], scalar1=-1.0, scalar2=None,
                    op0=ALU.mult))(), reads=[RWb], writes=[RWb])
            S.add("dve", lambda e: e.tensor_scalar(out=Wm[:, :, 576:592], in0=Wm[:, :, 576:592], scalar1=-1.0,
                                                   scalar2=None, op0=ALU.mult), reads=[RWm], writes=[RWm])
            Va4 = Va.rearrange("p k (i n) -> p k i n", n=192)
            S.add("pool", lambda e: e.memset(Va4[:, :, :, 64:128], 1.0), writes=[RVa1])
            mainc, statc, vc = Cyc([0, 1, 2, 3]), Cyc([4, 5]), Cyc([6, 7])

            def m1_tabs(t):
                cols = slice(t * TILE, (t + 1) * TILE)
                dma("sp", tabC2[t % 2][64:96, :], cos_d[:, cols], [], [RtC2[t % 2]], wslots["tabc%d" % (t % 2)])
                dma("sp", tabS2[t % 2][64:96, :], sin_d[:, cols], [], [RtS2[t % 2]], wslots["tabs%d" % (t % 2)])

            m1_tabs(0)
            norm_tile(sc, s, 0, G_MIX, [Hm2[0][:, c, :] for c in range(8)], RHm2[0], statc, act_sq=True)
            for t in range(NT):
                cols = slice(t * TILE, (t + 1) * TILE)
                Hm, RHm = Hm2[t % 2], RHm2[t % 2]
                ckv, Rckv = ckv2[t % 2], Rckv2[t % 2]
                tabC, tabS, RtC, RtS = tabC2[t % 2], tabS2[t % 2], RtC2[t % 2], RtS2[t % 2]

                def proj(wcol, m, Hm=Hm, RHm=RHm):
                    b = mainc.next()
                    for c in range(8):
                        mm(banks[b][0:m, :], Wm[:, c, wcol:wcol + m], Hm[:, c, :], c == 0, c == 7, [RWm, RHm[c]], [RB[b]])
                    return b

                def cq_a():
                    bq0, bq1 = proj(0, 128), proj(128, 128)
                    qa, Rqa = sc.sq()
                    act(qa, banks[bq0], AF.Square, [RB[bq0]], [Rqa])
                    qb, Rqb = sc.sq()
                    act(qb, banks[bq1], AF.Square, [RB[bq1]], [Rqb])
                    return (bq0, bq1, qa, Rqa, qb, Rqb)

                def cq_b(st_, t=t, cols=cols):
                    bq0, bq1, qa, Rqa, qb, Rqb = st_
                    b2 = statc.next()
                    mm(banks[b2], ONES, qa, True, False, [Rqa, Rc], [RB[b2]])
                    mm(banks[b2], ONES, qb, False, True, [Rqb, Rc], [RB[b2]])
                    r, Rr_ = rstd_from(sc, b2, (0, 128), 1.0 / 256)
                    stt("dve", cq[:, 0, cols], banks[bq0], gcol(G_QA), r, ALU.mult, ALU.mult, [RB[bq0], Rr_, Rc], [Rcq[0][t]])
                    stt("dve", cq[:, 1, cols], banks[bq1], gcol(G_QA + 1), r, ALU.mult, ALU.mult, [RB[bq1], Rr_, Rc], [Rcq[1][t]])

                def ckv_a():
                    bkv = proj(256, 128)
                    q, Rq = sc.sq()
                    act(q, banks[bkv], AF.Square, [RB[bkv]], [Rq])
                    return (bkv, q, Rq)

                def ckv_b(st_, ckv=ckv, Rckv=Rckv):
                    bkv, q, Rq = st_
                    b2 = statc.next()
                    mm(banks[b2], ONES, q, True, True, [Rq, Rc], [RB[b2]])
                    r, Rr_ = rstd_from(sc, b2, (0, 128), 1.0 / 128)
                    stt("dve", ckv, banks[bkv], gcol(G_KVA), r, ALU.mult, ALU.mult, [RB[bkv], Rr_, Rc], [Rckv])

                def kpe_a():
                    bka, bkb = proj(384, 128), proj(512, 128)
                    q, Rq = sc.sq()
                    act(q[0:96, :], banks[bka][0:96, :], AF.Square, [RB[bka]], [Rq])
                    return (bka, bkb, q, Rq)

                def kpe_b(st_, t=t, cols=cols, tabC=tabC, tabS=tabS, RtC=RtC, RtS=RtS):
                    bka, bkb, q, Rq = st_
                    b2 = statc.next()
                    mm(banks[b2][0:96, :], BD96[0:96, 0:96], q[0:96, :], True, True, [Rq, Rc], [RB[b2]])
                    r, Rr_ = rstd_from(sc, b2, (0, 96), 1.0)
                    stt("dve", t1[64:96, :], banks[bka][64:96, :], gcol(G_KRS, 64, 96), tabC[64:96, :], ALU.mult, ALU.mult,
                        [RB[bka], RtC, Rc], [Rt1])
                    stt("dve", t2[64:96, :], banks[bkb][64:96, :], gcol(G_KRW, 64, 96), tabS[64:96, :], ALU.mult, ALU.mult,
                        [RB[bkb], RtS, Rc], [Rt2])
                    tt("dve", t1[64:96, :], t1[64:96, :], t2[64:96, :], ALU.add, [Rt1, Rt2], [Rt1])
                    tt("dve", kpt[64:96, :], t1[64:96, :], r[64:96, :], ALU.mult, [Rt1, Rr_], [Rkpt])

                def mk_kn(hp, t=t, cols=cols, ckv=ckv, Rckv=Rckv):
                    def a():
                        b = mainc.next()
                        mm(banks[b], Wukv[:, hp * 128:(hp + 1) * 128], ckv, True, True, [RWukv, Rckv], [RB[b]])
                        q, Rq = sc.sq()
                        act(q, banks[b], AF.Square, [RB[b]], [Rq])
                        return (b, q, Rq)

                    def bfn(st_):
                        b, q, Rq = st_
                        b2 = statc.next()
                        mm(banks[b2], BD64, q, True, True, [Rq, Rc], [RB[b2]])
                        r, Rr_ = rstd_from(sc, b2, (0, 128), 1.0)
                        stt("dve", Kt[0:64, 2 * hp, cols], banks[b][0:64, :], gcol(G_KN, 0, 64), r[0:64, :],
                            ALU.mult, ALU.mult, [RB[b], Rr_, Rc], [RKn[2 * hp][t]])
                        stt("dve", Kt[0:64, 2 * hp + 1, cols], banks[b][64:128, :], gcol(G_KN, 64, 128), r[64:128, :],
                            ALU.mult, ALU.mult, [RB[b], Rr_, Rc], [RKn[2 * hp + 1][t]])
                    return (a, bfn)

                jobs = [(ckv_a, ckv_b), (cq_a, cq_b), (kpe_a, kpe_b)] + [mk_kn(hp) for hp in range(4)]
                run_jobs(jobs[:3])
                for blk in range(4):
                    b = vc.next()
                    mm(banks[b], ckv[:, blk * 128:(blk + 1) * 128], Wukv[:, 512:1024], True, True, [Rckv, RWukv], [RB[b]])
                    bv = banks[b].rearrange("p (i two n) -> p i two n", two=2, n=64)
                    copy("act", Va4[:, 4 * t + blk, :, 0:64], bv[:, :, 0, :], [RB[b], RVa1], [RVa[t]])
                    copy("dve", Va4[:, 4 * t + blk, :, 128:192], bv[:, :, 1, :], [RB[b], RVa1], [RVa[t]])
                if t + 1 < NT:
                    m1_tabs(t + 1)
                    norm_tile(sc, s, t + 1, G_MIX, [Hm2[(t + 1) % 2][:, c, :] for c in range(8)], RHm2[(t + 1) % 2], statc, act_sq=True)
                run_jobs(jobs[3:])
                dma("sp", Kt[64:96, :, cols], kpt[64:96, :].unsqueeze(1).broadcast_to([32, 8, TILE]), [Rkpt],
                    [RKp[h][t] for h in range(8)] + [Rkpdma], wslots["kpe"])

            S.new_phase()
            S.tag = "m2_s%d" % s
            A.release(m1_mark)
            Wo = v3(A.alloc(8 * 1024, BF16), 1024)
            RWo = S.res(tag + "wo")
            Qh = [A.alloc(512, BF16) for _ in range(3)]
            RQh = [S.res(tag + "qh%d" % i) for i in range(3)]
            PTm = [A.alloc(512, BF16) for _ in range(4)]
            RPTm = [S.res(tag + "ptm%d" % i) for i in range(4)]
            attn_m2 = [v3(A.alloc(4 * TILE, BF16), TILE) for _ in range(2)]
            Rattn_m2 = [[S.res(tag + "am%d_%d" % (i, c)) for c in range(4)] for i in range(2)]
            tabC2 = [A.alloc(512, F32)] * 2
            tabS2 = [A.alloc(512, F32)] * 2
            RtC2 = [S.res(tag + "tabCb")] * 2
            RtS2 = [S.res(tag + "tabSb")] * 2
            t1 = A.alloc(512, F32)
            t2 = A.alloc(512, F32)
            Rt1, Rt2 = S.res(tag + "t1b"), S.res(tag + "t2b")
            Rr = A.alloc(512, F32)
            RRr = S.res(tag + "Rrm")
            sc = Scratch(tag + "a")
            dma("pool", Wo, wo_d.rearrange("(c p) n -> p c n", p=128), [], [RWo], wslots["wo"])
            scale = 1.0 / math.sqrt(96.0)
            sbank = [0, 1, 2]
            obank = [3, 4]
            bA, bB, bM = 5, 6, 7
            bW = bM
            LOOK = 2
            NH = NT * 8
            NS = NH * 16

            def m2_tabs(t):
                cols = slice(t * TILE, (t + 1) * TILE)
                dma("sp", tabC2[t % 2][64:96, :], cos_d[:, cols], [], [RtC2[t % 2]], wslots["tabc%d" % (t % 2)])
                dma("sp", tabS2[t % 2][64:96, :], sin_d[:, cols], [], [RtS2[t % 2]], wslots["tabs%d" % (t % 2)])

            qstate = {}

            def qprod_a(th):
                t, h = th // 8, th % 8
                cols = slice(t * TILE, (t + 1) * TILE)
                for kc in range(2):
                    mm(banks[bA], WuqA[:, kc, h * 128:(h + 1) * 128], cq[:, kc, cols], kc == 0, kc == 1,
                       [RWa, Rcq[kc][t]], [RB[bA]])
                for kc in range(2):
                    mm(banks[bB], WuqB[:, kc, h * 128:(h + 1) * 128], cq[:, kc, cols], kc == 0, kc == 1,
                       [RWb, Rcq[kc][t]], [RB[bB]])
                q, Rq = sc.sq()
                act(q[0:96, :], banks[bA][0:96, :], AF.Square, [RB[bA]], [Rq])
                qstate[th] = (q, Rq)

            def qprod_b(th):
                t, h = th // 8, th % 8
                q, Rq = qstate.pop(th)
                qi = th % 3
                tabC, tabS, RtC, RtS = tabC2[t % 2], tabS2[t % 2], RtC2[t % 2], RtS2[t % 2]
                mm(banks[bM][0:96, :], BD96[0:96, 0:96], q[0:96, :], True, True, [Rq, Rc], [RB[bM]])
                r, Rr_ = rstd_from(sc, bM, (0, 96), 1.0)
                stt("dve", Qh[qi][0:64, :], banks[bA][0:64, :], gcol(G_QN, 0, 64), r[0:64, :], ALU.mult, ALU.mult,
                    [RB[bA], Rr_, Rc], [RQh[qi]])
                stt("dve", t1[64:96, :], banks[bA][64:96, :], gcol(G_QRS, 64, 96), tabC[64:96, :], ALU.mult, ALU.mult,
                    [RB[bA], RtC, Rc], [Rt1])
                stt("dve", t2[64:96, :], banks[bB][64:96, :], gcol(G_QRW, 64, 96), tabS[64:96, :], ALU.mult, ALU.mult,
                    [RB[bB], RtS, Rc], [Rt2])
                tt("dve", t1[64:96, :], t1[64:96, :], t2[64:96, :], ALU.add, [Rt1, Rt2], [Rt1])
                tt("dve", Qh[qi][64:96, :], t1[64:96, :], r[64:96, :], ALU.mult, [Rt1, Rr_], [RQh[qi]])

            def wo_piece(t, m):
                cols = slice(t * TILE, (t + 1) * TILE)
                dc_, c = m // 8, m % 8
                if c < 4:
                    rhs, rr = attn_m2[t % 2][:, c, :], Rattn_m2[t % 2][c]
                else:
                    rhs, rr = attn_s[:, c - 4, cols], Rattn_s[c - 4][t]
                mm(banks[bW], Wo[:, c, dc_ * 128:(dc_ + 1) * 128], rhs, c == 0, c == 7, [RWo, rr], [RB[bW]])
                if c == 7:
                    tt("dve", X[:, dc_, cols], banks[bW], X[:, dc_, cols], ALU.add, [RB[bW], XR[dc_][t]], [XR[dc_][t]])

            def m2_S(i):
                th, kc = i // 16, i % 16
                t, h = th // 8, th % 8
                qi = th % 3
                bs = sbank[i % 3]
                mm(banks[bs], Kt[0:96, h, kc * 128:(kc + 1) * 128], Qh[qi][0:96, :], True, True,
                   [RKn[h][kc // 4], RKp[h][kc // 4], RQh[qi]], [RB[bs]])
                act(PTm[i % 4], banks[bs], AF.Exp, [RB[bs]], [RPTm[i % 4]], scale=scale)
                if th + 1 < NH:
                    if kc == 0:
                        qprod_a(th + 1)
                    if kc == 3:
                        qprod_b(th + 1)
                if h == 6 and kc == 4 and t + 1 < NT:
                    m2_tabs(t + 1)
                if t > 0 and 6 <= kc < 14:
                    wo_piece(t - 1, h * 8 + (kc - 6))

            def m2_PV(i):
                th, kc = i // 16, i % 16
                t, h = th // 8, th % 8
                ob = obank[th % 2]
                half, pair = h % 2, h // 2
                vlo = pair * 192 + (0 if half == 0 else 64)
                mm(banks[ob], Va[:, kc, vlo:vlo + 128], PTm[i % 4], kc == 0, kc == 15,
                   [RVa[kc // 4], RVa1, RPTm[i % 4]], [RB[ob]])
                if kc == 15:
                    nr = (half * 64, half * 64 + 64)
                    dr = (64 - half * 64, 128 - half * 64)
                    r_o, r_i = Rr[nr[0]:nr[1], :], banks[ob][dr[0]:dr[1], :]
                    S.add("dve", lambda e: e.reciprocal(out=r_o, in_=r_i), reads=[RB[ob]], writes=[RRr])
                    tt("dve", attn_m2[t % 2][nr[0]:nr[1], pair, :], banks[ob][nr[0]:nr[1], :], r_o, ALU.mult,
                       [RB[ob], RRr], [Rattn_m2[t % 2][pair]])

            m2_tabs(0)
            qprod_a(0)
            qprod_b(0)
            for i in range(NS + LOOK):
                if i < NS:
                    m2_S(i)
                if i - LOOK >= 0:
                    m2_PV(i - LOOK)
            for m in range(64):
                wo_piece(NT - 1, m)

        ffn_segment([(0, 0, 0), (0, 0, 1)])
        mix_phase(0)
        ffn_segment([(0, 1, 0), (0, 1, 1), (1, 0, 0), (1, 0, 1)])
        mix_phase(1)
        ffn_segment([(1, 1, 0), (1, 1, 1)])
        S.final_wait("sp", yslots)
        S.emit(nc)
    return nc


def _host_constants():
    ident = np.eye(128, dtype=np.float32)
    cbf = np.zeros((128, 512), np.float32)
    cbf[:, 0:128] = 1.0
    bd64 = np.zeros((128, 128), np.float32)
    bd64[0:64, 0:64] = 1.0 / 64
    bd64[64:128, 64:128] = 1.0 / 64
    bd96 = np.zeros((128, 128), np.float32)
    bd96[0:64, 0:64] = 1.0 / 64
    bd96[64:96, 64:96] = 1.0 / 32
    cbf[:, 128:256] = bd64
    cbf[:, 256:384] = bd96
    cbf[:, 384:512] = ident
    i = np.arange(128)[:, None]
    c = np.arange(384)[None, :]
    rel = 128 + i - c
    valid = np.abs(rel) <= 128
    al = np.zeros((128, 8, 384), np.float32)
    for h in range(8):
        al[:, h, :] = np.where(valid, -np.abs(rel) * (2.0 ** (2 - h)), -30000.0)
    pos = np.arange(SEQ, dtype=np.float64)
    inv = 1.0 / (10000.0 ** (np.arange(0, 32, 2, dtype=np.float64) / 32))
    ang = pos[None, :] * inv[:, None]
    cos2 = np.concatenate([np.cos(ang), np.cos(ang)], 0).astype(np.float32)
    sin2 = np.concatenate([np.sin(ang), np.sin(ang)], 0).astype(np.float32)
    return ident, cbf, al.reshape(128, 8 * 384), cos2, sin2


_NC_CACHE = {}


def kernel(x, g_ffn1, w1_gate, w1_up, w1_down, g_mix, w_in, g_q_a, w_uq, g_kv_a, w_ukv,
           g_mla_qn, g_mla_qr, g_mla_kn, g_mla_kr, g_swa_q, g_swa_k, sink, w_o,
           g_ffn2, w2_gate, w2_up, w2_down):
    f = lambda a: np.ascontiguousarray(np.asarray(a, dtype=np.float32))
    x = f(x)
    w_in = f(w_in); w_uq = f(w_uq); w_ukv = f(w_ukv)
    hq, hkv, hkr = w_in[:, 0:256], w_in[:, 256:384], w_in[:, 384:416]
    sq, sk, sv = w_in[:, 416:928], w_in[:, 928:1056], w_in[:, 1056:1184]
    w_swa = np.concatenate([sq, sk[:, 0:64], sk[:, 0:64], sk[:, 64:128], sk[:, 64:128], sv], axis=1)
    pad = hkv[:, 0:64]
    pad2 = hkv[:, 64:96]
    w_mla = np.concatenate([hq, hkv, pad, hkr, pad2, pad, hkr[:, 16:32], hkr[:, 0:16], pad2], axis=1)
    uq = w_uq.reshape(256, 8, 96)
    w_uq_a = np.concatenate([uq, uq[:, :, 0:32]], axis=2).reshape(256, 1024)
    w_uq_b = np.concatenate([uq[:, :, 0:64], uq[:, :, 80:96], uq[:, :, 64:80], uq[:, :, 0:32]], axis=2).reshape(256, 1024)
    ukv = w_ukv.reshape(128, 8, 128)
    w_ukv_r = np.concatenate([ukv[:, :, 0:64].reshape(128, 512), ukv[:, :, 64:128].reshape(128, 512)], axis=1)
    gains = np.ones((128, NG), np.float32)
    gains[:, G_FFN1:G_FFN1 + 8] = f(g_ffn1).reshape(8, 128).T
    gains[:, G_MIX:G_MIX + 8] = f(g_mix).reshape(8, 128).T
    gains[:, G_FFN2:G_FFN2 + 8] = f(g_ffn2).reshape(8, 128).T
    gains[:, G_QA:G_QA + 2] = f(g_q_a).reshape(2, 128).T
    gains[:, G_KVA] = f(g_kv_a)
    gains[0:64, G_QN] = f(g_mla_qn)
    gains[:, G_KN] = np.tile(f(g_mla_kn), 2)
    gains[:, G_SWQ] = np.tile(f(g_swa_q), 2)
    gains[:, G_SWK] = np.tile(f(g_swa_k), 2)
    gqr, gkr = f(g_mla_qr), f(g_mla_kr)
    gains[64:96, G_QRS] = gqr
    gains[64:96, G_QRW] = np.concatenate([gqr[16:32], gqr[0:16]])
    gains[64:96, G_KRS] = gkr
    gains[64:96, G_KRW] = np.concatenate([gkr[16:32], gkr[0:16]])
    gains[:, G_SINK:G_SINK + 8] = np.broadcast_to(f(sink)[None, :], (128, 8))
    ident, cbf, alibi, cos2, sin2 = _host_constants()

    if "nc" not in _NC_CACHE:
        _NC_CACHE["nc"] = build_program()
    nc = _NC_CACHE["nc"]
    shared = {
        "w1_gate": f(w1_gate), "w1_up": f(w1_up), "w1_down": f(w1_down),
        "w2_gate": f(w2_gate), "w2_up": f(w2_up), "w2_down": f(w2_down),
        "w_swa": np.ascontiguousarray(w_swa), "w_mla": np.ascontiguousarray(w_mla),
        "w_ukv_r": np.ascontiguousarray(w_ukv_r), "w_uq_a": np.ascontiguousarray(w_uq_a), "w_uq_b": np.ascontiguousarray(w_uq_b),
        "w_o": f(w_o), "gains": gains, "ident": ident, "cbf": cbf, "alibi": alibi, "cos2": cos2, "sin2": sin2,
    }
    in_maps = []
    for c in range(NCORES):
        m = dict(shared)
        m["x"] = np.ascontiguousarray(x[NSEQ * c:NSEQ * (c + 1)].reshape(NSEQ * SEQ, D_MODEL))
        in_maps.append(m)
    res = run_bass_kernel_spmd(nc, in_maps, core_ids=list(range(NCORES)))
    out = np.stack([np.asarray(r["out"]).reshape(NSEQ, SEQ, D_MODEL) for r in res.results], axis=0)
    return out.reshape(NCORES * NSEQ, SEQ, D_MODEL).astype(np.float32)
```

```python
import math
from contextlib import ExitStack

import numpy as np
import concourse.bass as bass
import concourse.mybir as mybir
from concourse.bass_utils import run_bass_kernel_spmd

F32 = mybir.dt.float32
BF16 = mybir.dt.bfloat16
ALU = mybir.AluOpType
AF = mybir.ActivationFunctionType

ENGS = ("pe", "act", "dve", "pool", "sp")

D_MODEL = 1024
D_FF = 2816
SEQ = 2048
NSEQ = 2
NCORES = 8
TILE = 512
NT = SEQ // TILE
NJ = D_FF // 128
RMS_EPS = 1e-6


class Res:
    __slots__ = ("name", "w", "r")

    def __init__(self, name, inherit=()):
        self.name = name
        self.w = None
        self.r = list(inherit)


class DmaSlot:
    def __init__(self, name):
        self.name = name
        self.count = 0
        self.sem = None


class Op:
    __slots__ = ("eng", "fn", "reads", "writes", "deps", "signal", "sigval",
                 "dma", "slot", "waits", "idx", "tag")

    def __init__(self, eng, fn, reads, writes, dma, slot):
        self.eng = eng
        self.fn = fn
        self.reads = reads
        self.writes = writes
        self.deps = []
        self.signal = False
        self.sigval = None
        self.dma = dma
        self.slot = slot
        self.waits = []


class Sched:
    def __init__(self):
        self.ops = {e: [] for e in ENGS}
        self.all = []
        self.slots = []
        self.phase_res = []
        self.frontier = []
        self._final = None
        self.tag = None
        self.scopes = False

    def slot(self, name):
        s = DmaSlot(name)
        self.slots.append(s)
        return s

    def res(self, name, persist=False):
        if persist:
            return Res(name)
        r = Res(name, self.frontier)
        self.phase_res.append(r)
        return r

    def new_phase(self):
        last = {}
        for r in self.phase_res:
            for op in ([r.w] if r.w is not None else []) + r.r:
                key = ("slot", id(op.slot)) if op.dma else ("eng", op.eng)
                if key not in last or last[key].idx < op.idx:
                    last[key] = op
        for op in self.frontier:
            key = ("slot", id(op.slot)) if op.dma else ("eng", op.eng)
            if key not in last or last[key].idx < op.idx:
                last[key] = op
        self.frontier = list(last.values())

    def _dep(self, op, prod):
        if prod is None or prod is op:
            return
        if (not prod.dma) and (not op.dma) and prod.eng == op.eng:
            return
        op.deps.append(prod)

    def add(self, eng, fn, reads=(), writes=(), dma=False, slot=None):
        op = Op(eng, fn, tuple(reads), tuple(writes), dma, slot)
        op.idx = len(self.all)
        op.tag = self.tag
        for r in op.reads:
            self._dep(op, r.w)
        for w in op.writes:
            self._dep(op, w.w)
            for rd in w.r:
                self._dep(op, rd)
        for r in op.reads:
            r.r.append(op)
        for w in op.writes:
            w.w = op
            w.r = []
        self.ops[eng].append(op)
        self.all.append(op)
        return op

    def final_wait(self, eng, slots):
        self._final = (eng, list(slots))

    def finalize(self):
        for op in self.all:
            for d in op.deps:
                d.signal = True
        cnt = {e: 0 for e in ENGS}
        for op in self.all:
            if op.dma:
                op.slot.count += 16
                op.sigval = op.slot.count
            elif op.signal:
                cnt[op.eng] += 1
                op.sigval = cnt[op.eng]
        seen = {e: {} for e in ENGS}
        for op in self.all:
            need = {}
            for d in op.deps:
                key = ("slot", d.slot) if d.dma else ("eng", d.eng)
                if need.get(key, 0) < d.sigval:
                    need[key] = d.sigval
            for key, v in need.items():
                if seen[op.eng].get(key, 0) >= v:
                    continue
                seen[op.eng][key] = v
                op.waits.append((key, v))

    def emit(self, nc):
        self.finalize()
        with ExitStack() as st:
            esem = {e: st.enter_context(nc.semaphore("s_" + e)) for e in ENGS if e != "sp"}
            for s in self.slots:
                s.sem = st.enter_context(nc.semaphore("d_" + s.name))
            block = st.enter_context(nc.Block())
            fin = self._final

            def run(engname, eng):
                cur = [None, None]

                def set_scope(tag):
                    if not self.scopes or tag == cur[0]:
                        return
                    if cur[1] is not None:
                        cur[1].close()
                        cur[1] = None
                    cur[0] = tag
                    if tag is not None:
                        es = ExitStack()
                        es.enter_context(nc.named_scope(tag))
                        cur[1] = es

                for op in self.ops[engname]:
                    set_scope(op.tag)
                    for key, v in op.waits:
                        sem = key[1].sem if key[0] == "slot" else esem[key[1]]
                        eng.wait_ge(sem, v)
                    ins = op.fn(eng)
                    if op.dma:
                        ins.then_inc(op.slot.sem, 16)
                    elif op.signal:
                        ins.then_inc(esem[op.eng], 1)
                set_scope(None)
                if fin is not None and fin[0] == engname:
                    for s in fin[1]:
                        if s.count:
                            eng.wait_ge(s.sem, s.count)

            @block.tensor
            def _(e):
                run("pe", e)

            @block.scalar
            def _(e):
                run("act", e)

            @block.vector
            def _(e):
                run("dve", e)

            @block.gpsimd
            def _(e):
                run("pool", e)

            @block.sync
            def _(e):
                run("sp", e)


class Cyc:
    def __init__(self, ids):
        self.ids = list(ids)
        self.k = 0

    def next(self):
        v = self.ids[self.k % len(self.ids)]
        self.k += 1
        return v


class Arena:
    def __init__(self, ap2d, nwords):
        self.ap = ap2d
        self.n = nwords
        self.off = 0

    def alloc(self, nelem, dt):
        n4 = nelem if dt == F32 else (nelem + 1) // 2
        assert self.off + n4 <= self.n, ("SBUF arena overflow", self.off, n4, self.n)
        a = self.ap[:, self.off:self.off + n4]
        self.off += n4
        if dt != F32:
            a = a.bitcast(dt)
        return a

    def mark(self):
        return self.off

    def release(self, m):
        self.off = m


G_FFN1, G_MIX, G_FFN2, G_QA, G_KVA = 0, 8, 16, 24, 26
G_QN, G_KN, G_SWQ, G_SWK = 27, 28, 29, 30
G_QRS, G_QRW, G_KRS, G_KRW = 31, 32, 33, 34
G_SINK = 35
NG = 48


def build_program(scopes=False):
    nc = bass.Bass("TRN2", target_bir_lowering=False)
    NTOK = NSEQ * SEQ

    def din(name, shape):
        return nc.dram_tensor(name, list(shape), F32, kind="ExternalInput").ap()

    x_d = din("x", [NTOK, D_MODEL])
    out_d = nc.dram_tensor("out", [NTOK, D_MODEL], F32, kind="ExternalOutput").ap()
    wg_d = [din("w1_gate", [D_MODEL, D_FF]), din("w2_gate", [D_MODEL, D_FF])]
    wu_d = [din("w1_up", [D_MODEL, D_FF]), din("w2_up", [D_MODEL, D_FF])]
    wd_d = [din("w1_down", [D_FF, D_MODEL]), din("w2_down", [D_FF, D_MODEL])]
    wsw_d = din("w_swa", [D_MODEL, 896])
    wm_d = din("w_mla", [D_MODEL, 640])
    wukv_d = din("w_ukv_r", [128, 1024])
    wuqa_d = din("w_uq_a", [256, 1024])
    wuqb_d = din("w_uq_b", [256, 1024])
    wo_d = din("w_o", [D_MODEL, D_MODEL])
    gains_d = din("gains", [128, NG])
    ident_d = din("ident", [128, 128])
    cbf_d = din("cbf", [128, 512])
    alibi_d = din("alibi", [128, 8 * 384])
    cos_d = din("cos2", [32, SEQ])
    sin_d = din("sin2", [32, SEQ])

    S = Sched()
    S.scopes = scopes
    with ExitStack() as st:
        NW = 53200
        arena_t = st.enter_context(nc.sbuf_tensor("arena", [128, NW], F32))
        A = Arena(arena_t[:, :], NW)
        banks = [st.enter_context(nc.psum_tensor("bank%d" % i, [128, 512], F32))[:] for i in range(8)]
        RB = [S.res("bank%d" % i, persist=True) for i in range(8)]

        def v3(ap, b):
            return ap.rearrange("p (a b) -> p a b", b=b)

        X = v3(A.alloc(8 * SEQ, F32), SEQ)
        XR = [[S.res("X%d_%d" % (c, t), persist=True) for t in range(NT)] for c in range(8)]
        gains = A.alloc(NG, F32)
        ident = A.alloc(128, F32)
        epsc = A.alloc(2, F32)
        esink = A.alloc(8, F32)
        cbf = A.alloc(512, BF16)
        ONES, BD64, BD96, IDB = cbf[:, 0:128], cbf[:, 128:256], cbf[:, 256:384], cbf[:, 384:512]
        Rc = S.res("consts", persist=True)
        cslot = S.slot("consts")
        S.add("sp", lambda e: e.dma_start(out=gains, in_=gains_d), writes=[Rc], dma=True, slot=cslot)
        S.add("sp", lambda e: e.dma_start(out=ident, in_=ident_d), writes=[Rc], dma=True, slot=cslot)
        cslot2 = S.slot("consts_sw")
        S.add("pool", lambda e: e.dma_start(out=cbf, in_=cbf_d), writes=[Rc], dma=True, slot=cslot2)
        S.add("dve", lambda e: e.memset(epsc[:, 0:1], RMS_EPS), writes=[Rc])
        S.add("act", lambda e: e.activation(out=esink, in_=gains[:, G_SINK:G_SINK + 8], func=AF.Exp),
              reads=[Rc], writes=[Rc])
        EPS = epsc[:, 0:1]
        base_mark = A.mark()

        def gcol(i, lo=0, hi=128):
            return gains[lo:hi, i:i + 1]

        def mm(out, lhsT, rhs, start, stop, reads, writes):
            S.add("pe", lambda e: e.matmul(out, lhsT=lhsT, rhs=rhs, start=start, stop=stop),
                  reads=reads, writes=writes)

        def act(out, in_, func, reads, writes, scale=None, bias=None):
            kw = {}
            if scale is not None:
                kw["scale"] = scale
            if bias is not None:
                kw["bias"] = bias
            S.add("act", lambda e: e.activation(out=out, in_=in_, func=func, **kw), reads=reads, writes=writes)

        def stt(eng, out, in0, scalar, in1, op0, op1, reads, writes):
            S.add(eng, lambda e: e.scalar_tensor_tensor(out=out, in0=in0, scalar=scalar, in1=in1, op0=op0, op1=op1),
                  reads=reads, writes=writes)

        def tt(eng, out, in0, in1, op, reads, writes):
            S.add(eng, lambda e: e.tensor_tensor(out=out, in0=in0, in1=in1, op=op), reads=reads, writes=writes)

        def copy(eng, out, in_, reads, writes):
            if eng == "act":
                S.add("act", lambda e: e.activation(out=out, in_=in_, func=AF.Copy), reads=reads, writes=writes)
            else:
                S.add(eng, lambda e: e.tensor_copy(out=out, in_=in_), reads=reads, writes=writes)

        def dma(q, out, in_, reads, writes, slot):
            S.add(q, lambda e: e.dma_start(out=out, in_=in_), reads=reads, writes=writes, dma=True, slot=slot)

        class Scratch:
            def __init__(self, tag):
                self.sqb = [A.alloc(512, BF16) for _ in range(4)]
                self.Rsqb = [S.res(tag + "sqb%d" % i) for i in range(4)]
                self.lnt = A.alloc(512, F32)
                self.Rlnt = S.res(tag + "lnt")
                self.rstd = [A.alloc(512, F32) for _ in range(2)]
                self.Rrstd = [S.res(tag + "rstd%d" % i) for i in range(2)]
                self.k = 0
                self.kr = 0

            def sq(self):
                i = self.k % 4
                self.k += 1
                return self.sqb[i], self.Rsqb[i]

            def rs(self):
                i = self.kr % 2
                self.kr += 1
                return self.rstd[i], self.Rrstd[i]

        def rstd_from(sc, bank_i, rows, scale):
            lo, hi = rows
            r, Rr = sc.rs()
            act(sc.lnt[lo:hi, :], banks[bank_i][lo:hi, :], AF.Ln, [RB[bank_i], Rc], [sc.Rlnt],
                scale=scale, bias=epsc[lo:hi, 0:1])
            act(r[lo:hi, :], sc.lnt[lo:hi, :], AF.Exp, [sc.Rlnt], [Rr], scale=-0.5)
            return r, Rr

        def norm_tile(sc, s_unused, t, gbase, Hc, HR, statc, act_sq=False):
            cols = slice(t * TILE, (t + 1) * TILE)
            sb = statc.next()
            for c in range(8):
                q, Rq = sc.sq()
                if c % 2 == 0 or act_sq:
                    act(q, X[:, c, cols], AF.Square, [XR[c][t]], [Rq])
                else:
                    tt("dve", q, X[:, c, cols], X[:, c, cols], ALU.mult, [XR[c][t]], [Rq])
                mm(banks[sb], ONES, q, c == 0, c == 7, [Rq, Rc], [RB[sb]])
            r, Rr = rstd_from(sc, sb, (0, 128), 1.0 / D_MODEL)
            for c in range(8):
                stt("dve", Hc[c], X[:, c, cols], gcol(gbase + c), r, ALU.mult, ALU.mult,
                    [XR[c][t], Rr, Rc], [HR[c]])

        xslots = [S.slot("xin0"), S.slot("xin1")]
        yslots = [S.slot("yout0"), S.slot("yout1")]
        wgslots = [S.slot("wg0"), S.slot("wg1")]
        wuslots = [S.slot("wu0"), S.slot("wu1")]
        wdslots = [S.slot("wd0"), S.slot("wd1")]

        def ffn_segment(sts):
            S.new_phase()
            A.release(base_mark)
            tag = "fs%d_%d_%d_" % sts[0]
            actb = v3(A.alloc(NJ * 1024, BF16), 1024)
            actR = [[S.res(tag + "act%d_%d" % (j, u)) for u in range(2)] for j in range(NJ)]
            H2 = [v3(A.alloc(8 * 1024, BF16), 1024) for _ in range(2)]
            HR2 = [[[S.res(tag + "H%d_%d_%d" % (i, c, u)) for c in range(8)] for u in range(2)] for i in range(2)]
            wgs = [v3(A.alloc(8 * 256, BF16), 256) for _ in range(2)]
            wus = [v3(A.alloc(8 * 256, BF16), 256) for _ in range(2)]
            wds = [v3(A.alloc(NJ * 256, BF16), 256) for _ in range(2)]
            Rwg = [S.res(tag + "wg%d" % i) for i in range(2)]
            Rwu = [S.res(tag + "wu%d" % i) for i in range(2)]
            Rwd = [S.res(tag + "wd%d" % i) for i in range(2)]
            io = [A.alloc(1024, F32) for _ in range(2)]
            Rio = [S.res(tag + "io%d" % i) for i in range(2)]
            sg = [A.alloc(512, BF16) for _ in range(2)]
            Rsg = [S.res(tag + "sg%d" % i) for i in range(2)]
            sc = Scratch(tag)
            gc, uc, dc, mc = Cyc([0, 1]), Cyc([2, 3]), Cyc([4, 5]), Cyc([6, 7])
            cnt = {"sg": 0, "ev": 0, "io": 0, "cur": 0, "cur_o": 0}

            def prep_pieces(k):
                s, which, stile = sts[k]
                ptag = "ffn%d_s%d" % (which + 1, s)
                gbase = G_FFN1 if which == 0 else G_FFN2
                H, HR = H2[k % 2], HR2[k % 2]
                pcs = []
                for u in range(2):
                    t = stile * 2 + u
                    cols = slice(t * TILE, (t + 1) * TILE)
                    if which == 0:
                        for bb in range(4):
                            for half in range(2):
                                def pT(bb=bb, half=half, t=t):
                                    tok = s * SEQ + t * TILE + bb * 128
                                    tc0 = t * TILE + bb * 128
                                    if half == 0:
                                        ii = cnt["io"] % 2
                                        cnt["io"] += 1
                                        cnt["cur"] = ii
                                        dma("sp", io[ii], x_d[tok:tok + 128, :], [], [Rio[ii]], xslots[ii])
                                    ii = cnt["cur"]
                                    b = mc.next()
                                    for c4 in range(4):
                                        c = half * 4 + c4
                                        S.add("pe", (lambda b=b, c4=c4, c=c, ii=ii: lambda e: e.transpose(
                                            banks[b][:, c4 * 128:(c4 + 1) * 128], io[ii][:, c * 128:(c + 1) * 128], ident))(),
                                            reads=[Rio[ii], Rc], writes=[RB[b]])
                                    copy("act" if cnt["ev"] % 2 == 0 else "dve", X[:, half * 4:half * 4 + 4, tc0:tc0 + 128],
                                         banks[b].rearrange("p (k n) -> p k n", n=128), [RB[b]],
                                         [XR[half * 4 + c4][t] for c4 in range(4)])
                                    cnt["ev"] += 1
                                pcs.append(pT)
                    st_ = {}

                    def p_sq(part, t=t, cols=cols, st_=st_):
                        if part == 0:
                            st_["sb"] = mc.next()
                        st_["q%d" % part] = []
                        for c in range(part * 4, part * 4 + 4):
                            q, Rq = sc.sq()
                            if c % 2 == 0:
                                act(q, X[:, c, cols], AF.Square, [XR[c][t]], [Rq])
                            else:
                                tt("dve", q, X[:, c, cols], X[:, c, cols], ALU.mult, [XR[c][t]], [Rq])
                            st_["q%d" % part].append((q, Rq))

                    def p_stat(part, st_=st_):
                        sb = st_["sb"]
                        for i, (q, Rq) in enumerate(st_["q%d" % part]):
                            c = part * 4 + i
                            mm(banks[sb], ONES, q, c == 0, c == 7, [Rq, Rc], [RB[sb]])

                    def p_fin(t=t, cols=cols, u=u, st_=st_):
                        r, Rr_ = rstd_from(sc, st_["sb"], (0, 128), 1.0 / D_MODEL)
                        for c in range(8):
                            stt("dve", H[:, c, u * TILE:(u + 1) * TILE], X[:, c, cols], gcol(gbase + c), r, ALU.mult, ALU.mult,
                                [XR[c][t], Rr_, Rc], [HR[u][c]])
                    pcs += [lambda p_sq=p_sq: p_sq(0), lambda p_stat=p_stat: p_stat(0),
                            lambda p_sq=p_sq: p_sq(1), lambda p_stat=p_stat: p_stat(1), p_fin]

                def wrap(f):
                    def g():
                        old = S.tag
                        S.tag = ptag
                        f()
                        S.tag = old
                    return g
                return [wrap(f) for f in pcs]

            def output_pieces(k):
                s, which, stile = sts[k]
                ptag = "ffn%d_s%d" % (which + 1, s)
                pcs = []
                for u in range(2):
                    t = stile * 2 + u
                    for bb in range(4):
                        for half in range(2):
                            def pO(t=t, bb=bb, half=half):
                                tok = s * SEQ + t * TILE + bb * 128
                                tc0 = t * TILE + bb * 128
                                if half == 0:
                                    cnt["cur_o"] = cnt["io"] % 2
                                    cnt["io"] += 1
                                ii = cnt["cur_o"]
                                b = mc.next()
                                for c4 in range(4):
                                    c = half * 4 + c4
                                    S.add("pe", (lambda b=b, c4=c4, c=c, tc0=tc0: lambda e: e.transpose(
                                        banks[b][:, c4 * 128:(c4 + 1) * 128], X[:, c, tc0:tc0 + 128], ident))(),
                                        reads=[XR[c][t], Rc], writes=[RB[b]])
                                copy("act" if cnt["ev"] % 2 == 0 else "dve", io[ii][:, half * 512:(half + 1) * 512],
                                     banks[b], [RB[b]], [Rio[ii]])
                                cnt["ev"] += 1
                                if half == 1:
                                    dma("sp", out_d[tok:tok + 128, :], io[ii], [Rio[ii]], [], yslots[ii])
                            pcs.append(pO)

                def wrap(f):
                    def g():
                        old = S.tag
                        S.tag = ptag
                        f()
                        S.tag = old
                    return g
                return [wrap(f) for f in pcs]

            def phase_a(k, pieces=()):
                pieces = list(pieces)
                s, which, stile = sts[k]
                S.tag = "ffn%d_s%d" % (which + 1, s)
                H, HR = H2[k % 2], HR2[k % 2]
                wgv = wg_d[which].rearrange("(c p) n -> p c n", p=128)
                wuv = wu_d[which].rearrange("(c p) n -> p c n", p=128)
                for jp in range(NJ // 2):
                    sl = jp % 2
                    dma("pool", wgs[sl], wgv[:, :, jp * 256:(jp + 1) * 256], [], [Rwg[sl]], wgslots[sl])
                    dma("pool", wus[sl], wuv[:, :, jp * 256:(jp + 1) * 256], [], [Rwu[sl]], wuslots[sl])
                    for jj in range(2):
                        j = jp * 2 + jj
                        for u in range(2):
                            bg = gc.next()
                            bu = uc.next()
                            for c in range(8):
                                mm(banks[bg], wgs[sl][:, c, jj * 128:(jj + 1) * 128], H[:, c, u * TILE:(u + 1) * TILE],
                                   c == 0, c == 7, [Rwg[sl], HR[u][c]], [RB[bg]])
                            for c in range(8):
                                mm(banks[bu], wus[sl][:, c, jj * 128:(jj + 1) * 128], H[:, c, u * TILE:(u + 1) * TILE],
                                   c == 0, c == 7, [Rwu[sl], HR[u][c]], [RB[bu]])
                            i = cnt["sg"] % 2
                            cnt["sg"] += 1
                            act(sg[i], banks[bg], AF.Silu, [RB[bg]], [Rsg[i]])
                            tt("dve", actb[:, j, u * TILE:(u + 1) * TILE], sg[i], banks[bu], ALU.mult,
                               [Rsg[i], RB[bu]], [actR[j][u]])
                            if pieces:
                                pieces.pop(0)()
                while pieces:
                    pieces.pop(0)()

            def phase_b(k, pieces=()):
                pieces = list(pieces)
                s, which, stile = sts[k]
                wdv = wd_d[which].rearrange("(j p) n -> p j n", p=128)
                for cp in range(4):
                    S.tag = "ffn%d_s%d" % (which + 1, s)
                    sl = cp % 2
                    dma("pool", wds[sl], wdv[:, :, cp * 256:(cp + 1) * 256], [], [Rwd[sl]], wdslots[sl])
                    for cc in range(2):
                        c = cp * 2 + cc
                        for u in range(2):
                            t = stile * 2 + u
                            cols = slice(t * TILE, (t + 1) * TILE)
                            b = dc.next()
                            for j in range(NJ):
                                mm(banks[b], wds[sl][:, j, cc * 128:(cc + 1) * 128], actb[:, j, u * TILE:(u + 1) * TILE],
                                   j == 0, j == NJ - 1, [Rwd[sl], actR[j][u]], [RB[b]])
                            stt("dve", X[:, c, cols], banks[b], 0.5, X[:, c, cols], ALU.mult, ALU.add,
                                [RB[b], XR[c][t]], [XR[c][t]])
                            for _ in range(2):
                                if pieces:
                                    pieces.pop(0)()
                while pieces:
                    pieces.pop(0)()

            for p in prep_pieces(0):
                p()
            pend_out = []
            for k in range(len(sts)):
                phase_a(k, pend_out)
                nxt = prep_pieces(k + 1) if k + 1 < len(sts) else []
                phase_b(k, nxt)
                pend_out = output_pieces(k) if sts[k][1] == 1 else []
            for p in pend_out:
                p()

        wslots = {n: S.slot(n) for n in ("wsw", "alibi", "wm", "wukv", "wuqa", "wuqb", "wo", "tabc0", "tabs0", "tabc1", "tabs1", "kpe")}

        def mix_phase(s):
            S.new_phase()
            S.tag = "swa_s%d" % s
            A.release(base_mark)
            tag = "m%d_" % s
            attn_s = v3(A.alloc(4 * SEQ, BF16), SEQ)
            Rattn_s = [[S.res(tag + "as%d_%d" % (c, t)) for t in range(NT)] for c in range(4)]
            mix_mark = A.mark()

            Wsw = v3(A.alloc(8 * 896, BF16), 896)
            RWsw = S.res(tag + "wsw")
            alibi = v3(A.alloc(8 * 384, F32), 384)
            Ralibi = S.res(tag + "alibi")
            exs = [A.alloc(384, F32) for _ in range(3)]
            Rexs = [S.res(tag + "exs%d" % i) for i in range(3)]
            lnd = A.alloc(512, F32)
            Rlnd = S.res(tag + "lnd")
            Qs = v3(A.alloc(8 * SEQ, BF16), SEQ)
            RQs = [[S.res(tag + "qs%d_%d" % (c, t)) for t in range(NT)] for c in range(4)]
            RQz = [S.res(tag + "qz%d" % t) for t in range(NT)]
            Ks = v3(A.alloc(2 * SEQ, BF16), SEQ)
            RKs = [[S.res(tag + "ks%d_%d" % (g, t)) for t in range(NT)] for g in range(2)]
            Vs = v3(A.alloc(16 * 320, BF16), 320)
            RVs = [S.res(tag + "vs%d" % t) for t in range(NT)]
            RVs1 = S.res(tag + "vs_ones")
            Hs2 = [v3(A.alloc(8 * TILE, BF16), TILE) for _ in range(2)]
            RHs2 = [[S.res(tag + "hs%d_%d" % (i, c)) for c in range(8)] for i in range(2)]
            NPT = 6
            PT = [A.alloc(384, BF16) for _ in range(NPT)]
            RPT = [S.res(tag + "pts%d" % i) for i in range(NPT)]
            dent = A.alloc(512, F32)
            Rdent = S.res(tag + "dent")
            Rr = A.alloc(512, F32)
            RRr = S.res(tag + "Rr")
            sc = Scratch(tag + "s")
            dma("pool", Wsw, wsw_d.rearrange("(c p) n -> p c n", p=128), [], [RWsw], wslots["wsw"])
            dma("sp", alibi, alibi_d.rearrange("p (h n) -> p h n", n=384), [], [Ralibi], wslots["alibi"])
            act(alibi, alibi, AF.Exp, [Ralibi], [Ralibi], scale=0.125)
            for lo in (0, 128, 256):
                S.add("pool", (lambda lo=lo: lambda e: e.memset(Vs[:, :, lo:lo + 64], 1.0))(), writes=[RVs1])
            for t_ in range(NT):
                S.add("pool", (lambda t_=t_: lambda e: e.memset(Qs[:, :, t_ * TILE:(t_ + 1) * TILE], 0.0))(),
                      writes=[RQz[t_]] + [RQs[c][t_] for c in range(4)])
            mainc, statc, vc = Cyc([0, 1, 2, 3]), Cyc([4, 5]), Cyc([6, 7])

            def run_jobs(jobs):
                prev = None
                for jb in jobs:
                    st_ = jb[0]()
                    if prev is not None:
                        prev[0](prev[1])
                    prev = (jb[1], st_)
                if prev is not None:
                    prev[0](prev[1])

            norm_tile(sc, s, 0, G_MIX, [Hs2[0][:, c, :] for c in range(8)], RHs2[0], statc, act_sq=True)
            for t in range(NT):
                cols = slice(t * TILE, (t + 1) * TILE)
                Hs, RHs = Hs2[t % 2], RHs2[t % 2]

                def mk_job(wcol, gidx, out_ap, out_res, Hs=Hs, RHs=RHs, split_q=None):
                    def stage_a():
                        b = mainc.next()
                        for c in range(8):
                            mm(banks[b], Wsw[:, c, wcol:wcol + 128], Hs[:, c, :], c == 0, c == 7, [RWsw, RHs[c]], [RB[b]])
                        q, Rq = sc.sq()
                        act(q, banks[b], AF.Square, [RB[b]], [Rq])
                        return (b, q, Rq)

                    def stage_b(st_):
                        b, q, Rq = st_
                        b2 = statc.next()
                        mm(banks[b2], BD64, q, True, True, [Rq, Rc], [RB[b2]])
                        r, Rr_ = rstd_from(sc, b2, (0, 128), 1.0)
                        if split_q is None:
                            stt("dve", out_ap, banks[b], gcol(gidx), r, ALU.mult, ALU.mult, [RB[b], Rr_, Rc], [out_res])
                        else:
                            cq_, cols_ = split_q
                            for hf in range(2):
                                lo = hf * 64
                                stt("dve", Qs[lo:lo + 64, 2 * cq_ + hf, cols_], banks[b][lo:lo + 64, :], gcol(gidx, lo, lo + 64),
                                    r[lo:lo + 64, :], ALU.mult, ALU.mult, [RB[b], Rr_, Rc], [out_res])
                    return (stage_a, stage_b)

                jobs = [mk_job(cq_ * 128, G_SWQ, None, RQs[cq_][t], split_q=(cq_, cols)) for cq_ in range(4)]
                jobs += [mk_job(512 + g * 128, G_SWK, Ks[:, g, cols], RKs[g][t]) for g in range(2)]
                run_jobs(jobs[:3])
                if t + 1 < NT:
                    norm_tile(sc, s, t + 1, G_MIX, [Hs2[(t + 1) % 2][:, c, :] for c in range(8)], RHs2[(t + 1) % 2], statc, act_sq=True)
                run_jobs(jobs[3:])
                b = vc.next()
                for blk in range(4):
                    for c in range(8):
                        mm(banks[b][:, blk * 128:(blk + 1) * 128], Hs[:, c, blk * 128:(blk + 1) * 128],
                           Wsw[:, c, 768:896], c == 0, c == 7, [RWsw, RHs[c]], [RB[b]])
                bv = banks[b].rearrange("p (k n) -> p k n", n=128)
                copy("act", Vs[:, 4 * t:4 * t + 4, 64:128], bv[:, :, 0:64], [RB[b], RVs1], [RVs[t]])
                copy("dve", Vs[:, 4 * t:4 * t + 4, 192:256], bv[:, :, 64:128], [RB[b], RVs1], [RVs[t]])

            S.tag = "swaattn_s%d" % s
            sbank = [0, 1, 2]
            obank = [3, 4]
            LOOK = 2
            NS = 8 * 16

            def hinfo(h):
                g, half, qc = h // 4, h % 2, h // 2
                r0 = half * 64
                if half == 0:
                    vlo = 64 if g == 0 else 192
                else:
                    vlo = 0 if g == 0 else 128
                return g, half, qc, r0, vlo

            def swa_S(i):
                h, j = i // 16, i % 16
                g, half, qc, r0, vlo = hinfo(h)
                r1 = r0 + 64
                qlo, qhi = max(j - 1, 0), min(j + 1, 15)
                ncol = (qhi - qlo + 1) * 128
                off = (qlo - (j - 1)) * 128
                b = sbank[i % 3]
                qt_res = [RQs[qc][tt_] for tt_ in range((qlo * 128) // TILE, (qhi * 128) // TILE + 1)]
                mm(banks[b][:, 0:ncol], Ks[:, g, j * 128:(j + 1) * 128], Qs[:, h, qlo * 128:(qhi + 1) * 128],
                   True, True, [RKs[g][j // 4]] + qt_res, [RB[b]])
                ex, Rex = exs[i % 3], Rexs[i % 3]
                act(ex[:, 0:ncol], banks[b][:, 0:ncol], AF.Exp, [RB[b]], [Rex], scale=0.125)
                tt("pool" if i % 3 == 0 else "dve", PT[i % NPT][:, 0:ncol], ex[:, 0:ncol], alibi[:, h, off:off + ncol],
                   ALU.mult, [Rex, Ralibi], [RPT[i % NPT]])

            def swa_pv(h, n):
                g, half, qc, r0, vlo = hinfo(h)
                ob = obank[(h * 4 + n // 4) % 2]
                jjs = [jj for jj in (n - 1, n, n + 1) if 0 <= jj < 16]
                for k, jj in enumerate(jjs):
                    qlo = max(jj - 1, 0)
                    off = (n - qlo) * 128
                    pi = (h * 16 + jj) % NPT
                    mm(banks[ob][:, (n % 4) * 128:(n % 4 + 1) * 128], Vs[:, jj, vlo:vlo + 128],
                       PT[pi][:, off:off + 128], k == 0, k == len(jjs) - 1,
                       [RVs[jj // 4], RVs1, RPT[pi]], [RB[ob]])
                if n % 4 == 3:
                    tq = n // 4
                    nr = (r0, r0 + 64)
                    dr = (64 - r0, 128 - r0)
                    act(lnd[dr[0]:dr[1], :], banks[ob][dr[0]:dr[1], :], AF.Ln, [RB[ob], Rc], [Rlnd],
                        scale=1.0, bias=esink[dr[0]:dr[1], h:h + 1])
                    act(Rr[dr[0]:dr[1], :], lnd[dr[0]:dr[1], :], AF.Exp, [Rlnd], [RRr], scale=-1.0)
                    tt("dve", attn_s[nr[0]:nr[1], qc, tq * TILE:(tq + 1) * TILE], banks[ob][nr[0]:nr[1], :],
                       Rr[dr[0]:dr[1], :], ALU.mult, [RB[ob], RRr], [Rattn_s[qc][tq]])

            def swa_PV(i):
                h, j = i // 16, i % 16
                if j >= 1:
                    swa_pv(h, j - 1)
                if j == 15:
                    swa_pv(h, 15)

            for i in range(NS + LOOK):
                if i < NS:
                    swa_S(i)
                if i - LOOK >= 0:
                    swa_PV(i - LOOK)

            S.new_phase()
            S.tag = "m1_s%d" % s
            A.release(mix_mark)
            Kt = v3(A.alloc(8 * SEQ, BF16), SEQ)
            RKn = [[S.res(tag + "kn%d_%d" % (h, t)) for t in range(NT)] for h in range(8)]
            RKp = [[S.res(tag + "kp%d_%d" % (h, t)) for t in range(NT)] for h in range(8)]
            Va = v3(A.alloc(16 * 768, BF16), 768)
            RVa = [S.res(tag + "va%d" % t) for t in range(NT)]
            RVa1 = S.res(tag + "va_ones")
            cq = v3(A.alloc(2 * SEQ, BF16), SEQ)
            Rcq = [[S.res(tag + "cq%d_%d" % (k, t)) for t in range(NT)] for k in range(2)]
            WuqA = v3(A.alloc(2 * 1024, BF16), 1024)
            WuqB = v3(A.alloc(2 * 1024, BF16), 1024)
            RWa, RWb = S.res(tag + "wuqa"), S.res(tag + "wuqb")
            m1_mark = A.mark()
            Wm = v3(A.alloc(8 * 640, BF16), 640)
            RWm = S.res(tag + "wm")
            Wukv = A.alloc(1024, BF16)
            RWukv = S.res(tag + "wukv")
            Hm2 = [v3(A.alloc(8 * TILE, BF16), TILE) for _ in range(2)]
            RHm2 = [[S.res(tag + "hm%d_%d" % (i, c)) for c in range(8)] for i in range(2)]
            ckv2 = [A.alloc(512, BF16) for _ in range(2)]
            Rckv2 = [S.res(tag + "ckv%d" % i) for i in range(2)]
            tabC2 = [A.alloc(512, F32) for _ in range(2)]
            tabS2 = [A.alloc(512, F32) for _ in range(2)]
            RtC2 = [S.res(tag + "tabC%d" % i) for i in range(2)]
            RtS2 = [S.res(tag + "tabS%d" % i) for i in range(2)]
            t1 = A.alloc(512, F32)
            t2 = A.alloc(512, F32)
            Rt1, Rt2 = S.res(tag + "t1"), S.res(tag + "t2")
            kpt = A.alloc(512, BF16)
            Rkpt = S.res(tag + "kpt")
            Rkpdma = S.res(tag + "kpdma")
            sc = Scratch(tag + "m")
            dma("pool", Wm, wm_d.rearrange("(c p) n -> p c n", p=128), [], [RWm], wslots["wm"])
            dma("pool", Wukv, wukv_d, [], [RWukv], wslots["wukv"])
            dma("pool", WuqA, wuqa_d.rearrange("(c p) n -> p c n", p=128), [], [RWa], wslots["wuqa"])
            dma("pool", WuqB, wuqb_d.rearrange("(c p) n -> p c n", p=128), [], [RWb], wslots["wuqb"])
            WuqB4 = WuqB.rearrange("p k (h n) -> p k h n", n=128)
            for kc in range(2):
                S.add("dve", (lambda kc=kc: lambda e: e.tensor_scalar(
                    out=WuqB4[:, kc, :, 64:80], in0=WuqB4[:, kc, :, 64:80], scalar1=-1.0, scalar2=None,
                    op0=ALU.mult))(), reads=[RWb], writes=[RWb])
            S.add("dve", lambda e: e.tensor_scalar(out=Wm[:, :, 576:592], in0=Wm[:, :, 576:592], scalar1=-1.0,
                                                   scalar2=None, op0=ALU.mult), reads=[RWm], writes=[RWm])
            Va4 = Va.rearrange("p k (i n) -> p k i n", n=192)
            S.add("pool", lambda e: e.memset(Va4[:, :, :, 64:128], 1.0), writes=[RVa1])
            mainc, statc, vc = Cyc([0, 1, 2, 3]), Cyc([4, 5]), Cyc([6, 7])

            def m1_tabs(t):
                cols = slice(t * TILE, (t + 1) * TILE)
                dma("sp", tabC2[t % 2][64:96, :], cos_d[:, cols], [], [RtC2[t % 2]], wslots["tabc%d" % (t % 2)])
                dma("sp", tabS2[t % 2][64:96, :], sin_d[:, cols], [], [RtS2[t % 2]], wslots["tabs%d" % (t % 2)])

            m1_tabs(0)
            norm_tile(sc, s, 0, G_MIX, [Hm2[0][:, c, :] for c in range(8)], RHm2[0], statc, act_sq=True)
            for t in range(NT):
                cols = slice(t * TILE, (t + 1) * TILE)
                Hm, RHm = Hm2[t % 2], RHm2[t % 2]
                ckv, Rckv = ckv2[t % 2], Rckv2[t % 2]
                tabC, tabS, RtC, RtS = tabC2[t % 2], tabS2[t % 2], RtC2[t % 2], RtS2[t % 2]

                def proj(wcol, m, Hm=Hm, RHm=RHm):
                    b = mainc.next()
                    for c in range(8):
                        mm(banks[b][0:m, :], Wm[:, c, wcol:wcol + m], Hm[:, c, :], c == 0, c == 7, [RWm, RHm[c]], [RB[b]])
                    return b

                def cq_a():
                    bq0, bq1 = proj(0, 128), proj(128, 128)
                    qa, Rqa = sc.sq()
                    act(qa, banks[bq0], AF.Square, [RB[bq0]], [Rqa])
                    qb, Rqb = sc.sq()
                    act(qb, banks[bq1], AF.Square, [RB[bq1]], [Rqb])
                    return (bq0, bq1, qa, Rqa, qb, Rqb)

                def cq_b(st_, t=t, cols=cols):
                    bq0, bq1, qa, Rqa, qb, Rqb = st_
                    b2 = statc.next()
                    mm(banks[b2], ONES, qa, True, False, [Rqa, Rc], [RB[b2]])
                    mm(banks[b2], ONES, qb, False, True, [Rqb, Rc], [RB[b2]])
                    r, Rr_ = rstd_from(sc, b2, (0, 128), 1.0 / 256)
                    stt("dve", cq[:, 0, cols], banks[bq0], gcol(G_QA), r, ALU.mult, ALU.mult, [RB[bq0], Rr_, Rc], [Rcq[0][t]])
                    stt("dve", cq[:, 1, cols], banks[bq1], gcol(G_QA + 1), r, ALU.mult, ALU.mult, [RB[bq1], Rr_, Rc], [Rcq[1][t]])

                def ckv_a():
                    bkv = proj(256, 128)
                    q, Rq = sc.sq()
                    act(q, banks[bkv], AF.Square, [RB[bkv]], [Rq])
                    return (bkv, q, Rq)

                def ckv_b(st_, ckv=ckv, Rckv=Rckv):
                    bkv, q, Rq = st_
                    b2 = statc.next()
                    mm(banks[b2], ONES, q, True, True, [Rq, Rc], [RB[b2]])
                    r, Rr_ = rstd_from(sc, b2, (0, 128), 1.0 / 128)
                    stt("dve", ckv, banks[bkv], gcol(G_KVA), r, ALU.mult, ALU.mult, [RB[bkv], Rr_, Rc], [Rckv])

                def kpe_a():
                    bka, bkb = proj(384, 128), proj(512, 128)
                    q, Rq = sc.sq()
                    act(q[0:96, :], banks[bka][0:96, :], AF.Square, [RB[bka]], [Rq])
                    return (bka, bkb, q, Rq)

                def kpe_b(st_, t=t, cols=cols, tabC=tabC, tabS=tabS, RtC=RtC, RtS=RtS):
                    bka, bkb, q, Rq = st_
                    b2 = statc.next()
                    mm(banks[b2][0:96, :], BD96[0:96, 0:96], q[0:96, :], True, True, [Rq, Rc], [RB[b2]])
                    r, Rr_ = rstd_from(sc, b2, (0, 96), 1.0)
                    stt("dve", t1[64:96, :], banks[bka][64:96, :], gcol(G_KRS, 64, 96), tabC[64:96, :], ALU.mult, ALU.mult,
                        [RB[bka], RtC, Rc], [Rt1])
                    stt("dve", t2[64:96, :], banks[bkb][64:96, :], gcol(G_KRW, 64, 96), tabS[64:96, :], ALU.mult, ALU.mult,
                        [RB[bkb], RtS, Rc], [Rt2])
                    tt("dve", t1[64:96, :], t1[64:96, :], t2[64:96, :], ALU.add, [Rt1, Rt2], [Rt1])
                    tt("dve", kpt[64:96, :], t1[64:96, :], r[64:96, :], ALU.mult, [Rt1, Rr_], [Rkpt])

                def mk_kn(hp, t=t, cols=cols, ckv=ckv, Rckv=Rckv):
                    def a():
                        b = mainc.next()
                        mm(banks[b], Wukv[:, hp * 128:(hp + 1) * 128], ckv, True, True, [RWukv, Rckv], [RB[b]])
                        q, Rq = sc.sq()
                        act(q, banks[b], AF.Square, [RB[b]], [Rq])
                        return (b, q, Rq)

                    def bfn(st_):
                        b, q, Rq = st_
                        b2 = statc.next()
                        mm(banks[b2], BD64, q, True, True, [Rq, Rc], [RB[b2]])
                        r, Rr_ = rstd_from(sc, b2, (0, 128), 1.0)
                        stt("dve", Kt[0:64, 2 * hp, cols], banks[b][0:64, :], gcol(G_KN, 0, 64), r[0:64, :],
                            ALU.mult, ALU.mult, [RB[b], Rr_, Rc], [RKn[2 * hp][t]])
                        stt("dve", Kt[0:64, 2 * hp + 1, cols], banks[b][64:128, :], gcol(G_KN, 64, 128), r[64:128, :],
                            ALU.mult, ALU.mult, [RB[b], Rr_, Rc], [RKn[2 * hp + 1][t]])
                    return (a, bfn)

                jobs = [(ckv_a, ckv_b), (cq_a, cq_b), (kpe_a, kpe_b)] + [mk_kn(hp) for hp in range(4)]
                run_jobs(jobs[:3])
                for blk in range(4):
                    b = vc.next()
                    mm(banks[b], ckv[:, blk * 128:(blk + 1) * 128], Wukv[:, 512:1024], True, True, [Rckv, RWukv], [RB[b]])
                    bv = banks[b].rearrange("p (i two n) -> p i two n", two=2, n=64)
                    copy("act", Va4[:, 4 * t + blk, :, 0:64], bv[:, :, 0, :], [RB[b], RVa1], [RVa[t]])
                    copy("dve", Va4[:, 4 * t + blk, :, 128:192], bv[:, :, 1, :], [RB[b], RVa1], [RVa[t]])
                if t + 1 < NT:
                    m1_tabs(t + 1)
                    norm_tile(sc, s, t + 1, G_MIX, [Hm2[(t + 1) % 2][:, c, :] for c in range(8)], RHm2[(t + 1) % 2], statc, act_sq=True)
                run_jobs(jobs[3:])
                dma("sp", Kt[64:96, :, cols], kpt[64:96, :].unsqueeze(1).broadcast_to([32, 8, TILE]), [Rkpt],
                    [RKp[h][t] for h in range(8)] + [Rkpdma], wslots["kpe"])

            S.new_phase()
            S.tag = "m2_s%d" % s
            A.release(m1_mark)
            Wo = v3(A.alloc(8 * 1024, BF16), 1024)
            RWo = S.res(tag + "wo")
            Qh = [A.alloc(512, BF16) for _ in range(3)]
            RQh = [S.res(tag + "qh%d" % i) for i in range(3)]
            PTm = [A.alloc(512, BF16) for _ in range(4)]
            RPTm = [S.res(tag + "ptm%d" % i) for i in range(4)]
            attn_m2 = [v3(A.alloc(4 * TILE, BF16), TILE) for _ in range(2)]
            Rattn_m2 = [[S.res(tag + "am%d_%d" % (i, c)) for c in range(4)] for i in range(2)]
            tabC2 = [A.alloc(512, F32)] * 2
            tabS2 = [A.alloc(512, F32)] * 2
            RtC2 = [S.res(tag + "tabCb")] * 2
            RtS2 = [S.res(tag + "tabSb")] * 2
            t1 = A.alloc(512, F32)
            t2 = A.alloc(512, F32)
            Rt1, Rt2 = S.res(tag + "t1b"), S.res(tag + "t2b")
            Rr = A.alloc(512, F32)
            RRr = S.res(tag + "Rrm")
            sc = Scratch(tag + "a")
            dma("pool", Wo, wo_d.rearrange("(c p) n -> p c n", p=128), [], [RWo], wslots["wo"])
            scale = 1.0 / math.sqrt(96.0)
            sbank = [0, 1, 2]
            obank = [3, 4]
            bA, bB, bM = 5, 6, 7
            bW = bM
            LOOK = 2
            NH = NT * 8
            NS = NH * 16

            def m2_tabs(t):
                cols = slice(t * TILE, (t + 1) * TILE)
                dma("sp", tabC2[t % 2][64:96, :], cos_d[:, cols], [], [RtC2[t % 2]], wslots["tabc%d" % (t % 2)])
                dma("sp", tabS2[t % 2][64:96, :], sin_d[:, cols], [], [RtS2[t % 2]], wslots["tabs%d" % (t % 2)])

            qstate = {}

            def qprod_a(th):
                t, h = th // 8, th % 8
                cols = slice(t * TILE, (t + 1) * TILE)
                for kc in range(2):
                    mm(banks[bA], WuqA[:, kc, h * 128:(h + 1) * 128], cq[:, kc, cols], kc == 0, kc == 1,
                       [RWa, Rcq[kc][t]], [RB[bA]])
                for kc in range(2):
                    mm(banks[bB], WuqB[:, kc, h * 128:(h + 1) * 128], cq[:, kc, cols], kc == 0, kc == 1,
                       [RWb, Rcq[kc][t]], [RB[bB]])
                q, Rq = sc.sq()
                act(q[0:96, :], banks[bA][0:96, :], AF.Square, [RB[bA]], [Rq])
                qstate[th] = (q, Rq)

            def qprod_b(th):
                t, h = th // 8, th % 8
                q, Rq = qstate.pop(th)
                qi = th % 3
                tabC, tabS, RtC, RtS = tabC2[t % 2], tabS2[t % 2], RtC2[t % 2], RtS2[t % 2]
                mm(banks[bM][0:96, :], BD96[0:96, 0:96], q[0:96, :], True, True, [Rq, Rc], [RB[bM]])
                r, Rr_ = rstd_from(sc, bM, (0, 96), 1.0)
                stt("dve", Qh[qi][0:64, :], banks[bA][0:64, :], gcol(G_QN, 0, 64), r[0:64, :], ALU.mult, ALU.mult,
                    [RB[bA], Rr_, Rc], [RQh[qi]])
                stt("dve", t1[64:96, :], banks[bA][64:96, :], gcol(G_QRS, 64, 96), tabC[64:96, :], ALU.mult, ALU.mult,
                    [RB[bA], RtC, Rc], [Rt1])
                stt("dve", t2[64:96, :], banks[bB][64:96, :], gcol(G_QRW, 64, 96), tabS[64:96, :], ALU.mult, ALU.mult,
                    [RB[bB], RtS, Rc], [Rt2])
                tt("dve", t1[64:96, :], t1[64:96, :], t2[64:96, :], ALU.add, [Rt1, Rt2], [Rt1])
                tt("dve", Qh[qi][64:96, :], t1[64:96, :], r[64:96, :], ALU.mult, [Rt1, Rr_], [RQh[qi]])

            def wo_piece(t, m):
                cols = slice(t * TILE, (t + 1) * TILE)
                dc_, c = m // 8, m % 8
                if c < 4:
                    rhs, rr = attn_m2[t % 2][:, c, :], Rattn_m2[t % 2][c]
                else:
                    rhs, rr = attn_s[:, c - 4, cols], Rattn_s[c - 4][t]
                mm(banks[bW], Wo[:, c, dc_ * 128:(dc_ + 1) * 128], rhs, c == 0, c == 7, [RWo, rr], [RB[bW]])
                if c == 7:
                    tt("dve", X[:, dc_, cols], banks[bW], X[:, dc_, cols], ALU.add, [RB[bW], XR[dc_][t]], [XR[dc_][t]])

            def m2_S(i):
                th, kc = i // 16, i % 16
                t, h = th // 8, th % 8
                qi = th % 3
                bs = sbank[i % 3]
                mm(banks[bs], Kt[0:96, h, kc * 128:(kc + 1) * 128], Qh[qi][0:96, :], True, True,
                   [RKn[h][kc // 4], RKp[h][kc // 4], RQh[qi]], [RB[bs]])
                act(PTm[i % 4], banks[bs], AF.Exp, [RB[bs]], [RPTm[i % 4]], scale=scale)
                if th + 1 < NH:
                    if kc == 0:
                        qprod_a(th + 1)
                    if kc == 3:
                        qprod_b(th + 1)
                if h == 6 and kc == 4 and t + 1 < NT:
                    m2_tabs(t + 1)
                if t > 0 and 6 <= kc < 14:
                    wo_piece(t - 1, h * 8 + (kc - 6))

            def m2_PV(i):
                th, kc = i // 16, i % 16
                t, h = th // 8, th % 8
                ob = obank[th % 2]
                half, pair = h % 2, h // 2
                vlo = pair * 192 + (0 if half == 0 else 64)
                mm(banks[ob], Va[:, kc, vlo:vlo + 128], PTm[i % 4], kc == 0, kc == 15,
                   [RVa[kc // 4], RVa1, RPTm[i % 4]], [RB[ob]])
                if kc == 15:
                    nr = (half * 64, half * 64 + 64)
                    dr = (64 - half * 64, 128 - half * 64)
                    r_o, r_i = Rr[nr[0]:nr[1], :], banks[ob][dr[0]:dr[1], :]
                    S.add("dve", lambda e: e.reciprocal(out=r_o, in_=r_i), reads=[RB[ob]], writes=[RRr])
                    tt("dve", attn_m2[t % 2][nr[0]:nr[1], pair, :], banks[ob][nr[0]:nr[1], :], r_o, ALU.mult,
                       [RB[ob], RRr], [Rattn_m2[t % 2][pair]])

            m2_tabs(0)
            qprod_a(0)
            qprod_b(0)
            for i in range(NS + LOOK):
                if i < NS:
                    m2_S(i)
                if i - LOOK >= 0:
                    m2_PV(i - LOOK)
            for m in range(64):
                wo_piece(NT - 1, m)

        ffn_segment([(0, 0, 0), (0, 0, 1)])
        mix_phase(0)
        ffn_segment([(0, 1, 0), (0, 1, 1), (1, 0, 0), (1, 0, 1)])
        mix_phase(1)
        ffn_segment([(1, 1, 0), (1, 1, 1)])
        S.final_wait("sp", yslots)
        S.emit(nc)
    return nc


def _host_constants():
    ident = np.eye(128, dtype=np.float32)
    cbf = np.zeros((128, 512), np.float32)
    cbf[:, 0:128] = 1.0
    bd64 = np.zeros((128, 128), np.float32)
    bd64[0:64, 0:64] = 1.0 / 64
    bd64[64:128, 64:128] = 1.0 / 64
    bd96 = np.zeros((128, 128), np.float32)
    bd96[0:64, 0:64] = 1.0 / 64
    bd96[64:96, 64:96] = 1.0 / 32
    cbf[:, 128:256] = bd64
    cbf[:, 256:384] = bd96
    cbf[:, 384:512] = ident
    i = np.arange(128)[:, None]
    c = np.arange(384)[None, :]
    rel = 128 + i - c
    valid = np.abs(rel) <= 128
    al = np.zeros((128, 8, 384), np.float32)
    for h in range(8):
        al[:, h, :] = np.where(valid, -np.abs(rel) * (2.0 ** (2 - h)), -30000.0)
    pos = np.arange(SEQ, dtype=np.float64)
    inv = 1.0 / (10000.0 ** (np.arange(0, 32, 2, dtype=np.float64) / 32))
    ang = pos[None, :] * inv[:, None]
    cos2 = np.concatenate([np.cos(ang), np.cos(ang)], 0).astype(np.float32)
    sin2 = np.concatenate([np.sin(ang), np.sin(ang)], 0).astype(np.float32)
    return ident, cbf, al.reshape(128, 8 * 384), cos2, sin2


_NC_CACHE = {}


def kernel(x, g_ffn1, w1_gate, w1_up, w1_down, g_mix, w_in, g_q_a, w_uq, g_kv_a, w_ukv,
           g_mla_qn, g_mla_qr, g_mla_kn, g_mla_kr, g_swa_q, g_swa_k, sink, w_o,
           g_ffn2, w2_gate, w2_up, w2_down):
    f = lambda a: np.ascontiguousarray(np.asarray(a, dtype=np.float32))
    x = f(x)
    w_in = f(w_in); w_uq = f(w_uq); w_ukv = f(w_ukv)
    hq, hkv, hkr = w_in[:, 0:256], w_in[:, 256:384], w_in[:, 384:416]
    sq, sk, sv = w_in[:, 416:928], w_in[:, 928:1056], w_in[:, 1056:1184]
    w_swa = np.concatenate([sq, sk[:, 0:64], sk[:, 0:64], sk[:, 64:128], sk[:, 64:128], sv], axis=1)
    pad = hkv[:, 0:64]
    pad2 = hkv[:, 64:96]
    w_mla = np.concatenate([hq, hkv, pad, hkr, pad2, pad, hkr[:, 16:32], hkr[:, 0:16], pad2], axis=1)
    uq = w_uq.reshape(256, 8, 96)
    w_uq_a = np.concatenate([uq, uq[:, :, 0:32]], axis=2).reshape(256, 1024)
    w_uq_b = np.concatenate([uq[:, :, 0:64], uq[:, :, 80:96], uq[:, :, 64:80], uq[:, :, 0:32]], axis=2).reshape(256, 1024)
    ukv = w_ukv.reshape(128, 8, 128)
    w_ukv_r = np.concatenate([ukv[:, :, 0:64].reshape(128, 512), ukv[:, :, 64:128].reshape(128, 512)], axis=1)
    gains = np.ones((128, NG), np.float32)
    gains[:, G_FFN1:G_FFN1 + 8] = f(g_ffn1).reshape(8, 128).T
    gains[:, G_MIX:G_MIX + 8] = f(g_mix).reshape(8, 128).T
    gains[:, G_FFN2:G_FFN2 + 8] = f(g_ffn2).reshape(8, 128).T
    gains[:, G_QA:G_QA + 2] = f(g_q_a).reshape(2, 128).T
    gains[:, G_KVA] = f(g_kv_a)
    gains[0:64, G_QN] = f(g_mla_qn)
    gains[:, G_KN] = np.tile(f(g_mla_kn), 2)
    gains[:, G_SWQ] = np.tile(f(g_swa_q), 2)
    gains[:, G_SWK] = np.tile(f(g_swa_k), 2)
    gqr, gkr = f(g_mla_qr), f(g_mla_kr)
    gains[64:96, G_QRS] = gqr
    gains[64:96, G_QRW] = np.concatenate([gqr[16:32], gqr[0:16]])
    gains[64:96, G_KRS] = gkr
    gains[64:96, G_KRW] = np.concatenate([gkr[16:32], gkr[0:16]])
    gains[:, G_SINK:G_SINK + 8] = np.broadcast_to(f(sink)[None, :], (128, 8))
    ident, cbf, alibi, cos2, sin2 = _host_constants()

    if "nc" not in _NC_CACHE:
        _NC_CACHE["nc"] = build_program()
    nc = _NC_CACHE["nc"]
    shared = {
        "w1_gate": f(w1_gate), "w1_up": f(w1_up), "w1_down": f(w1_down),
        "w2_gate": f(w2_gate), "w2_up": f(w2_up), "w2_down": f(w2_down),
        "w_swa": np.ascontiguousarray(w_swa), "w_mla": np.ascontiguousarray(w_mla),
        "w_ukv_r": np.ascontiguousarray(w_ukv_r), "w_uq_a": np.ascontiguousarray(w_uq_a), "w_uq_b": np.ascontiguousarray(w_uq_b),
        "w_o": f(w_o), "gains": gains, "ident": ident, "cbf": cbf, "alibi": alibi, "cos2": cos2, "sin2": sin2,
    }
    in_maps = []
    for c in range(NCORES):
        m = dict(shared)
        m["x"] = np.ascontiguousarray(x[NSEQ * c:NSEQ * (c + 1)].reshape(NSEQ * SEQ, D_MODEL))
        in_maps.append(m)
    res = run_bass_kernel_spmd(nc, in_maps, core_ids=list(range(NCORES)))
    out = np.stack([np.asarray(r["out"]).reshape(NSEQ, SEQ, D_MODEL) for r in res.results], axis=0)
    return out.reshape(NCORES * NSEQ, SEQ, D_MODEL).astype(np.float32)
```

```python
import math
from contextlib import ExitStack

import numpy as np
import concourse.bass as bass
import concourse.mybir as mybir
from concourse.bass_utils import run_bass_kernel_spmd

F32 = mybir.dt.float32
BF16 = mybir.dt.bfloat16
ALU = mybir.AluOpType
AF = mybir.ActivationFunctionType

ENGS = ("pe", "act", "dve", "pool", "sp")

D_MODEL = 1024
D_FF = 2816
SEQ = 2048
NSEQ = 2
NCORES = 8
TILE = 512
NT = SEQ // TILE
NJ = D_FF // 128
RMS_EPS = 1e-6


class Res:
    __slots__ = ("name", "w", "r")

    def __init__(self, name, inherit=()):
        self.name = name
        self.w = None
        self.r = list(inherit)


class DmaSlot:
    def __init__(self, name):
        self.name = name
        self.count = 0
        self.sem = None


class Op:
    __slots__ = ("eng", "fn", "reads", "writes", "deps", "signal", "sigval",
                 "dma", "slot", "waits", "idx", "tag")

    def __init__(self, eng, fn, reads, writes, dma, slot):
        self.eng = eng
        self.fn = fn
        self.reads = reads
        self.writes = writes
        self.deps = []
        self.signal = False
        self.sigval = None
        self.dma = dma
        self.slot = slot
        self.waits = []


class Sched:
    def __init__(self):
        self.ops = {e: [] for e in ENGS}
        self.all = []
        self.slots = []
        self.phase_res = []
        self.frontier = []
        self._final = None
        self.tag = None
        self.scopes = False

    def slot(self, name):
        s = DmaSlot(name)
        self.slots.append(s)
        return s

    def res(self, name, persist=False):
        if persist:
            return Res(name)
        r = Res(name, self.frontier)
        self.phase_res.append(r)
        return r

    def new_phase(self):
        last = {}
        for r in self.phase_res:
            for op in ([r.w] if r.w is not None else []) + r.r:
                key = ("slot", id(op.slot)) if op.dma else ("eng", op.eng)
                if key not in last or last[key].idx < op.idx:
                    last[key] = op
        for op in self.frontier:
            key = ("slot", id(op.slot)) if op.dma else ("eng", op.eng)
            if key not in last or last[key].idx < op.idx:
                last[key] = op
        self.frontier = list(last.values())

    def _dep(self, op, prod):
        if prod is None or prod is op:
            return
        if (not prod.dma) and (not op.dma) and prod.eng == op.eng:
            return
        op.deps.append(prod)

    def add(self, eng, fn, reads=(), writes=(), dma=False, slot=None):
        op = Op(eng, fn, tuple(reads), tuple(writes), dma, slot)
        op.idx = len(self.all)
        op.tag = self.tag
        for r in op.reads:
            self._dep(op, r.w)
        for w in op.writes:
            self._dep(op, w.w)
            for rd in w.r:
                self._dep(op, rd)
        for r in op.reads:
            r.r.append(op)
        for w in op.writes:
            w.w = op
            w.r = []
        self.ops[eng].append(op)
        self.all.append(op)
        return op

    def final_wait(self, eng, slots):
        self._final = (eng, list(slots))

    def finalize(self):
        for op in self.all:
            for d in op.deps:
                d.signal = True
        cnt = {e: 0 for e in ENGS}
        for op in self.all:
            if op.dma:
                op.slot.count += 16
                op.sigval = op.slot.count
            elif op.signal:
                cnt[op.eng] += 1
                op.sigval = cnt[op.eng]
        seen = {e: {} for e in ENGS}
        for op in self.all:
            need = {}
            for d in op.deps:
                key = ("slot", d.slot) if d.dma else ("eng", d.eng)
                if need.get(key, 0) < d.sigval:
                    need[key] = d.sigval
            for key, v in need.items():
                if seen[op.eng].get(key, 0) >= v:
                    continue
                seen[op.eng][key] = v
                op.waits.append((key, v))

    def emit(self, nc):
        self.finalize()
        with ExitStack() as st:
            esem = {e: st.enter_context(nc.semaphore("s_" + e)) for e in ENGS if e != "sp"}
            for s in self.slots:
                s.sem = st.enter_context(nc.semaphore("d_" + s.name))
            block = st.enter_context(nc.Block())
            fin = self._final

            def run(engname, eng):
                cur = [None, None]

                def set_scope(tag):
                    if not self.scopes or tag == cur[0]:
                        return
                    if cur[1] is not None:
                        cur[1].close()
                        cur[1] = None
                    cur[0] = tag
                    if tag is not None:
                        es = ExitStack()
                        es.enter_context(nc.named_scope(tag))
                        cur[1] = es

                for op in self.ops[engname]:
                    set_scope(op.tag)
                    for key, v in op.waits:
                        sem = key[1].sem if key[0] == "slot" else esem[key[1]]
                        eng.wait_ge(sem, v)
                    ins = op.fn(eng)
                    if op.dma:
                        ins.then_inc(op.slot.sem, 16)
                    elif op.signal:
                        ins.then_inc(esem[op.eng], 1)
                set_scope(None)
                if fin is not None and fin[0] == engname:
                    for s in fin[1]:
                        if s.count:
                            eng.wait_ge(s.sem, s.count)

            @block.tensor
            def _(e):
                run("pe", e)

            @block.scalar
            def _(e):
                run("act", e)

            @block.vector
            def _(e):
                run("dve", e)

            @block.gpsimd
            def _(e):
                run("pool", e)

            @block.sync
            def _(e):
                run("sp", e)


class Cyc:
    def __init__(self, ids):
        self.ids = list(ids)
        self.k = 0

    def next(self):
        v = self.ids[self.k % len(self.ids)]
        self.k += 1
        return v


class Arena:
    def __init__(self, ap2d, nwords):
        self.ap = ap2d
        self.n = nwords
        self.off = 0

    def alloc(self, nelem, dt):
        n4 = nelem if dt == F32 else (nelem + 1) // 2
        assert self.off + n4 <= self.n, ("SBUF arena overflow", self.off, n4, self.n)
        a = self.ap[:, self.off:self.off + n4]
        self.off += n4
        if dt != F32:
            a = a.bitcast(dt)
        return a

    def mark(self):
        return self.off

    def release(self, m):
        self.off = m


G_FFN1, G_MIX, G_FFN2, G_QA, G_KVA = 0, 8, 16, 24, 26
G_QN, G_KN, G_SWQ, G_SWK = 27, 28, 29, 30
G_QRS, G_QRW, G_KRS, G_KRW = 31, 32, 33, 34
G_SINK = 35
NG = 48


def build_program(scopes=False):
    nc = bass.Bass("TRN2", target_bir_lowering=False)
    NTOK = NSEQ * SEQ

    def din(name, shape):
        return nc.dram_tensor(name, list(shape), F32, kind="ExternalInput").ap()

    x_d = din("x", [NTOK, D_MODEL])
    out_d = nc.dram_tensor("out", [NTOK, D_MODEL], F32, kind="ExternalOutput").ap()
    wg_d = [din("w1_gate", [D_MODEL, D_FF]), din("w2_gate", [D_MODEL, D_FF])]
    wu_d = [din("w1_up", [D_MODEL, D_FF]), din("w2_up", [D_MODEL, D_FF])]
    wd_d = [din("w1_down", [D_FF, D_MODEL]), din("w2_down", [D_FF, D_MODEL])]
    wsw_d = din("w_swa", [D_MODEL, 896])
    wm_d = din("w_mla", [D_MODEL, 640])
    wukv_d = din("w_ukv_r", [128, 1024])
    wuqa_d = din("w_uq_a", [256, 1024])
    wuqb_d = din("w_uq_b", [256, 1024])
    wo_d = din("w_o", [D_MODEL, D_MODEL])
    gains_d = din("gains", [128, NG])
    ident_d = din("ident", [128, 128])
    cbf_d = din("cbf", [128, 512])
    alibi_d = din("alibi", [128, 8 * 384])
    cos_d = din("cos2", [32, SEQ])
    sin_d = din("sin2", [32, SEQ])

    S = Sched()
    S.scopes = scopes
    with ExitStack() as st:
        NW = 53200
        arena_t = st.enter_context(nc.sbuf_tensor("arena", [128, NW], F32))
        A = Arena(arena_t[:, :], NW)
        banks = [st.enter_context(nc.psum_tensor("bank%d" % i, [128, 512], F32))[:] for i in range(8)]
        RB = [S.res("bank%d" % i, persist=True) for i in range(8)]

        def v3(ap, b):
            return ap.rearrange("p (a b) -> p a b", b=b)

        X = v3(A.alloc(8 * SEQ, F32), SEQ)
        XR = [[S.res("X%d_%d" % (c, t), persist=True) for t in range(NT)] for c in range(8)]
        gains = A.alloc(NG, F32)
        ident = A.alloc(128, F32)
        epsc = A.alloc(2, F32)
        esink = A.alloc(8, F32)
        cbf = A.alloc(512, BF16)
        ONES, BD64, BD96, IDB = cbf[:, 0:128], cbf[:, 128:256], cbf[:, 256:384], cbf[:, 384:512]
        Rc = S.res("consts", persist=True)
        cslot = S.slot("consts")
        S.add("sp", lambda e: e.dma_start(out=gains, in_=gains_d), writes=[Rc], dma=True, slot=cslot)
        S.add("sp", lambda e: e.dma_start(out=ident, in_=ident_d), writes=[Rc], dma=True, slot=cslot)
        cslot2 = S.slot("consts_sw")
        S.add("pool", lambda e: e.dma_start(out=cbf, in_=cbf_d), writes=[Rc], dma=True, slot=cslot2)
        S.add("dve", lambda e: e.memset(epsc[:, 0:1], RMS_EPS), writes=[Rc])
        S.add("act", lambda e: e.activation(out=esink, in_=gains[:, G_SINK:G_SINK + 8], func=AF.Exp),
              reads=[Rc], writes=[Rc])
        EPS = epsc[:, 0:1]
        base_mark = A.mark()

        def gcol(i, lo=0, hi=128):
            return gains[lo:hi, i:i + 1]

        def mm(out, lhsT, rhs, start, stop, reads, writes):
            S.add("pe", lambda e: e.matmul(out, lhsT=lhsT, rhs=rhs, start=start, stop=stop),
                  reads=reads, writes=writes)

        def act(out, in_, func, reads, writes, scale=None, bias=None):
            kw = {}
            if scale is not None:
                kw["scale"] = scale
            if bias is not None:
                kw["bias"] = bias
            S.add("act", lambda e: e.activation(out=out, in_=in_, func=func, **kw), reads=reads, writes=writes)

        def stt(eng, out, in0, scalar, in1, op0, op1, reads, writes):
            S.add(eng, lambda e: e.scalar_tensor_tensor(out=out, in0=in0, scalar=scalar, in1=in1, op0=op0, op1=op1),
                  reads=reads, writes=writes)

        def tt(eng, out, in0, in1, op, reads, writes):
            S.add(eng, lambda e: e.tensor_tensor(out=out, in0=in0, in1=in1, op=op), reads=reads, writes=writes)

        def copy(eng, out, in_, reads, writes):
            if eng == "act":
                S.add("act", lambda e: e.activation(out=out, in_=in_, func=AF.Copy), reads=reads, writes=writes)
            else:
                S.add(eng, lambda e: e.tensor_copy(out=out, in_=in_), reads=reads, writes=writes)

        def dma(q, out, in_, reads, writes, slot):
            S.add(q, lambda e: e.dma_start(out=out, in_=in_), reads=reads, writes=writes, dma=True, slot=slot)

        class Scratch:
            def __init__(self, tag):
                self.sqb = [A.alloc(512, BF16) for _ in range(4)]
                self.Rsqb = [S.res(tag + "sqb%d" % i) for i in range(4)]
                self.lnt = A.alloc(512, F32)
                self.Rlnt = S.res(tag + "lnt")
                self.rstd = [A.alloc(512, F32) for _ in range(2)]
                self.Rrstd = [S.res(tag + "rstd%d" % i) for i in range(2)]
                self.k = 0
                self.kr = 0

            def sq(self):
                i = self.k % 4
                self.k += 1
                return self.sqb[i], self.Rsqb[i]

            def rs(self):
                i = self.kr % 2
                self.kr += 1
                return self.rstd[i], self.Rrstd[i]

        def rstd_from(sc, bank_i, rows, scale):
            lo, hi = rows
            r, Rr = sc.rs()
            act(sc.lnt[lo:hi, :], banks[bank_i][lo:hi, :], AF.Ln, [RB[bank_i], Rc], [sc.Rlnt],
                scale=scale, bias=epsc[lo:hi, 0:1])
            act(r[lo:hi, :], sc.lnt[lo:hi, :], AF.Exp, [sc.Rlnt], [Rr], scale=-0.5)
            return r, Rr

        def norm_tile(sc, s_unused, t, gbase, Hc, HR, statc, act_sq=False):
            cols = slice(t * TILE, (t + 1) * TILE)
            sb = statc.next()
            for c in range(8):
                q, Rq = sc.sq()
                if c % 2 == 0 or act_sq:
                    act(q, X[:, c, cols], AF.Square, [XR[c][t]], [Rq])
                else:
                    tt("dve", q, X[:, c, cols], X[:, c, cols], ALU.mult, [XR[c][t]], [Rq])
                mm(banks[sb], ONES, q, c == 0, c == 7, [Rq, Rc], [RB[sb]])
            r, Rr = rstd_from(sc, sb, (0, 128), 1.0 / D_MODEL)
            for c in range(8):
                stt("dve", Hc[c], X[:, c, cols], gcol(gbase + c), r, ALU.mult, ALU.mult,
                    [XR[c][t], Rr, Rc], [HR[c]])

        xslots = [S.slot("xin0"), S.slot("xin1")]
        yslots = [S.slot("yout0"), S.slot("yout1")]
        wgslots = [S.slot("wg0"), S.slot("wg1")]
        wuslots = [S.slot("wu0"), S.slot("wu1")]
        wdslots = [S.slot("wd0"), S.slot("wd1")]

        def ffn_segment(sts):
            S.new_phase()
            A.release(base_mark)
            tag = "fs%d_%d_%d_" % sts[0]
            actb = v3(A.alloc(NJ * 1024, BF16), 1024)
            actR = [[S.res(tag + "act%d_%d" % (j, u)) for u in range(2)] for j in range(NJ)]
            H2 = [v3(A.alloc(8 * 1024, BF16), 1024) for _ in range(2)]
            HR2 = [[[S.res(tag + "H%d_%d_%d" % (i, c, u)) for c in range(8)] for u in range(2)] for i in range(2)]
            wgs = [v3(A.alloc(8 * 256, BF16), 256) for _ in range(2)]
            wus = [v3(A.alloc(8 * 256, BF16), 256) for _ in range(2)]
            wds = [v3(A.alloc(NJ * 256, BF16), 256) for _ in range(2)]
            Rwg = [S.res(tag + "wg%d" % i) for i in range(2)]
            Rwu = [S.res(tag + "wu%d" % i) for i in range(2)]
            Rwd = [S.res(tag + "wd%d" % i) for i in range(2)]
            io = [A.alloc(1024, F32) for _ in range(2)]
            Rio = [S.res(tag + "io%d" % i) for i in range(2)]
            sg = [A.alloc(512, BF16) for _ in range(2)]
            Rsg = [S.res(tag + "sg%d" % i) for i in range(2)]
            sc = Scratch(tag)
            gc, uc, dc, mc = Cyc([0, 1]), Cyc([2, 3]), Cyc([4, 5]), Cyc([6, 7])
            cnt = {"sg": 0, "ev": 0, "io": 0, "cur": 0, "cur_o": 0}

            def prep_pieces(k):
                s, which, stile = sts[k]
                ptag = "ffn%d_s%d" % (which + 1, s)
                gbase = G_FFN1 if which == 0 else G_FFN2
                H, HR = H2[k % 2], HR2[k % 2]
                pcs = []
                for u in range(2):
                    t = stile * 2 + u
                    cols = slice(t * TILE, (t + 1) * TILE)
                    if which == 0:
                        for bb in range(4):
                            for half in range(2):
                                def pT(bb=bb, half=half, t=t):
                                    tok = s * SEQ + t * TILE + bb * 128
                                    tc0 = t * TILE + bb * 128
                                    if half == 0:
                                        ii = cnt["io"] % 2
                                        cnt["io"] += 1
                                        cnt["cur"] = ii
                                        dma("sp", io[ii], x_d[tok:tok + 128, :], [], [Rio[ii]], xslots[ii])
                                    ii = cnt["cur"]
                                    b = mc.next()
                                    for c4 in range(4):
                                        c = half * 4 + c4
                                        S.add("pe", (lambda b=b, c4=c4, c=c, ii=ii: lambda e: e.transpose(
                                            banks[b][:, c4 * 128:(c4 + 1) * 128], io[ii][:, c * 128:(c + 1) * 128], ident))(),
                                            reads=[Rio[ii], Rc], writes=[RB[b]])
                                    copy("act" if cnt["ev"] % 2 == 0 else "dve", X[:, half * 4:half * 4 + 4, tc0:tc0 + 128],
                                         banks[b].rearrange("p (k n) -> p k n", n=128), [RB[b]],
                                         [XR[half * 4 + c4][t] for c4 in range(4)])
                                    cnt["ev"] += 1
                                pcs.append(pT)
                    st_ = {}

                    def p_sq(part, t=t, cols=cols, st_=st_):
                        if part == 0:
                            st_["sb"] = mc.next()
                        st_["q%d" % part] = []
                        for c in range(part * 4, part * 4 + 4):
                            q, Rq = sc.sq()
                            if c % 2 == 0:
                                act(q, X[:, c, cols], AF.Square, [XR[c][t]], [Rq])
                            else:
                                tt("dve", q, X[:, c, cols], X[:, c, cols], ALU.mult, [XR[c][t]], [Rq])
                            st_["q%d" % part].append((q, Rq))

                    def p_stat(part, st_=st_):
                        sb = st_["sb"]
                        for i, (q, Rq) in enumerate(st_["q%d" % part]):
                            c = part * 4 + i
                            mm(banks[sb], ONES, q, c == 0, c == 7, [Rq, Rc], [RB[sb]])

                    def p_fin(t=t, cols=cols, u=u, st_=st_):
                        r, Rr_ = rstd_from(sc, st_["sb"], (0, 128), 1.0 / D_MODEL)
                        for c in range(8):
                            stt("dve", H[:, c, u * TILE:(u + 1) * TILE], X[:, c, cols], gcol(gbase + c), r, ALU.mult, ALU.mult,
                                [XR[c][t], Rr_, Rc], [HR[u][c]])
                    pcs += [lambda p_sq=p_sq: p_sq(0), lambda p_stat=p_stat: p_stat(0),
                            lambda p_sq=p_sq: p_sq(1), lambda p_stat=p_stat: p_stat(1), p_fin]

                def wrap(f):
                    def g():
                        old = S.tag
                        S.tag = ptag
                        f()
                        S.tag = old
                    return g
                return [wrap(f) for f in pcs]

            def output_pieces(k):
                s, which, stile = sts[k]
                ptag = "ffn%d_s%d" % (which + 1, s)
                pcs = []
                for u in range(2):
                    t = stile * 2 + u
                    for bb in range(4):
                        for half in range(2):
                            def pO(t=t, bb=bb, half=half):
                                tok = s * SEQ + t * TILE + bb * 128
                                tc0 = t * TILE + bb * 128
                                if half == 0:
                                    cnt["cur_o"] = cnt["io"] % 2
                                    cnt["io"] += 1
                                ii = cnt["cur_o"]
                                b = mc.next()
                                for c4 in range(4):
                                    c = half * 4 + c4
                                    S.add("pe", (lambda b=b, c4=c4, c=c, tc0=tc0: lambda e: e.transpose(
                                        banks[b][:, c4 * 128:(c4 + 1) * 128], X[:, c, tc0:tc0 + 128], ident))(),
                                        reads=[XR[c][t], Rc], writes=[RB[b]])
                                copy("act" if cnt["ev"] % 2 == 0 else "dve", io[ii][:, half * 512:(half + 1) * 512],
                                     banks[b], [RB[b]], [Rio[ii]])
                                cnt["ev"] += 1
                                if half == 1:
                                    dma("sp", out_d[tok:tok + 128, :], io[ii], [Rio[ii]], [], yslots[ii])
                            pcs.append(pO)

                def wrap(f):
                    def g():
                        old = S.tag
                        S.tag = ptag
                        f()
                        S.tag = old
                    return g
                return [wrap(f) for f in pcs]

            def phase_a(k, pieces=()):
                pieces = list(pieces)
                s, which, stile = sts[k]
                S.tag = "ffn%d_s%d" % (which + 1, s)
                H, HR = H2[k % 2], HR2[k % 2]
                wgv = wg_d[which].rearrange("(c p) n -> p c n", p=128)
                wuv = wu_d[which].rearrange("(c p) n -> p c n", p=128)
                for jp in range(NJ // 2):
                    sl = jp % 2
                    dma("pool", wgs[sl], wgv[:, :, jp * 256:(jp + 1) * 256], [], [Rwg[sl]], wgslots[sl])
                    dma("pool", wus[sl], wuv[:, :, jp * 256:(jp + 1) * 256], [], [Rwu[sl]], wuslots[sl])
                    for jj in range(2):
                        j = jp * 2 + jj
                        for u in range(2):
                            bg = gc.next()
                            bu = uc.next()
                            for c in range(8):
                                mm(banks[bg], wgs[sl][:, c, jj * 128:(jj + 1) * 128], H[:, c, u * TILE:(u + 1) * TILE],
                                   c == 0, c == 7, [Rwg[sl], HR[u][c]], [RB[bg]])
                            for c in range(8):
                                mm(banks[bu], wus[sl][:, c, jj * 128:(jj + 1) * 128], H[:, c, u * TILE:(u + 1) * TILE],
                                   c == 0, c == 7, [Rwu[sl], HR[u][c]], [RB[bu]])
                            i = cnt["sg"] % 2
                            cnt["sg"] += 1
                            act(sg[i], banks[bg], AF.Silu, [RB[bg]], [Rsg[i]])
                            tt("dve", actb[:, j, u * TILE:(u + 1) * TILE], sg[i], banks[bu], ALU.mult,
                               [Rsg[i], RB[bu]], [actR[j][u]])
                            if pieces:
                                pieces.pop(0)()
                while pieces:
                    pieces.pop(0)()

            def phase_b(k, pieces=()):
                pieces = list(pieces)
                s, which, stile = sts[k]
                wdv = wd_d[which].rearrange("(j p) n -> p j n", p=128)
                for cp in range(4):
                    S.tag = "ffn%d_s%d" % (which + 1, s)
                    sl = cp % 2
                    dma("pool", wds[sl], wdv[:, :, cp * 256:(cp + 1) * 256], [], [Rwd[sl]], wdslots[sl])
                    for cc in range(2):
                        c = cp * 2 + cc
                        for u in range(2):
                            t = stile * 2 + u
                            cols = slice(t * TILE, (t + 1) * TILE)
                            b = dc.next()
                            for j in range(NJ):
                                mm(banks[b], wds[sl][:, j, cc * 128:(cc + 1) * 128], actb[:, j, u * TILE:(u + 1) * TILE],
                                   j == 0, j == NJ - 1, [Rwd[sl], actR[j][u]], [RB[b]])
                            stt("dve", X[:, c, cols], banks[b], 0.5, X[:, c, cols], ALU.mult, ALU.add,
                                [RB[b], XR[c][t]], [XR[c][t]])
                            for _ in range(2):
                                if pieces:
                                    pieces.pop(0)()
                while pieces:
                    pieces.pop(0)()

            for p in prep_pieces(0):
                p()
            pend_out = []
            for k in range(len(sts)):
                phase_a(k, pend_out)
                nxt = prep_pieces(k + 1) if k + 1 < len(sts) else []
                phase_b(k, nxt)
                pend_out = output_pieces(k) if sts[k][1] == 1 else []
            for p in pend_out:
                p()

        wslots = {n: S.slot(n) for n in ("wsw", "alibi", "wm", "wukv", "wuqa", "wuqb", "wo", "tabc0", "tabs0", "tabc1", "tabs1", "kpe")}

        def mix_phase(s):
            S.new_phase()
            S.tag = "swa_s%d" % s
            A.release(base_mark)
            tag = "m%d_" % s
            attn_s = v3(A.alloc(4 * SEQ, BF16), SEQ)
            Rattn_s = [[S.res(tag + "as%d_%d" % (c, t)) for t in range(NT)] for c in range(4)]
            mix_mark = A.mark()

            Wsw = v3(A.alloc(8 * 896, BF16), 896)
            RWsw = S.res(tag + "wsw")
            alibi = v3(A.alloc(8 * 384, F32), 384)
            Ralibi = S.res(tag + "alibi")
            exs = [A.alloc(384, F32) for _ in range(3)]
            Rexs = [S.res(tag + "exs%d" % i) for i in range(3)]
            lnd = A.alloc(512, F32)
            Rlnd = S.res(tag + "lnd")
            Qs = v3(A.alloc(8 * SEQ, BF16), SEQ)
            RQs = [[S.res(tag + "qs%d_%d" % (c, t)) for t in range(NT)] for c in range(4)]
            RQz = [S.res(tag + "qz%d" % t) for t in range(NT)]
            Ks = v3(A.alloc(2 * SEQ, BF16), SEQ)
            RKs = [[S.res(tag + "ks%d_%d" % (g, t)) for t in range(NT)] for g in range(2)]
            Vs = v3(A.alloc(16 * 320, BF16), 320)
            RVs = [S.res(tag + "vs%d" % t) for t in range(NT)]
            RVs1 = S.res(tag + "vs_ones")
            Hs2 = [v3(A.alloc(8 * TILE, BF16), TILE) for _ in range(2)]
            RHs2 = [[S.res(tag + "hs%d_%d" % (i, c)) for c in range(8)] for i in range(2)]
            NPT = 6
            PT = [A.alloc(384, BF16) for _ in range(NPT)]
            RPT = [S.res(tag + "pts%d" % i) for i in range(NPT)]
            dent = A.alloc(512, F32)
            Rdent = S.res(tag + "dent")
            Rr = A.alloc(512, F32)
            RRr = S.res(tag + "Rr")
            sc = Scratch(tag + "s")
            dma("pool", Wsw, wsw_d.rearrange("(c p) n -> p c n", p=128), [], [RWsw], wslots["wsw"])
            dma("sp", alibi, alibi_d.rearrange("p (h n) -> p h n", n=384), [], [Ralibi], wslots["alibi"])
            act(alibi, alibi, AF.Exp, [Ralibi], [Ralibi], scale=0.125)
            for lo in (0, 128, 256):
                S.add("pool", (lambda lo=lo: lambda e: e.memset(Vs[:, :, lo:lo + 64], 1.0))(), writes=[RVs1])
            for t_ in range(NT):
                S.add("pool", (lambda t_=t_: lambda e: e.memset(Qs[:, :, t_ * TILE:(t_ + 1) * TILE], 0.0))(),
                      writes=[RQz[t_]] + [RQs[c][t_] for c in range(4)])
            mainc, statc, vc = Cyc([0, 1, 2, 3]), Cyc([4, 5]), Cyc([6, 7])

            def run_jobs(jobs):
                prev = None
                for jb in jobs:
                    st_ = jb[0]()
                    if prev is not None:
                        prev[0](prev[1])
                    prev = (jb[1], st_)
                if prev is not None:
                    prev[0](prev[1])

            norm_tile(sc, s, 0, G_MIX, [Hs2[0][:, c, :] for c in range(8)], RHs2[0], statc, act_sq=True)
            for t in range(NT):
                cols = slice(t * TILE, (t + 1) * TILE)
                Hs, RHs = Hs2[t % 2], RHs2[t % 2]

                def mk_job(wcol, gidx, out_ap, out_res, Hs=Hs, RHs=RHs, split_q=None):
                    def stage_a():
                        b = mainc.next()
                        for c in range(8):
                            mm(banks[b], Wsw[:, c, wcol:wcol + 128], Hs[:, c, :], c == 0, c == 7, [RWsw, RHs[c]], [RB[b]])
                        q, Rq = sc.sq()
                        act(q, banks[b], AF.Square, [RB[b]], [Rq])
                        return (b, q, Rq)

                    def stage_b(st_):
                        b, q, Rq = st_
                        b2 = statc.next()
                        mm(banks[b2], BD64, q, True, True, [Rq, Rc], [RB[b2]])
                        r, Rr_ = rstd_from(sc, b2, (0, 128), 1.0)
                        if split_q is None:
                            stt("dve", out_ap, banks[b], gcol(gidx), r, ALU.mult, ALU.mult, [RB[b], Rr_, Rc], [out_res])
                        else:
                            cq_, cols_ = split_q
                            for hf in range(2):
                                lo = hf * 64
                                stt("dve", Qs[lo:lo + 64, 2 * cq_ + hf, cols_], banks[b][lo:lo + 64, :], gcol(gidx, lo, lo + 64),
                                    r[lo:lo + 64, :], ALU.mult, ALU.mult, [RB[b], Rr_, Rc], [out_res])
                    return (stage_a, stage_b)

                jobs = [mk_job(cq_ * 128, G_SWQ, None, RQs[cq_][t], split_q=(cq_, cols)) for cq_ in range(4)]
                jobs += [mk_job(512 + g * 128, G_SWK, Ks[:, g, cols], RKs[g][t]) for g in range(2)]
                run_jobs(jobs[:3])
                if t + 1 < NT:
                    norm_tile(sc, s, t + 1, G_MIX, [Hs2[(t + 1) % 2][:, c, :] for c in range(8)], RHs2[(t + 1) % 2], statc, act_sq=True)
                run_jobs(jobs[3:])
                b = vc.next()
                for blk in range(4):
                    for c in range(8):
                        mm(banks[b][:, blk * 128:(blk + 1) * 128], Hs[:, c, blk * 128:(blk + 1) * 128],
                           Wsw[:, c, 768:896], c == 0, c == 7, [RWsw, RHs[c]], [RB[b]])
                bv = banks[b].rearrange("p (k n) -> p k n", n=128)
                copy("act", Vs[:, 4 * t:4 * t + 4, 64:128], bv[:, :, 0:64], [RB[b], RVs1], [RVs[t]])
                copy("dve", Vs[:, 4 * t:4 * t + 4, 192:256], bv[:, :, 64:128], [RB[b], RVs1], [RVs[t]])

            S.tag = "swaattn_s%d" % s
            sbank = [0, 1, 2]
            obank = [3, 4]
            LOOK = 2
            NS = 8 * 16

            def hinfo(h):
                g, half, qc = h // 4, h % 2, h // 2
                r0 = half * 64
                if half == 0:
                    vlo = 64 if g == 0 else 192
                else:
                    vlo = 0 if g == 0 else 128
                return g, half, qc, r0, vlo

            def swa_S(i):
                h, j = i // 16, i % 16
                g, half, qc, r0, vlo = hinfo(h)
                r1 = r0 + 64
                qlo, qhi = max(j - 1, 0), min(j + 1, 15)
                ncol = (qhi - qlo + 1) * 128
                off = (qlo - (j - 1)) * 128
                b = sbank[i % 3]
                qt_res = [RQs[qc][tt_] for tt_ in range((qlo * 128) // TILE, (qhi * 128) // TILE + 1)]
                mm(banks[b][:, 0:ncol], Ks[:, g, j * 128:(j + 1) * 128], Qs[:, h, qlo * 128:(qhi + 1) * 128],
                   True, True, [RKs[g][j // 4]] + qt_res, [RB[b]])
                ex, Rex = exs[i % 3], Rexs[i % 3]
                act(ex[:, 0:ncol], banks[b][:, 0:ncol], AF.Exp, [RB[b]], [Rex], scale=0.125)
                tt("dve", PT[i % NPT][:, 0:ncol], ex[:, 0:ncol], alibi[:, h, off:off + ncol],
                   ALU.mult, [Rex, Ralibi], [RPT[i % NPT]])

            def swa_pv(h, n):
                g, half, qc, r0, vlo = hinfo(h)
                ob = obank[(h * 4 + n // 4) % 2]
                jjs = [jj for jj in (n - 1, n, n + 1) if 0 <= jj < 16]
                for k, jj in enumerate(jjs):
                    qlo = max(jj - 1, 0)
                    off = (n - qlo) * 128
                    pi = (h * 16 + jj) % NPT
                    mm(banks[ob][:, (n % 4) * 128:(n % 4 + 1) * 128], Vs[:, jj, vlo:vlo + 128],
                       PT[pi][:, off:off + 128], k == 0, k == len(jjs) - 1,
                       [RVs[jj // 4], RVs1, RPT[pi]], [RB[ob]])
                if n % 4 == 3:
                    tq = n // 4
                    nr = (r0, r0 + 64)
                    dr = (64 - r0, 128 - r0)
                    act(lnd[dr[0]:dr[1], :], banks[ob][dr[0]:dr[1], :], AF.Ln, [RB[ob], Rc], [Rlnd],
                        scale=1.0, bias=esink[dr[0]:dr[1], h:h + 1])
                    act(Rr[dr[0]:dr[1], :], lnd[dr[0]:dr[1], :], AF.Exp, [Rlnd], [RRr], scale=-1.0)
                    tt("dve", attn_s[nr[0]:nr[1], qc, tq * TILE:(tq + 1) * TILE], banks[ob][nr[0]:nr[1], :],
                       Rr[dr[0]:dr[1], :], ALU.mult, [RB[ob], RRr], [Rattn_s[qc][tq]])

            def swa_PV(i):
                h, j = i // 16, i % 16
                if j >= 1:
                    swa_pv(h, j - 1)
                if j == 15:
                    swa_pv(h, 15)

            for i in range(NS + LOOK):
                if i < NS:
                    swa_S(i)
                if i - LOOK >= 0:
                    swa_PV(i - LOOK)

            S.new_phase()
            S.tag = "m1_s%d" % s
            A.release(mix_mark)
            Kt = v3(A.alloc(8 * SEQ, BF16), SEQ)
            RKn = [[S.res(tag + "kn%d_%d" % (h, t)) for t in range(NT)] for h in range(8)]
            RKp = [[S.res(tag + "kp%d_%d" % (h, t)) for t in range(NT)] for h in range(8)]
            Va = v3(A.alloc(16 * 768, BF16), 768)
            RVa = [S.res(tag + "va%d" % t) for t in range(NT)]
            RVa1 = S.res(tag + "va_ones")
            cq = v3(A.alloc(2 * SEQ, BF16), SEQ)
            Rcq = [[S.res(tag + "cq%d_%d" % (k, t)) for t in range(NT)] for k in range(2)]
            WuqA = v3(A.alloc(2 * 1024, BF16), 1024)
            WuqB = v3(A.alloc(2 * 1024, BF16), 1024)
            RWa, RWb = S.res(tag + "wuqa"), S.res(tag + "wuqb")
            m1_mark = A.mark()
            Wm = v3(A.alloc(8 * 640, BF16), 640)
            RWm = S.res(tag + "wm")
            Wukv = A.alloc(1024, BF16)
            RWukv = S.res(tag + "wukv")
            Hm2 = [v3(A.alloc(8 * TILE, BF16), TILE) for _ in range(2)]
            RHm2 = [[S.res(tag + "hm%d_%d" % (i, c)) for c in range(8)] for i in range(2)]
            ckv2 = [A.alloc(512, BF16) for _ in range(2)]
            Rckv2 = [S.res(tag + "ckv%d" % i) for i in range(2)]
            tabC2 = [A.alloc(512, F32) for _ in range(2)]
            tabS2 = [A.alloc(512, F32) for _ in range(2)]
            RtC2 = [S.res(tag + "tabC%d" % i) for i in range(2)]
            RtS2 = [S.res(tag + "tabS%d" % i) for i in range(2)]
            t1 = A.alloc(512, F32)
            t2 = A.alloc(512, F32)
            Rt1, Rt2 = S.res(tag + "t1"), S.res(tag + "t2")
            kpt = A.alloc(512, BF16)
            Rkpt = S.res(tag + "kpt")
            Rkpdma = S.res(tag + "kpdma")
            sc = Scratch(tag + "m")
            dma("pool", Wm, wm_d.rearrange("(c p) n -> p c n", p=128), [], [RWm], wslots["wm"])
            dma("pool", Wukv, wukv_d, [], [RWukv], wslots["wukv"])
            dma("pool", WuqA, wuqa_d.rearrange("(c p) n -> p c n", p=128), [], [RWa], wslots["wuqa"])
            dma("pool", WuqB, wuqb_d.rearrange("(c p) n -> p c n", p=128), [], [RWb], wslots["wuqb"])
            WuqB4 = WuqB.rearrange("p k (h n) -> p k h n", n=128)
            for kc in range(2):
                S.add("dve", (lambda kc=kc: lambda e: e.tensor_scalar(
                    out=WuqB4[:, kc, :, 64:80], in0=WuqB4[:, kc, :, 64:80], scalar1=-1.0, scalar2=None,
                    op0=ALU.mult))(), reads=[RWb], writes=[RWb])
            S.add("dve", lambda e: e.tensor_scalar(out=Wm[:, :, 576:592], in0=Wm[:, :, 576:592], scalar1=-1.0,
                                                   scalar2=None, op0=ALU.mult), reads=[RWm], writes=[RWm])
            Va4 = Va.rearrange("p k (i n) -> p k i n", n=192)
            S.add("pool", lambda e: e.memset(Va4[:, :, :, 64:128], 1.0), writes=[RVa1])
            mainc, statc, vc = Cyc([0, 1, 2, 3]), Cyc([4, 5]), Cyc([6, 7])

            def m1_tabs(t):
                cols = slice(t * TILE, (t + 1) * TILE)
                dma("sp", tabC2[t % 2][64:96, :], cos_d[:, cols], [], [RtC2[t % 2]], wslots["tabc%d" % (t % 2)])
                dma("sp", tabS2[t % 2][64:96, :], sin_d[:, cols], [], [RtS2[t % 2]], wslots["tabs%d" % (t % 2)])

            m1_tabs(0)
            norm_tile(sc, s, 0, G_MIX, [Hm2[0][:, c, :] for c in range(8)], RHm2[0], statc, act_sq=True)
            for t in range(NT):
                cols = slice(t * TILE, (t + 1) * TILE)
                Hm, RHm = Hm2[t % 2], RHm2[t % 2]
                ckv, Rckv = ckv2[t % 2], Rckv2[t % 2]
                tabC, tabS, RtC, RtS = tabC2[t % 2], tabS2[t % 2], RtC2[t % 2], RtS2[t % 2]

                def proj(wcol, m, Hm=Hm, RHm=RHm):
                    b = mainc.next()
                    for c in range(8):
                        mm(banks[b][0:m, :], Wm[:, c, wcol:wcol + m], Hm[:, c, :], c == 0, c == 7, [RWm, RHm[c]], [RB[b]])
                    return b

                def cq_a():
                    bq0, bq1 = proj(0, 128), proj(128, 128)
                    qa, Rqa = sc.sq()
                    act(qa, banks[bq0], AF.Square, [RB[bq0]], [Rqa])
                    qb, Rqb = sc.sq()
                    act(qb, banks[bq1], AF.Square, [RB[bq1]], [Rqb])
                    return (bq0, bq1, qa, Rqa, qb, Rqb)

                def cq_b(st_, t=t, cols=cols):
                    bq0, bq1, qa, Rqa, qb, Rqb = st_
                    b2 = statc.next()
                    mm(banks[b2], ONES, qa, True, False, [Rqa, Rc], [RB[b2]])
                    mm(banks[b2], ONES, qb, False, True, [Rqb, Rc], [RB[b2]])
                    r, Rr_ = rstd_from(sc, b2, (0, 128), 1.0 / 256)
                    stt("dve", cq[:, 0, cols], banks[bq0], gcol(G_QA), r, ALU.mult, ALU.mult, [RB[bq0], Rr_, Rc], [Rcq[0][t]])
                    stt("dve", cq[:, 1, cols], banks[bq1], gcol(G_QA + 1), r, ALU.mult, ALU.mult, [RB[bq1], Rr_, Rc], [Rcq[1][t]])

                def ckv_a():
                    bkv = proj(256, 128)
                    q, Rq = sc.sq()
                    act(q, banks[bkv], AF.Square, [RB[bkv]], [Rq])
                    return (bkv, q, Rq)

                def ckv_b(st_, ckv=ckv, Rckv=Rckv):
                    bkv, q, Rq = st_
                    b2 = statc.next()
                    mm(banks[b2], ONES, q, True, True, [Rq, Rc], [RB[b2]])
                    r, Rr_ = rstd_from(sc, b2, (0, 128), 1.0 / 128)
                    stt("dve", ckv, banks[bkv], gcol(G_KVA), r, ALU.mult, ALU.mult, [RB[bkv], Rr_, Rc], [Rckv])

                def kpe_a():
                    bka, bkb = proj(384, 128), proj(512, 128)
                    q, Rq = sc.sq()
                    act(q[0:96, :], banks[bka][0:96, :], AF.Square, [RB[bka]], [Rq])
                    return (bka, bkb, q, Rq)

                def kpe_b(st_, t=t, cols=cols, tabC=tabC, tabS=tabS, RtC=RtC, RtS=RtS):
                    bka, bkb, q, Rq = st_
                    b2 = statc.next()
                    mm(banks[b2][0:96, :], BD96[0:96, 0:96], q[0:96, :], True, True, [Rq, Rc], [RB[b2]])
                    r, Rr_ = rstd_from(sc, b2, (0, 96), 1.0)
                    stt("dve", t1[64:96, :], banks[bka][64:96, :], gcol(G_KRS, 64, 96), tabC[64:96, :], ALU.mult, ALU.mult,
                        [RB[bka], RtC, Rc], [Rt1])
                    stt("dve", t2[64:96, :], banks[bkb][64:96, :], gcol(G_KRW, 64, 96), tabS[64:96, :], ALU.mult, ALU.mult,
                        [RB[bkb], RtS, Rc], [Rt2])
                    tt("dve", t1[64:96, :], t1[64:96, :], t2[64:96, :], ALU.add, [Rt1, Rt2], [Rt1])
                    tt("dve", kpt[64:96, :], t1[64:96, :], r[64:96, :], ALU.mult, [Rt1, Rr_], [Rkpt])

                def mk_kn(hp, t=t, cols=cols, ckv=ckv, Rckv=Rckv):
                    def a():
                        b = mainc.next()
                        mm(banks[b], Wukv[:, hp * 128:(hp + 1) * 128], ckv, True, True, [RWukv, Rckv], [RB[b]])
                        q, Rq = sc.sq()
                        act(q, banks[b], AF.Square, [RB[b]], [Rq])
                        return (b, q, Rq)

                    def bfn(st_):
                        b, q, Rq = st_
                        b2 = statc.next()
                        mm(banks[b2], BD64, q, True, True, [Rq, Rc], [RB[b2]])
                        r, Rr_ = rstd_from(sc, b2, (0, 128), 1.0)
                        stt("dve", Kt[0:64, 2 * hp, cols], banks[b][0:64, :], gcol(G_KN, 0, 64), r[0:64, :],
                            ALU.mult, ALU.mult, [RB[b], Rr_, Rc], [RKn[2 * hp][t]])
                        stt("dve", Kt[0:64, 2 * hp + 1, cols], banks[b][64:128, :], gcol(G_KN, 64, 128), r[64:128, :],
                            ALU.mult, ALU.mult, [RB[b], Rr_, Rc], [RKn[2 * hp + 1][t]])
                    return (a, bfn)

                jobs = [(ckv_a, ckv_b), (cq_a, cq_b), (kpe_a, kpe_b)] + [mk_kn(hp) for hp in range(4)]
                run_jobs(jobs[:3])
                for blk in range(4):
                    b = vc.next()
                    mm(banks[b], ckv[:, blk * 128:(blk + 1) * 128], Wukv[:, 512:1024], True, True, [Rckv, RWukv], [RB[b]])
                    bv = banks[b].rearrange("p (i two n) -> p i two n", two=2, n=64)
                    copy("act", Va4[:, 4 * t + blk, :, 0:64], bv[:, :, 0, :], [RB[b], RVa1], [RVa[t]])
                    copy("dve", Va4[:, 4 * t + blk, :, 128:192], bv[:, :, 1, :], [RB[b], RVa1], [RVa[t]])
                if t + 1 < NT:
                    m1_tabs(t + 1)
                    norm_tile(sc, s, t + 1, G_MIX, [Hm2[(t + 1) % 2][:, c, :] for c in range(8)], RHm2[(t + 1) % 2], statc, act_sq=True)
                run_jobs(jobs[3:])
                dma("sp", Kt[64:96, :, cols], kpt[64:96, :].unsqueeze(1).broadcast_to([32, 8, TILE]), [Rkpt],
                    [RKp[h][t] for h in range(8)] + [Rkpdma], wslots["kpe"])

            S.new_phase()
            S.tag = "m2_s%d" % s
            A.release(m1_mark)
            Wo = v3(A.alloc(8 * 1024, BF16), 1024)
            RWo = S.res(tag + "wo")
            Qh = [A.alloc(512, BF16) for _ in range(3)]
            RQh = [S.res(tag + "qh%d" % i) for i in range(3)]
            PTm = [A.alloc(512, BF16) for _ in range(4)]
            RPTm = [S.res(tag + "ptm%d" % i) for i in range(4)]
            attn_m2 = [v3(A.alloc(4 * TILE, BF16), TILE) for _ in range(2)]
            Rattn_m2 = [[S.res(tag + "am%d_%d" % (i, c)) for c in range(4)] for i in range(2)]
            tabC2 = [A.alloc(512, F32)] * 2
            tabS2 = [A.alloc(512, F32)] * 2
            RtC2 = [S.res(tag + "tabCb")] * 2
            RtS2 = [S.res(tag + "tabSb")] * 2
            t1 = A.alloc(512, F32)
            t2 = A.alloc(512, F32)
            Rt1, Rt2 = S.res(tag + "t1b"), S.res(tag + "t2b")
            Rr = A.alloc(512, F32)
            RRr = S.res(tag + "Rrm")
            sc = Scratch(tag + "a")
            dma("pool", Wo, wo_d.rearrange("(c p) n -> p c n", p=128), [], [RWo], wslots["wo"])
            scale = 1.0 / math.sqrt(96.0)
            sbank = [0, 1, 2]
            obank = [3, 4]
            bA, bB, bM = 5, 6, 7
            bW = bM
            LOOK = 2
            NH = NT * 8
            NS = NH * 16

            def m2_tabs(t):
                cols = slice(t * TILE, (t + 1) * TILE)
                dma("sp", tabC2[t % 2][64:96, :], cos_d[:, cols], [], [RtC2[t % 2]], wslots["tabc%d" % (t % 2)])
                dma("sp", tabS2[t % 2][64:96, :], sin_d[:, cols], [], [RtS2[t % 2]], wslots["tabs%d" % (t % 2)])

            qstate = {}

            def qprod_a(th):
                t, h = th // 8, th % 8
                cols = slice(t * TILE, (t + 1) * TILE)
                for kc in range(2):
                    mm(banks[bA], WuqA[:, kc, h * 128:(h + 1) * 128], cq[:, kc, cols], kc == 0, kc == 1,
                       [RWa, Rcq[kc][t]], [RB[bA]])
                for kc in range(2):
                    mm(banks[bB], WuqB[:, kc, h * 128:(h + 1) * 128], cq[:, kc, cols], kc == 0, kc == 1,
                       [RWb, Rcq[kc][t]], [RB[bB]])
                q, Rq = sc.sq()
                act(q[0:96, :], banks[bA][0:96, :], AF.Square, [RB[bA]], [Rq])
                qstate[th] = (q, Rq)

            def qprod_b(th):
                t, h = th // 8, th % 8
                q, Rq = qstate.pop(th)
                qi = th % 3
                tabC, tabS, RtC, RtS = tabC2[t % 2], tabS2[t % 2], RtC2[t % 2], RtS2[t % 2]
                mm(banks[bM][0:96, :], BD96[0:96, 0:96], q[0:96, :], True, True, [Rq, Rc], [RB[bM]])
                r, Rr_ = rstd_from(sc, bM, (0, 96), 1.0)
                stt("dve", Qh[qi][0:64, :], banks[bA][0:64, :], gcol(G_QN, 0, 64), r[0:64, :], ALU.mult, ALU.mult,
                    [RB[bA], Rr_, Rc], [RQh[qi]])
                stt("dve", t1[64:96, :], banks[bA][64:96, :], gcol(G_QRS, 64, 96), tabC[64:96, :], ALU.mult, ALU.mult,
                    [RB[bA], RtC, Rc], [Rt1])
                stt("dve", t2[64:96, :], banks[bB][64:96, :], gcol(G_QRW, 64, 96), tabS[64:96, :], ALU.mult, ALU.mult,
                    [RB[bB], RtS, Rc], [Rt2])
                tt("dve", t1[64:96, :], t1[64:96, :], t2[64:96, :], ALU.add, [Rt1, Rt2], [Rt1])
                tt("dve", Qh[qi][64:96, :], t1[64:96, :], r[64:96, :], ALU.mult, [Rt1, Rr_], [RQh[qi]])

            def wo_piece(t, m):
                cols = slice(t * TILE, (t + 1) * TILE)
                dc_, c = m // 8, m % 8
                if c < 4:
                    rhs, rr = attn_m2[t % 2][:, c, :], Rattn_m2[t % 2][c]
                else:
                    rhs, rr = attn_s[:, c - 4, cols], Rattn_s[c - 4][t]
                mm(banks[bW], Wo[:, c, dc_ * 128:(dc_ + 1) * 128], rhs, c == 0, c == 7, [RWo, rr], [RB[bW]])
                if c == 7:
                    tt("dve", X[:, dc_, cols], banks[bW], X[:, dc_, cols], ALU.add, [RB[bW], XR[dc_][t]], [XR[dc_][t]])

            def m2_S(i):
                th, kc = i // 16, i % 16
                t, h = th // 8, th % 8
                qi = th % 3
                bs = sbank[i % 3]
                mm(banks[bs], Kt[0:96, h, kc * 128:(kc + 1) * 128], Qh[qi][0:96, :], True, True,
                   [RKn[h][kc // 4], RKp[h][kc // 4], RQh[qi]], [RB[bs]])
                act(PTm[i % 4], banks[bs], AF.Exp, [RB[bs]], [RPTm[i % 4]], scale=scale)
                if th + 1 < NH:
                    if kc == 0:
                        qprod_a(th + 1)
                    if kc == 3:
                        qprod_b(th + 1)
                if h == 6 and kc == 4 and t + 1 < NT:
                    m2_tabs(t + 1)
                if t > 0 and 6 <= kc < 14:
                    wo_piece(t - 1, h * 8 + (kc - 6))

            def m2_PV(i):
                th, kc = i // 16, i % 16
                t, h = th // 8, th % 8
                ob = obank[th % 2]
                half, pair = h % 2, h // 2
                vlo = pair * 192 + (0 if half == 0 else 64)
                mm(banks[ob], Va[:, kc, vlo:vlo + 128], PTm[i % 4], kc == 0, kc == 15,
                   [RVa[kc // 4], RVa1, RPTm[i % 4]], [RB[ob]])
                if kc == 15:
                    nr = (half * 64, half * 64 + 64)
                    dr = (64 - half * 64, 128 - half * 64)
                    r_o, r_i = Rr[nr[0]:nr[1], :], banks[ob][dr[0]:dr[1], :]
                    S.add("dve", lambda e: e.reciprocal(out=r_o, in_=r_i), reads=[RB[ob]], writes=[RRr])
                    tt("dve", attn_m2[t % 2][nr[0]:nr[1], pair, :], banks[ob][nr[0]:nr[1], :], r_o, ALU.mult,
                       [RB[ob], RRr], [Rattn_m2[t % 2][pair]])

            m2_tabs(0)
            qprod_a(0)
            qprod_b(0)
            for i in range(NS + LOOK):
                if i < NS:
                    m2_S(i)
                if i - LOOK >= 0:
                    m2_PV(i - LOOK)
            for m in range(64):
                wo_piece(NT - 1, m)

        ffn_segment([(0, 0, 0), (0, 0, 1)])
        mix_phase(0)
        ffn_segment([(0, 1, 0), (0, 1, 1), (1, 0, 0), (1, 0, 1)])
        mix_phase(1)
        ffn_segment([(1, 1, 0), (1, 1, 1)])
        S.final_wait("sp", yslots)
        S.emit(nc)
    return nc


def _host_constants():
    ident = np.eye(128, dtype=np.float32)
    cbf = np.zeros((128, 512), np.float32)
    cbf[:, 0:128] = 1.0
    bd64 = np.zeros((128, 128), np.float32)
    bd64[0:64, 0:64] = 1.0 / 64
    bd64[64:128, 64:128] = 1.0 / 64
    bd96 = np.zeros((128, 128), np.float32)
    bd96[0:64, 0:64] = 1.0 / 64
    bd96[64:96, 64:96] = 1.0 / 32
    cbf[:, 128:256] = bd64
    cbf[:, 256:384] = bd96
    cbf[:, 384:512] = ident
    i = np.arange(128)[:, None]
    c = np.arange(384)[None, :]
    rel = 128 + i - c
    valid = np.abs(rel) <= 128
    al = np.zeros((128, 8, 384), np.float32)
    for h in range(8):
        al[:, h, :] = np.where(valid, -np.abs(rel) * (2.0 ** (2 - h)), -30000.0)
    pos = np.arange(SEQ, dtype=np.float64)
    inv = 1.0 / (10000.0 ** (np.arange(0, 32, 2, dtype=np.float64) / 32))
    ang = pos[None, :] * inv[:, None]
    cos2 = np.concatenate([np.cos(ang), np.cos(ang)], 0).astype(np.float32)
    sin2 = np.concatenate([np.sin(ang), np.sin(ang)], 0).astype(np.float32)
    return ident, cbf, al.reshape(128, 8 * 384), cos2, sin2


_NC_CACHE = {}


def kernel(x, g_ffn1, w1_gate, w1_up, w1_down, g_mix, w_in, g_q_a, w_uq, g_kv_a, w_ukv,
           g_mla_qn, g_mla_qr, g_mla_kn, g_mla_kr, g_swa_q, g_swa_k, sink, w_o,
           g_ffn2, w2_gate, w2_up, w2_down):
    f = lambda a: np.ascontiguousarray(np.asarray(a, dtype=np.float32))
    x = f(x)
    w_in = f(w_in); w_uq = f(w_uq); w_ukv = f(w_ukv)
    hq, hkv, hkr = w_in[:, 0:256], w_in[:, 256:384], w_in[:, 384:416]
    sq, sk, sv = w_in[:, 416:928], w_in[:, 928:1056], w_in[:, 1056:1184]
    w_swa = np.concatenate([sq, sk[:, 0:64], sk[:, 0:64], sk[:, 64:128], sk[:, 64:128], sv], axis=1)
    pad = hkv[:, 0:64]
    pad2 = hkv[:, 64:96]
    w_mla = np.concatenate([hq, hkv, pad, hkr, pad2, pad, hkr[:, 16:32], hkr[:, 0:16], pad2], axis=1)
    uq = w_uq.reshape(256, 8, 96)
    w_uq_a = np.concatenate([uq, uq[:, :, 0:32]], axis=2).reshape(256, 1024)
    w_uq_b = np.concatenate([uq[:, :, 0:64], uq[:, :, 80:96], uq[:, :, 64:80], uq[:, :, 0:32]], axis=2).reshape(256, 1024)
    ukv = w_ukv.reshape(128, 8, 128)
    w_ukv_r = np.concatenate([ukv[:, :, 0:64].reshape(128, 512), ukv[:, :, 64:128].reshape(128, 512)], axis=1)
    gains = np.ones((128, NG), np.float32)
    gains[:, G_FFN1:G_FFN1 + 8] = f(g_ffn1).reshape(8, 128).T
    gains[:, G_MIX:G_MIX + 8] = f(g_mix).reshape(8, 128).T
    gains[:, G_FFN2:G_FFN2 + 8] = f(g_ffn2).reshape(8, 128).T
    gains[:, G_QA:G_QA + 2] = f(g_q_a).reshape(2, 128).T
    gains[:, G_KVA] = f(g_kv_a)
    gains[0:64, G_QN] = f(g_mla_qn)
    gains[:, G_KN] = np.tile(f(g_mla_kn), 2)
    gains[:, G_SWQ] = np.tile(f(g_swa_q), 2)
    gains[:, G_SWK] = np.tile(f(g_swa_k), 2)
    gqr, gkr = f(g_mla_qr), f(g_mla_kr)
    gains[64:96, G_QRS] = gqr
    gains[64:96, G_QRW] = np.concatenate([gqr[16:32], gqr[0:16]])
    gains[64:96, G_KRS] = gkr
    gains[64:96, G_KRW] = np.concatenate([gkr[16:32], gkr[0:16]])
    gains[:, G_SINK:G_SINK + 8] = np.broadcast_to(f(sink)[None, :], (128, 8))
    ident, cbf, alibi, cos2, sin2 = _host_constants()

    if "nc" not in _NC_CACHE:
        _NC_CACHE["nc"] = build_program()
    nc = _NC_CACHE["nc"]
    shared = {
        "w1_gate": f(w1_gate), "w1_up": f(w1_up), "w1_down": f(w1_down),
        "w2_gate": f(w2_gate), "w2_up": f(w2_up), "w2_down": f(w2_down),
        "w_swa": np.ascontiguousarray(w_swa), "w_mla": np.ascontiguousarray(w_mla),
        "w_ukv_r": np.ascontiguousarray(w_ukv_r), "w_uq_a": np.ascontiguousarray(w_uq_a), "w_uq_b": np.ascontiguousarray(w_uq_b),
        "w_o": f(w_o), "gains": gains, "ident": ident, "cbf": cbf, "alibi": alibi, "cos2": cos2, "sin2": sin2,
    }
    in_maps = []
    for c in range(NCORES):
        m = dict(shared)
        m["x"] = np.ascontiguousarray(x[NSEQ * c:NSEQ * (c + 1)].reshape(NSEQ * SEQ, D_MODEL))
        in_maps.append(m)
    res = run_bass_kernel_spmd(nc, in_maps, core_ids=list(range(NCORES)))
    out = np.stack([np.asarray(r["out"]).reshape(NSEQ, SEQ, D_MODEL) for r in res.results], axis=0)
    return out.reshape(NCORES * NSEQ, SEQ, D_MODEL).astype(np.float32)
```
